# Optimizing a Trainium2 kernel written in Bass

```python
import math
import jax
import jax.numpy as jnp
from jax import lax
import numpy as np

D_MODEL = 4096
BATCH = 4
SEQ = 4096
DEPTH = 2
DEC_BATCH = 16
DEC_SEQ = 16
PAST_LEN = 2048

CHUNK = 64
MIX_WIDTH = D_MODEL
DN_WIDTH = MIX_WIDTH // 2
DN_HEAD_DIM = 128
DN_HEADS = DN_WIDTH // DN_HEAD_DIM
CONV_W = 4
DN_CONV_DIM = 3 * DN_WIDTH
ATT_WIDTH = MIX_WIDTH - DN_WIDTH
ATT_HEAD_DIM = 64
ATT_Q_HEADS = ATT_WIDTH // ATT_HEAD_DIM
ATT_KV_HEADS = ATT_Q_HEADS // 8
ATT_GROUP = ATT_Q_HEADS // ATT_KV_HEADS
ATT_KV_WIDTH = ATT_KV_HEADS * ATT_HEAD_DIM
WINDOW = 128
ROPE_THETA = 10000.0
NORM_EPS = 1e-6
IN_SIZES = (DN_CONV_DIM, DN_WIDTH, DN_HEADS, DN_HEADS, ATT_WIDTH, ATT_KV_WIDTH, ATT_KV_WIDTH, ATT_WIDTH)
IN_DIM = sum(IN_SIZES)

kernel_name = 'hybrid_gdn_swa_sink_stream_step'


def rms_norm(x, gain):
    xf = x.astype(jnp.float32)
    y = xf * lax.rsqrt(jnp.mean(xf * xf, axis=-1, keepdims=True) + NORM_EPS)
    return (y * gain.astype(jnp.float32)).astype(x.dtype)


def l2_norm(x):
    xf = x.astype(jnp.float32)
    return xf * lax.rsqrt(jnp.sum(xf * xf, axis=-1, keepdims=True) + NORM_EPS)


def rope(x, pos):
    half = x.shape[-1] // 2
    inv = ROPE_THETA ** (-jnp.arange(half, dtype=jnp.float32) / half)
    ang = pos.astype(jnp.float32)[:, None] * inv[None, :]
    cos = jnp.cos(ang)[None, :, None, :]
    sin = jnp.sin(ang)[None, :, None, :]
    x1 = x[..., :half].astype(jnp.float32)
    x2 = x[..., half:].astype(jnp.float32)
    return jnp.concatenate([x1 * cos - x2 * sin, x2 * cos + x1 * sin], axis=-1).astype(x.dtype)


def split_cols(p):
    points = []
    acc = 0
    for s in IN_SIZES[:-1]:
        acc += s
        points.append(acc)
    return jnp.split(p, points, axis=-1)


def causal_short_conv(x, buf, w):
    t = x.shape[1]
    xp = jnp.concatenate([buf.astype(x.dtype), x], axis=1)
    y = xp[:, 0:t] * w[0]
    for j in range(1, CONV_W):
        y = y + xp[:, j:j + t] * w[j]
    return jax.nn.silu(y), xp[:, -(CONV_W - 1):]


def gated_delta_rule(q, k, v, g, beta, s0):
    n, t, h, dk = q.shape
    dv = v.shape[-1]
    cs = min(CHUNK, t)
    nc = t // cs

    def blk(a):
        return a.reshape(n, nc, cs, *a.shape[2:])

    q, k, v, g, beta = blk(q) * dk ** -0.5, blk(k), blk(v), blk(g), blk(beta)
    gc = jnp.cumsum(g, axis=2)
    idx = jnp.arange(cs)
    causal = (idx[:, None] >= idx[None, :])[None, None, :, :, None]
    strict = (idx[:, None] > idx[None, :])[None, None, :, :, None]
    diff = gc[:, :, :, None, :] - gc[:, :, None, :, :]
    gamma = jnp.exp(jnp.where(causal, diff, -jnp.inf))
    kk = jnp.einsum('ncihd,ncjhd->ncijh', k, k)
    a_mat = jnp.where(strict, beta[:, :, :, None, :] * kk * gamma, 0.0).transpose(0, 1, 4, 2, 3)
    rhs_u = (beta[..., None] * v).transpose(0, 1, 3, 2, 4)
    rhs_w = (beta[..., None] * jnp.exp(gc)[..., None] * k).transpose(0, 1, 3, 2, 4)
    sol = lax.linalg.triangular_solve(a_mat, jnp.concatenate([rhs_u, rhs_w], axis=-1),
                                      left_side=True, lower=True, unit_diagonal=True)
    u, w = sol[..., :dv], sol[..., dv:]
    qk = jnp.einsum('ncihd,ncjhd->nchij', q, k) * gamma.transpose(0, 1, 4, 2, 3)
    qdec = (q * jnp.exp(gc)[..., None]).transpose(0, 1, 3, 2, 4)
    kdec = (k * jnp.exp(gc[:, :, -1:, :] - gc)[..., None]).transpose(0, 1, 3, 2, 4)
    blk_decay = jnp.exp(gc[:, :, -1, :])
    xs = tuple(a.swapaxes(0, 1) for a in (u, w, qk, qdec, kdec, blk_decay))

    def step(s, inp):
        u_c, w_c, qk_c, qdec_c, kdec_c, dec_c = inp
        v_new = u_c - jnp.einsum('nhid,nhde->nhie', w_c, s)
        o = jnp.einsum('nhid,nhde->nhie', qdec_c, s) + jnp.einsum('nhij,nhje->nhie', qk_c, v_new)
        s = s * dec_c[..., None, None] + jnp.einsum('nhjd,nhje->nhde', kdec_c, v_new)
        return s, o

    s_final, o = lax.scan(step, s0, xs)
    o = o.transpose(1, 0, 3, 2, 4).reshape(n, t, h, dv)
    return o, s_final


def deltanet_branch(qkv_pre, z, b, a, conv_buf, s0, conv_w, a_log, dt_bias, norm_g):
    n, t, _ = qkv_pre.shape
    qkv, new_buf = causal_short_conv(qkv_pre, conv_buf, conv_w)
    q, k, v = jnp.split(qkv, 3, axis=-1)
    q = l2_norm(q.reshape(n, t, DN_HEADS, DN_HEAD_DIM))
    k = l2_norm(k.reshape(n, t, DN_HEADS, DN_HEAD_DIM))
    v = v.reshape(n, t, DN_HEADS, DN_HEAD_DIM).astype(jnp.float32)
    beta = jax.nn.sigmoid(b.astype(jnp.float32))
    g = -jnp.exp(a_log.astype(jnp.float32)) * jax.nn.softplus(a.astype(jnp.float32) + dt_bias.astype(jnp.float32))
    o, s_new = gated_delta_rule(q, k, v, g, beta, s0.astype(jnp.float32))
    zg = jax.nn.silu(z.reshape(n, t, DN_HEADS, DN_HEAD_DIM).astype(jnp.float32))
    o = rms_norm(o, norm_g) * zg
    return o.reshape(n, t, DN_WIDTH).astype(qkv_pre.dtype), new_buf, s_new


def sink_attention(q, k, v, sinks, mask):
    s = jnp.einsum('...tkgd,...skd->...kgts', q, k).astype(jnp.float32) * ATT_HEAD_DIM ** -0.5
    if mask is not None:
        s = jnp.where(mask, s, -jnp.inf)
    sink = sinks.astype(jnp.float32).reshape(ATT_KV_HEADS, ATT_GROUP)[:, :, None, None]
    m = jnp.maximum(jnp.max(s, axis=-1, keepdims=True), sink)
    p = jnp.exp(s - m)
    p = (p / (jnp.sum(p, axis=-1, keepdims=True) + jnp.exp(sink - m))).astype(v.dtype)
    return jnp.einsum('...kgts,...skd->...tkgd', p, v)


def to_band(x, nc, nb):
    n = x.shape[0]
    xp = jnp.pad(x, ((0, 0), (nb * CHUNK, 0), (0, 0), (0, 0)))
    xp = xp.reshape(n, nc + nb, CHUNK, *x.shape[2:])
    return jnp.concatenate([xp[:, j:j + nc] for j in range(nb + 1)], axis=2)


def swa_branch(q, k, v, z, pos, sinks, cache_k, cache_v):
    n, t, _ = q.shape
    q = rope(q.reshape(n, t, ATT_Q_HEADS, ATT_HEAD_DIM), pos)
    k = rope(k.reshape(n, t, ATT_KV_HEADS, ATT_HEAD_DIM), pos)
    v = v.reshape(n, t, ATT_KV_HEADS, ATT_HEAD_DIM)
    if cache_k is None:
        nc = t // CHUNK
        nb = WINDOW // CHUNK
        band = (nb + 1) * CHUNK
        qb = q.reshape(n, nc, CHUNK, ATT_KV_HEADS, ATT_GROUP, ATT_HEAD_DIM)
        kpos = (jnp.arange(nc)[:, None] - nb) * CHUNK + jnp.arange(band)[None, :]
        mask = (kpos >= 0)[None, :, None, None, None, :]
        o = sink_attention(qb, to_band(k, nc, nb), to_band(v, nc, nb), sinks, mask)
        k_all, v_all = k, v
        rows = WINDOW
    else:
        k_all = jnp.concatenate([cache_k.astype(k.dtype), k], axis=1)
        v_all = jnp.concatenate([cache_v.astype(v.dtype), v], axis=1)
        qg = q.reshape(n, t, ATT_KV_HEADS, ATT_GROUP, ATT_HEAD_DIM)
        o = sink_attention(qg, k_all, v_all, sinks, None)
        rows = cache_k.shape[1]
    o = o.reshape(n, t, ATT_WIDTH) * jax.nn.silu(z)
    return o, k_all[:, -rows:], v_all[:, -rows:]


def layer(x, c, pos, conv_buf, dn_state, win_k, win_v,
          w_ada, b_ada, g_pre, g_post, w_in, conv_w, a_log, dt_bias, dn_norm, sinks, w_out):
    shift, scale, gate = jnp.split(jax.nn.silu(c) @ w_ada + b_ada, 3, axis=-1)
    h = rms_norm(x, g_pre) * (1.0 + scale[:, None, :]) + shift[:, None, :]
    qkv_pre, z_dn, b_dn, a_dn, q_at, k_at, v_at, z_at = split_cols(h @ w_in)
    o_dn, conv_new, s_new = deltanet_branch(qkv_pre, z_dn, b_dn, a_dn, conv_buf, dn_state,
                                            conv_w, a_log, dt_bias, dn_norm)
    o_at, k_new, v_new = swa_branch(q_at, k_at, v_at, z_at, pos, sinks, win_k, win_v)
    y = jnp.concatenate([o_dn, o_at], axis=-1) @ w_out
    x = x + gate[:, None, :] * rms_norm(y, g_post)
    return x, conv_new, s_new.astype(x.dtype), k_new, v_new


def setup_inputs(seed: int = 0) -> dict:
    key = jax.random.key(seed)
    ks = jax.random.split(key, 22)
    f32 = jnp.float32

    def nrm(k, shape, s):
        return jax.random.normal(k, shape, f32) * s

    win_rows = min(WINDOW, PAST_LEN)
    dt = jnp.exp(jax.random.uniform(ks[15], (DEPTH, DN_HEADS), f32, math.log(1e-3), math.log(1e-1)))
    return {
        'x_prompt': nrm(ks[0], (BATCH, SEQ, D_MODEL), 1.0),
        'x_sample': nrm(ks[1], (DEC_BATCH, DEC_SEQ, D_MODEL), 1.0),
        'state_conv': nrm(ks[2], (DEPTH, DEC_BATCH, CONV_W - 1, DN_CONV_DIM), 1.0),
        'state_dn': nrm(ks[3], (DEPTH, DEC_BATCH, DN_HEADS, DN_HEAD_DIM, DN_HEAD_DIM), 0.05),
        'cache_k': nrm(ks[4], (DEPTH, DEC_BATCH, win_rows, ATT_KV_HEADS, ATT_HEAD_DIM), 1.0),
        'cache_v': nrm(ks[5], (DEPTH, DEC_BATCH, win_rows, ATT_KV_HEADS, ATT_HEAD_DIM), 1.0),
        'c_prompt': nrm(ks[6], (BATCH, D_MODEL), 1.0),
        'c_sample': nrm(ks[7], (DEC_BATCH, D_MODEL), 1.0),
        'w_ada': nrm(ks[8], (DEPTH, D_MODEL, 3 * D_MODEL), 0.5 * D_MODEL ** -0.5),
        'b_ada': nrm(ks[9], (DEPTH, 3 * D_MODEL), 0.02),
        'g_pre': 1.0 + nrm(ks[10], (DEPTH, D_MODEL), 0.05),
        'g_post': 1.0 + nrm(ks[11], (DEPTH, D_MODEL), 0.05),
        'w_in': nrm(ks[12], (DEPTH, D_MODEL, IN_DIM), D_MODEL ** -0.5),
        'conv_w': nrm(ks[13], (DEPTH, CONV_W, DN_CONV_DIM), CONV_W ** -0.5),
        'a_log': jnp.log(jax.random.uniform(ks[14], (DEPTH, DN_HEADS), f32, 1.0, 16.0)),
        'dt_bias': dt + jnp.log(-jnp.expm1(-dt)),
        'dn_norm': 1.0 + nrm(ks[16], (DEPTH, DN_HEAD_DIM), 0.05),
        'sinks': nrm(ks[17], (DEPTH, ATT_Q_HEADS), 0.5),
        'w_out': nrm(ks[18], (DEPTH, MIX_WIDTH, D_MODEL), MIX_WIDTH ** -0.5),
    }


def reference(x_prompt, x_sample, state_conv, state_dn, cache_k, cache_v, c_prompt, c_sample,
              w_ada, b_ada, g_pre, g_post, w_in, conv_w, a_log, dt_bias, dn_norm, sinks, w_out):
    bp, tp, _ = x_prompt.shape
    pos_p = jnp.arange(tp)
    pos_s = PAST_LEN + jnp.arange(x_sample.shape[1])
    zero_conv = jnp.zeros((bp, CONV_W - 1, DN_CONV_DIM), x_prompt.dtype)
    zero_state = jnp.zeros((bp, DN_HEADS, DN_HEAD_DIM, DN_HEAD_DIM), jnp.float32)
    xp, xs = x_prompt, x_sample
    conv_p, dn_p, k_p, v_p = [], [], [], []
    conv_s, dn_s, k_s, v_s = [], [], [], []
    for l in range(DEPTH):
        lw = (w_ada[l], b_ada[l], g_pre[l], g_post[l], w_in[l], conv_w[l],
              a_log[l], dt_bias[l], dn_norm[l], sinks[l], w_out[l])
        xp, cb, sb, kb, vb = layer(xp, c_prompt, pos_p, zero_conv, zero_state, None, None, *lw)
        conv_p.append(cb); dn_p.append(sb); k_p.append(kb); v_p.append(vb)
        xs, cb, sb, kb, vb = layer(xs, c_sample, pos_s, state_conv[l], state_dn[l],
                                   cache_k[l], cache_v[l], *lw)
        conv_s.append(cb); dn_s.append(sb); k_s.append(kb); v_s.append(vb)
    return (xp, xs,
            jnp.stack(conv_p), jnp.stack(dn_p), jnp.stack(k_p), jnp.stack(v_p),
            jnp.stack(conv_s), jnp.stack(dn_s), jnp.stack(k_s), jnp.stack(v_s))
```

```python
import math
import numpy as np
from contextlib import ExitStack
import concourse.bass as bass
import concourse.mybir as mybir
from concourse.bass_utils import run_bass_kernel_spmd

F32 = mybir.dt.float32
BF16 = mybir.dt.bfloat16
I32 = mybir.dt.int32
ALU = mybir.AluOpType
AF = mybir.ActivationFunctionType
AX = mybir.AxisListType

SEM_LIMIT = 30000
DMA_SLOTS = 12
NEG = -30000.0
EPS = 1e-6


class Buf:
    __slots__ = ("name", "w", "r")

    def __init__(self, name):
        self.name = name
        self.w = {}
        self.r = {}


class V:
    __slots__ = ("ap", "bufs")

    def __init__(self, ap, bufs):
        self.ap = ap
        self.bufs = bufs

    def __getitem__(self, idx):
        return V(self.ap[idx], self.bufs)

    def re(self, pat, **kw):
        return V(self.ap.rearrange(pat, **kw), self.bufs)

    def wb(self, *bufs):
        return V(self.ap, list(bufs))

    def bc(self, shape):
        return V(self.ap.broadcast_to(list(shape)), self.bufs)

    def sub(self, idx, name="s"):
        return V(self.ap[idx], [Buf(name)])


class Op:
    __slots__ = ("eng", "fn", "raw", "oth", "dma", "sig", "sem", "val", "id")


class Prog:
    def __init__(self, nc, stack):
        self.nc = nc
        self.stack = stack
        self.ops = []
        self.nname = 0

    def sb(self, shape, dt=F32, name=None):
        self.nname += 1
        name = f"{name or 't'}_{self.nname}"
        h = self.stack.enter_context(self.nc.sbuf_tensor(name, list(shape), dt))
        return V(h[:], [Buf(name)])

    def ps(self, shape, dt=F32, name=None):
        self.nname += 1
        name = f"{name or 'p'}_{self.nname}"
        h = self.stack.enter_context(self.nc.psum_tensor(name, list(shape), dt))
        return V(h[:], [Buf(name)])

    def tmp(self, shape, dt=F32, name="tmp", bufs=1):
        if not hasattr(self, "pools"):
            self.pools = {}
        shape = list(shape)
        key = (name, str(dt), len(shape))
        cands = self.pools.setdefault(key, [])
        pool = None
        for pl in cands:
            if all(a >= b for a, b in zip(pl[2], shape)) and len(pl[0]) >= bufs:
                pool = pl
                break
        if pool is None:
            full = [128] + shape[1:]
            pool = [[self.sb(full, dt, name) for _ in range(bufs)], 0, full]
            cands.append(pool)
        v = pool[0][pool[1] % len(pool[0])]
        pool[1] += 1
        if pool[2] != shape:
            v = v[tuple(slice(0, n) for n in shape)]
        return v

    def dram(self, name, shape, dt=F32, kind="Internal"):
        t = self.nc.dram_tensor(name, list(shape), dt, kind=kind)
        return V(t.ap(), [Buf(name)])

    def op(self, eng, fn, reads=(), writes=(), dma=False):
        i = len(self.ops)
        raw = set()
        oth = set()
        for v in reads:
            for b in v.bufs:
                raw.update(b.w.values())
        for v in writes:
            for b in v.bufs:
                oth.update(b.w.values())
                oth.update(b.r.values())
        key = ("d", i) if dma else eng
        for v in reads:
            for b in v.bufs:
                b.r[key] = i
        for v in writes:
            for b in v.bufs:
                b.w = {key: i}
                b.r = {}
        o = Op()
        o.eng, o.fn, o.raw, o.oth, o.dma, o.sig, o.id = eng, fn, raw, oth - raw, dma, False, i
        o.sem = o.val = None
        self.ops.append(o)
        return i

    def emit(self):
        nc = self.nc
        ops = self.ops
        engs = ["pe", "act", "dve", "pool", "sp"]
        waited = {e: {} for e in engs}
        waited_d = {e: set() for e in engs}
        need = []
        for o in ops:
            E = o.eng
            lst = []
            for d in sorted(o.raw | o.oth):
                p = ops[d]
                if p.dma:
                    if d in waited_d[E]:
                        continue
                    waited_d[E].add(d)
                    lst.append(d)
                else:
                    if p.eng == E and not o.dma:
                        if E == "pe" or (E in ("act", "dve") and d not in o.raw):
                            continue
                    if waited[E].get(p.eng, -1) >= d:
                        continue
                    waited[E][p.eng] = d
                    lst.append(d)
                    p.sig = True
            need.append(lst)
        cnt = {e: 0 for e in engs}
        for o in ops:
            if o.sig and not o.dma:
                cnt[o.eng] += 1
                o.val = cnt[o.eng]
        nsem = {e: (cnt[e] + SEM_LIMIT - 1) // SEM_LIMIT for e in engs}
        sems = {e: [self.stack.enter_context(nc.semaphore(f"s_{e}_{k}")) for k in range(nsem[e])] for e in engs}
        dma_engs = sorted({o.eng for o in ops if o.dma})
        dsem = {e: [[self.stack.enter_context(nc.semaphore(f"d_{e}_{k}_0")), 0, None] for k in range(DMA_SLOTS)]
                for e in dma_engs}
        dcount = {e: 0 for e in dma_engs}
        pre_wait = {}
        for o in ops:
            if o.dma:
                e = o.eng
                k = dcount[e] % DMA_SLOTS
                dcount[e] += 1
                slot = dsem[e][k]
                if slot[1] + 16 > SEM_LIMIT:
                    slot[0] = self.stack.enter_context(nc.semaphore(f"d_{e}_{k}_{o.id}"))
                    slot[1] = 0
                if slot[2] is not None:
                    pre_wait[o.id] = slot[2]
                slot[1] += 16
                o.sem, o.val = slot[0], slot[1]
                slot[2] = o.id

        def semval(p):
            if p.dma:
                return p.sem, p.val
            n = p.val - 1
            return sems[p.eng][n // SEM_LIMIT], (n % SEM_LIMIT) + 1

        by_eng = {e: [o for o in ops if o.eng == e] for e in engs}
        self.stats = {e: len(by_eng[e]) for e in engs}
        self.stats["sig"] = dict(cnt)

        def run(ename, eng):
            done_d = set()
            for o in by_eng[ename]:
                if o.id in pre_wait:
                    p = ops[pre_wait[o.id]]
                    if p.id not in done_d:
                        eng.wait_ge(p.sem, p.val)
                        done_d.add(p.id)
                for d in need[o.id]:
                    p = ops[d]
                    if p.dma:
                        if p.id in done_d:
                            continue
                        done_d.add(p.id)
                    s, v = semval(p)
                    eng.wait_ge(s, v)
                ins = o.fn(eng)
                if o.dma:
                    ins.then_inc(o.sem, 16)
                elif o.sig:
                    s, v = semval(o)
                    ins.then_inc(s, 1)

        with nc.Block() as block:
            @block.tensor
            def _(e):
                run("pe", e)

            @block.scalar
            def _(e):
                run("act", e)

            @block.vector
            def _(e):
                run("dve", e)

            @block.gpsimd
            def _(e):
                run("pool", e)

            @block.sync
            def _(e):
                run("sp", e)

    def dma(self, out, in_, eng="sp"):
        self.op(eng, lambda e: e.dma_start(out=out.ap, in_=in_.ap), [in_], [out], dma=True)

    def mm(self, out, lhsT, rhs, start=True, stop=True):
        self.op("pe", lambda e: e.matmul(out.ap, lhsT.ap, rhs.ap, start=start, stop=stop), [lhsT, rhs], [out])

    def tr(self, out, in_, ident):
        self.op("pe", lambda e: e.transpose(out.ap, in_.ap, ident.ap), [in_, ident], [out])

    def act(self, out, in_, func, bias=None, scale=None, accum=None):
        reads = [in_]
        kw = {}
        if isinstance(bias, V):
            reads.append(bias)
            kw["bias"] = bias.ap
        elif bias is not None:
            kw["bias"] = bias
        if isinstance(scale, V):
            reads.append(scale)
            kw["scale"] = scale.ap
        elif scale is not None:
            kw["scale"] = scale
        writes = [out]
        if accum is not None:
            kw["accum_out"] = accum.ap
            writes.append(accum)
        self.op("act", lambda e: e.activation(out.ap, in_.ap, func, **kw), reads, writes)

    def tt(self, eng, out, a, b, op):
        self.op(eng, lambda e: e.tensor_tensor(out.ap, a.ap, b.ap, op), [a, b], [out])

    def ts(self, eng, out, a, s1, op0, s2=None, op1=None):
        reads = [a]
        x1 = s1.ap if isinstance(s1, V) else s1
        x2 = s2.ap if isinstance(s2, V) else s2
        if isinstance(s1, V):
            reads.append(s1)
        if isinstance(s2, V):
            reads.append(s2)
        kw = {}
        if op1 is not None:
            kw["op1"] = op1
        self.op(eng, lambda e: e.tensor_scalar(out.ap, a.ap, x1, x2, op0, **kw), reads, [out])

    def stt(self, eng, out, a, s, b, op0, op1):
        reads = [a, b]
        x = s.ap if isinstance(s, V) else s
        if isinstance(s, V):
            reads.append(s)
        self.op(eng, lambda e: e.scalar_tensor_tensor(out.ap, a.ap, x, b.ap, op0, op1), reads, [out])

    def cp(self, eng, out, in_):
        if eng == "act":
            self.op(eng, lambda e: e.copy(out.ap, in_.ap), [in_], [out])
        else:
            self.op(eng, lambda e: e.tensor_copy(out.ap, in_.ap), [in_], [out])

    def memset(self, eng, out, val):
        self.op(eng, lambda e: e.memset(out.ap, val), [], [out])

    def recip(self, out, in_):
        self.op("dve", lambda e: e.reciprocal(out.ap, in_.ap), [in_], [out])

    def red(self, out, in_, op, axis=AX.X):
        self.op("dve", lambda e: e.tensor_reduce(out.ap, in_.ap, axis, op), [in_], [out])

    def asel(self, out, in_, pattern, cmp, fill, base, cm):
        self.op("pool", lambda e: e.affine_select(out.ap, in_.ap, pattern, cmp, fill, base=base,
                                                  channel_multiplier=cm), [in_], [out])

    def iota(self, out, pattern, base, cm):
        self.op("pool", lambda e: e.iota(out.ap, pattern, base=base, channel_multiplier=cm), [], [out])

    def fence(self, views, eng="sp"):
        self.op(eng, lambda e: e.nop(), list(views), [])


class StopBuild(Exception):
    pass


def make_cfg(D=4096, SEQ=4096, DEPTH=2, NSMP=2, DEC_SEQ=16, PAST=2048, NT=512, THETA=10000.0, STOP=0):
    c = dict(D=D, SEQ=SEQ, DEPTH=DEPTH, NSMP=NSMP, DEC_SEQ=DEC_SEQ, PAST=PAST, NT=NT, THETA=THETA, STOP=STOP)
    c["KC"] = D // 128
    c["DNW"] = D // 2
    c["H"] = c["DNW"] // 128
    c["CONV"] = 3 * c["DNW"]
    c["CT"] = c["CONV"] // 128
    c["AW"] = D - c["DNW"]
    c["QH"] = c["AW"] // 64
    c["KVH"] = c["QH"] // 8
    c["KVW"] = c["KVH"] * 64
    c["c_z"] = c["CONV"]
    c["c_b"] = c["c_z"] + c["DNW"]
    c["c_a"] = c["c_b"] + c["H"]
    c["c_q"] = c["c_a"] + c["H"]
    c["c_k"] = c["c_q"] + c["AW"]
    c["c_v"] = c["c_k"] + c["KVW"]
    c["c_za"] = c["c_v"] + c["KVW"]
    c["IN_DIM"] = c["c_za"] + c["AW"]
    c["WIN"] = 128
    return c


def build(cfg):
    D, SEQ, DEPTH, NSMP, LS, NT = cfg["D"], cfg["SEQ"], cfg["DEPTH"], cfg["NSMP"], cfg["DEC_SEQ"], cfg["NT"]
    KC, H, CT, CONV, DNW, AW, QH, KVH, KVW = (cfg[k] for k in ("KC", "H", "CT", "CONV", "DNW", "AW", "QH", "KVH", "KVW"))
    IN_DIM = cfg["IN_DIM"]
    NR = 1 + NSMP
    NBT = NT // 128
    NTILES = SEQ // NT
    NBS = SEQ // 128
    PW = max(NT, NSMP * LS)
    nc = bass.Bass("TRN2", target_bir_lowering=False)
    st = ExitStack()
    P = Prog(nc, st)

    EI, EO = "ExternalInput", "ExternalOutput"
    xp = P.dram("xp", [SEQ, D], F32, EI)
    xs = P.dram("xs", [NSMP * LS, D], F32, EI)
    sconv = P.dram("sconv", [DEPTH, NSMP, 3, CONV], F32, EI)
    sdn = P.dram("sdn", [DEPTH, NSMP, H, 128, 128], F32, EI)
    ck = P.dram("ck", [DEPTH, NSMP, 128, KVW], F32, EI)
    cv = P.dram("cv", [DEPTH, NSMP, 128, KVW], F32, EI)
    cvec = P.dram("cvec", [NR, D], F32, EI)
    w_ada = P.dram("w_ada", [DEPTH, D, 3 * D], F32, EI)
    b_ada = P.dram("b_ada", [DEPTH, 3 * D], F32, EI)
    g_pre = P.dram("g_pre", [DEPTH, D], F32, EI)
    g_post = P.dram("g_post", [DEPTH, D], F32, EI)
    w_in = P.dram("w_in", [DEPTH, D, IN_DIM], F32, EI)
    conv_w = P.dram("conv_w", [DEPTH, 4, CONV], F32, EI)
    a_log = P.dram("a_log", [DEPTH, H], F32, EI)
    dt_bias = P.dram("dt_bias", [DEPTH, H], F32, EI)
    dn_norm = P.dram("dn_norm", [DEPTH, 128], F32, EI)
    sinks = P.dram("sinks", [DEPTH, QH], F32, EI)
    w_out = P.dram("w_out", [DEPTH, D, D], F32, EI)

    yp = P.dram("yp", [SEQ, D], F32, EO)
    ys = P.dram("ys", [NSMP * LS, D], F32, EO)
    convp = P.dram("convp", [DEPTH, 3, CONV], F32, EO)
    dnp = P.dram("dnp", [DEPTH, H, 128, 128], F32, EO)
    kpo = P.dram("kpo", [DEPTH, 128, KVW], F32, EO)
    vpo = P.dram("vpo", [DEPTH, 128, KVW], F32, EO)
    convs = P.dram("convs", [DEPTH, NSMP, 3, CONV], F32, EO)
    dns = P.dram("dns", [DEPTH, NSMP, H, 128, 128], F32, EO)
    kso = P.dram("kso", [DEPTH, NSMP, 128, KVW], F32, EO)
    vso = P.dram("vso", [DEPTH, NSMP, 128, KVW], F32, EO)
    outs = [yp, ys, convp, dnp, kpo, vpo, convs, dns, kso, vso]
    ggd = P.dram("ggd", [NR, D], F32)
    xmid_p = [P.dram(f"xmid_p{l}", [SEQ, D], F32) for l in range(DEPTH - 1)]
    xmid_s = [P.dram(f"xmid_s{l}", [NSMP * LS, D], F32) for l in range(DEPTH - 1)]

    bankA = [P.ps([128, 512], F32, "bA") for _ in range(2)]
    bankT = [P.ps([128, 1024], BF16, "bT") for _ in range(2)]
    bankD = [P.ps([128, 512], F32, "bD") for _ in range(3)]
    bankS = P.ps([128, 512], F32, "bS")
    dslots = []
    for q in range(4):
        for b in bankD:
            dslots.append(V(b.ap[:, q * 128:(q + 1) * 128], b.bufs))
    tslots = []
    for q in range(8):
        for b in bankT:
            tslots.append(V(b.ap[:, q * 128:(q + 1) * 128], b.bufs))
    sslots = [V(bankS.ap[:, q * 256:(q + 1) * 256], bankS.bufs) for q in range(2)]
    ctr = {"A": 0, "D": 0, "T": 0, "S": 0, "W": 0}

    def nxt(kind, lst):
        v = lst[ctr[kind] % len(lst)]
        ctr[kind] += 1
        return v

    psA = lambda: nxt("A", bankA)
    psD = lambda: nxt("D", dslots)
    psT = lambda: nxt("T", tslots)
    psS = lambda: nxt("S", sslots)

    ident = P.sb([128, 128], F32, "ident")
    identb = P.sb([128, 128], BF16, "identb")
    ones = P.sb([128, 128], F32, "ones")
    P.memset("pool", ident, 0.0)
    P.asel(ident, ident, [[-1, 128]], ALU.not_equal, 1.0, 0, 1)
    P.cp("dve", identb, ident)
    P.memset("pool", ones, 1.0)

    def make_masks(nt, CL):
        nch = nt // CL
        tmp = P.sb([nt, nt], F32, "mtmp")
        mS = P.sb([nt, nt], BF16, "mS")
        mU = P.sb([nt, nt], BF16, "mU")
        tri = P.sb([nt, nt], F32, "tri")
        P.memset("pool", tmp, 0.0)
        P.asel(tmp, tmp, [[-1, nt]], ALU.is_gt, NEG, 0, 1)
        for c in range(1, nch):
            P.memset("pool", tmp[c * CL:(c + 1) * CL, 0:c * CL], NEG)
        P.cp("dve", mS, tmp)
        tmp2 = P.sb([nt, nt], F32, "mtmp2")
        P.memset("pool", tmp2, 0.0)
        P.asel(tmp2, tmp2, [[1, nt]], ALU.is_ge, NEG, 0, -1)
        for c in range(0, nch - 1):
            P.memset("pool", tmp2[c * CL:(c + 1) * CL, (c + 1) * CL:nt], NEG)
        P.cp("dve", mU, tmp2)
        P.memset("pool", tri, 1.0)
        P.asel(tri, tri, [[1, nt]], ALU.is_ge, 0.0, 0, -1)
        for c in range(0, nch - 1):
            P.memset("pool", tri[c * CL:(c + 1) * CL, (c + 1) * CL:nt], 0.0)
        elast = []
        for c in range(nch):
            e = P.sb([nt, 128], F32, "elast")
            P.memset("pool", e, 0.0)
            P.asel(e, e, [[0, 128]], ALU.not_equal, 1.0, -(c * CL + CL - 1), 1)
            elast.append(e)
        return dict(nt=nt, CL=CL, nch=nch, mS=mS, mU=mU, tri=tri, elast=elast)

    MK_P = make_masks(128, 64)
    MK_S = make_masks(LS, min(64, LS))
    amf = P.sb([128, 256], F32, "amf")
    amask = P.sb([128, 256], BF16, "amask")
    P.memset("pool", amf, 0.0)
    P.memset("pool", amf[0:64, 192:256], NEG)
    P.memset("pool", amf[64:128, 0:64], NEG)
    P.cp("dve", amask, amf)

    def rope_tables(npart, nblk, pos0):
        cosT = P.sb([npart, nblk, 32], F32, "cosT")
        sinT = P.sb([npart, nblk, 32], F32, "sinT")
        fi = P.tmp([npart, 32], I32, "fi")
        P.iota(fi, [[1, 32]], 0, 0)
        ff = P.tmp([npart, 32], F32, "ff")
        P.cp("dve", ff, fi)
        inv = P.tmp([npart, 32], F32, "inv")
        P.act(inv, ff, AF.Exp, scale=-math.log(cfg["THETA"]) / 32.0)
        CHB = 8
        for b0 in range(0, nblk, CHB):
            n = min(CHB, nblk - b0)
            posi = P.tmp([npart, n], I32, "posi")
            P.iota(posi, [[128, n]], pos0 + 128 * b0, 1)
            posf = P.tmp([npart, n], F32, "posf")
            P.cp("dve", posf, posi)
            ang = P.tmp([npart, n, 32], F32, "ang")
            P.tt("dve", ang, posf.re("p (b o) -> p b o", o=1).bc([npart, n, 32]),
                 inv.re("p (o f) -> p o f", o=1).bc([npart, n, 32]), ALU.mult)
            for shift, tab in ((0.0, sinT), (0.25, cosT)):
                r = P.tmp([npart, n, 32], F32, "rr")
                P.ts("dve", r, ang, 1.0 / (2.0 * math.pi), ALU.mult, shift, ALU.add)
                ri = P.tmp([npart, n, 32], I32, "ri")
                P.cp("dve", ri, r)
                rf = P.tmp([npart, n, 32], F32, "rf")
                P.cp("dve", rf, ri)
                P.tt("dve", r, r, rf, ALU.subtract)
                P.ts("dve", rf, r, 0.5, ALU.is_gt)
                P.tt("dve", r, r, rf, ALU.subtract)
                P.ts("dve", rf, r, -0.5, ALU.is_lt)
                P.tt("dve", r, r, rf, ALU.add)
                P.act(tab[:, b0:b0 + n, :], r, AF.Sin, scale=2.0 * math.pi)
        return cosT, sinT

    cos_p, sin_p = rope_tables(128, NBS, 0)
    cos_s, sin_s = rope_tables(LS, 1, cfg["PAST"])

    hT = P.sb([128, KC, PW], BF16, "hT")
    hT_bufs = [Buf("hTy") for _ in range(max(NBT, NSMP))]
    hT = hT.wb(*hT_bufs)
    ybuf_flat = V(hT.ap.rearrange("p k t -> p (k t)"), hT_bufs)
    oT = P.sb([128, KC, PW], BF16, "oT")
    oT_flat = oT.re("p k t -> p (k t)")
    xnb = oT_flat[:, D:2 * D]
    NWB = 2
    NBUFG = max(NBT, NSMP)
    wbufs = [P.sb([128, KC, 128], BF16, "wb") for _ in range(NWB)]
    xbuf = P.sb([128, D], F32, "xbuf")
    S_all = P.sb([128, H, 128], F32, "S_all")
    S_bf = P.sb([128, H, 128], BF16, "S_bf")
    S_bufs = [Buf("S") for _ in range(H)]
    Sb_bufs = [Buf("Sb") for _ in range(H)]
    hist = P.sb([128, CT, NSMP, 3], F32, "hist")
    hist_bufs = [Buf("hist") for _ in range(CT)]
    kring = P.sb([128, KVH, 2, 128], BF16, "kring")
    vring = P.sb([128, KVH, 2, 64], BF16, "vring")
    kr_bufs = [[Buf("kr") for _ in range(2)] for _ in range(KVH)]
    vr_bufs = [[Buf("vr") for _ in range(2)] for _ in range(KVH)]
    gsT = P.sb([128, KC, NR], F32, "gsT")
    shT = P.sb([128, KC, NR], F32, "shT")
    cwT = P.sb([128, CT, 4], F32, "cwT")
    nA_b = P.sb([128, H], F32, "nA")
    dtb_b = P.sb([128, H], F32, "dtb")
    dnn_b = P.sb([128, 128], F32, "dnn")
    sink_b = P.sb([128, QH], F32, "sink")
    nsink_b = P.sb([128, QH], F32, "nsink")

    wctr = [0]

    def load_w(src_cols_view):
        wb = wbufs[wctr[0] % NWB]
        wctr[0] += 1
        ncols = src_cols_view.ap.shape[1]
        dst = wb[:, :, 0:ncols]
        P.dma(dst, src_cols_view.re("(kc p) n -> p kc n", p=128), eng="pool")
        return dst

    def layer_params(l):
        cs = xbuf[0:NR, :]
        P.dma(cs, cvec)
        P.act(cs, cs, AF.Silu)
        scT = P.tmp([128, KC, NR], BF16, "scT")
        for kc in range(KC):
            pt = psD()
            P.tr(pt[:, 0:NR], cs[:, kc * 128:(kc + 1) * 128], ident[0:NR, 0:NR])
            P.cp("dve", scT[:, kc, :], pt[:, 0:NR])
        ncol = 3 * KC
        baT = P.tmp([128, ncol], F32, "baT")
        for c0 in range(0, ncol, 128):
            n = min(128, ncol - c0)
            rows = P.tmp([128, 128], F32, "rows", 2)
            P.dma(rows[0:n, :], b_ada[l, c0 * 128:(c0 + n) * 128].re("(r p) -> r p", p=128))
            pt = psD()
            P.tr(pt[:, 0:n], rows[0:n, :], ident[0:n, 0:n])
            P.cp("dve", baT[:, c0:c0 + n], pt[:, 0:n])
        gpT = P.tmp([128, KC], F32, "gpT")
        rows = P.tmp([128, 128], F32, "rows", 2)
        P.dma(rows[0:KC, :], g_pre[l].re("(r p) -> r p", p=128))
        pt = psD()
        P.tr(pt[:, 0:KC], rows[0:KC, :], ident[0:KC, 0:KC])
        P.cp("dve", gpT, pt[:, 0:KC])
        adaT = P.tmp([128, 2 * KC, NR], F32, "adaT")
        for j in range(3 * KC):
            wb = load_w(w_ada[l, :, j * 128:(j + 1) * 128])
            if j < 2 * KC:
                pt = psD()
                for kc in range(KC):
                    P.mm(pt[:, 0:NR], wb[:, kc, :], scT[:, kc, :], start=(kc == 0), stop=(kc == KC - 1))
                P.ts("dve", adaT[:, j, :], pt[:, 0:NR], baT[:, j:j + 1], ALU.add)
            else:
                pt = psD()
                for kc in range(KC):
                    P.mm(pt[0:NR, :], scT[:, kc, :], wb[:, kc, :], start=(kc == 0), stop=(kc == KC - 1))
                c0 = (j - 2 * KC) * 128
                bg = P.tmp([NR, 128], F32, "bg", 2)
                gp = P.tmp([NR, 128], F32, "gp", 2)
                P.dma(bg, b_ada[l:l + 1, 2 * D + c0:2 * D + c0 + 128].bc([NR, 128]))
                P.dma(gp, g_post[l:l + 1, c0:c0 + 128].bc([NR, 128]))
                gt = P.tmp([NR, 128], F32, "gt", 2)
                P.tt("dve", gt, pt[0:NR, :], bg, ALU.add)
                P.tt("pool", gt, gt, gp, ALU.mult)
                P.dma(ggd[:, c0:c0 + 128], gt)
        for r in range(NR):
            P.stt("dve", gsT[:, :, r], adaT[:, KC:2 * KC, r], 1.0, gpT, ALU.add, ALU.mult)
        P.cp("dve", shT, adaT[:, 0:KC, :])
        CH = min(32, KC)
        for c0 in range(0, CT, CH):
            n = min(CH, CT - c0)
            cwr = xbuf[0:4, 0:n * 128]
            P.dma(cwr, conv_w[l, :, c0 * 128:(c0 + n) * 128])
            pt = psD()
            for i in range(n):
                P.tr(pt[:, i * 4:(i + 1) * 4], cwr[:, i * 128:(i + 1) * 128], ident[0:4, 0:4])
            P.cp("dve", cwT[:, c0:c0 + n, :], pt[:, 0:4 * n].re("p (c j) -> p c j", j=4))
        al = P.tmp([128, H], F32, "al")
        P.dma(al, a_log[l:l + 1, :].bc([128, H]))
        P.act(al, al, AF.Exp)
        P.ts("dve", nA_b, al, -1.0, ALU.mult)
        P.dma(dtb_b, dt_bias[l:l + 1, :].bc([128, H]))
        P.dma(dnn_b, dn_norm[l:l + 1, :].bc([128, 128]))
        P.dma(sink_b, sinks[l:l + 1, :].bc([128, QH]))
        P.ts("dve", nsink_b, sink_b, -1.0, ALU.mult)

    def dn_gates(blk, ba_ps, MK):
        nt, CL, nch = MK["nt"], MK["CL"], MK["nch"]
        g = {}
        beta = P.tmp([nt, H], F32, "beta", NBUFG)
        P.act(beta, ba_ps[:, 0:H], AF.Sigmoid)
        xa = P.tmp([nt, H], F32, "xa", NBUFG)
        P.tt("dve", xa, ba_ps[:, H:2 * H], dtb_b[0:nt, :], ALU.add)
        P.act(xa, xa, AF.Exp)
        P.act(xa, xa, AF.Ln, bias=1.0)
        gg = P.tmp([nt, H], F32, "gg", NBUFG)
        P.tt("dve", gg, xa, nA_b[0:nt, :], ALU.mult)
        pt = psD()
        P.mm(pt[0:nt, 0:H], MK["tri"], gg)
        gc = P.tmp([nt, H], F32, "gc", NBUFG)
        P.cp("dve", gc, pt[0:nt, 0:H])
        egc = P.tmp([nt, H], F32, "egc", NBUFG)
        P.act(egc, gc, AF.Exp)
        bge = P.tmp([nt, H], F32, "bge", NBUFG)
        P.tt("dve", bge, beta, egc, ALU.mult)
        dec = P.tmp([128, nch, H], F32, "dec", NBUFG)
        kds = P.tmp([nt, H], F32, "kds", NBUFG)
        for c in range(nch):
            pg = psD()
            P.mm(pg[:, 0:H], MK["elast"][c], gc)
            P.act(dec[:, c, :], pg[:, 0:H], AF.Exp)
            rs = slice(c * CL, (c + 1) * CL)
            P.tt("dve", kds[rs, :], pg[rs, 0:H], gc[rs, :], ALU.subtract)
        P.act(kds, kds, AF.Exp)
        G1 = P.tmp([nt, H, 2], F32, "G1", NBUFG)
        G2 = P.tmp([nt, H, 2], F32, "G2", NBUFG)
        P.memset("pool", G1, 1.0)
        P.memset("pool", G2, 1.0)
        P.cp("dve", G1[:, :, 0], gc)
        P.ts("dve", G2[:, :, 1], gc, -1.0, ALU.mult)
        g.update(beta=beta, egc=egc, bge=bge, dec=dec, kds=kds, G1=G1, G2=G2)
        return g

    def dn_block(h, g, MK, qT, kT, vT, zs, Sv, Sbv, oT_dst):
        nt, CL, nch = MK["nt"], MK["CL"], MK["nch"]
        p1 = psD()
        P.mm(p1[0:2, 0:nt], g["G1"][:, h, :], ident[0:nt, 0:nt])
        R1 = P.tmp([2, nt], F32, "R1")
        P.cp("act", R1, p1[0:2, 0:nt])
        p2 = psD()
        P.mm(p2[0:2, 0:nt], g["G2"][:, h, :], ident[0:nt, 0:nt])
        R2 = P.tmp([2, nt], F32, "R2")
        P.cp("dve", R2, p2[0:2, 0:nt])
        stop_at(11)
        pd = psD()
        P.mm(pd[0:nt, 0:nt], identb[0:nt, 0:nt], MK["mS"], start=True, stop=False)
        P.mm(pd[0:nt, 0:nt], R1, R2, start=False, stop=True)
        gamS = P.tmp([nt, nt], F32, "gamS")
        P.act(gamS, pd[0:nt, 0:nt], AF.Exp)
        pdt = psD()
        P.mm(pdt[0:nt, 0:nt], identb[0:nt, 0:nt], MK["mU"], start=True, stop=False)
        P.mm(pdt[0:nt, 0:nt], R2, R1, start=False, stop=True)
        gamT = P.tmp([nt, nt], F32, "gamT")
        P.act(gamT, pdt[0:nt, 0:nt], AF.Exp)
        stop_at(12)
        pkk = psD()
        P.mm(pkk[0:nt, 0:nt], kT, kT)
        A = P.tmp([nt, nt], F32, "A")
        P.stt("dve", A, pkk[0:nt, 0:nt], g["beta"][:, h:h + 1], gamS, ALU.mult, ALU.mult)
        pkq = psD()
        P.mm(pkq[0:nt, 0:nt], kT, qT)
        qkmT = P.tmp([nt, nt], BF16, "qkmT")
        P.tt("dve", qkmT, pkq[0:nt, 0:nt], gamT, ALU.mult)
        stop_at(13)
        pat = psD()
        P.tr(pat[0:nt, 0:nt], A, ident[0:nt, 0:nt])
        AT = P.tmp([nt, nt], F32, "AT")
        P.cp("act", AT, pat[0:nt, 0:nt])
        RT = P.tmp([nt, nt], F32, "RT")
        P.tt("pool", RT, ident[0:nt, 0:nt], AT, ALU.subtract)
        stop_at(14)
        X, XT = A, AT
        nlev = int(math.log2(CL)) - 1
        for lev in range(nlev):
            px = psD()
            P.mm(px[0:nt, 0:nt], XT, X)
            X2 = P.tmp([nt, nt], F32, "X2", 2)
            P.cp("act", X2, px[0:nt, 0:nt])
            if lev < nlev - 1:
                pxt = psD()
                P.mm(pxt[0:nt, 0:nt], X, XT)
                X2T = P.tmp([nt, nt], F32, "X2T", 2)
                P.cp("dve", X2T, pxt[0:nt, 0:nt])
            pr = psD()
            P.mm(pr[0:nt, 0:nt], X2, RT)
            RT2 = P.tmp([nt, nt], F32, "RT2", 2)
            P.tt("dve", RT2, pr[0:nt, 0:nt], RT, ALU.add)
            RT = RT2
            if lev < nlev - 1:
                X, XT = X2, X2T
        stop_at(15)
        ptk = psT()
        P.tr(ptk[0:nt, :], kT, identb)
        ptv = psT()
        P.tr(ptv[0:nt, :], vT, identb)
        stop_at(151)
        RHSw = P.tmp([nt, 128], F32, "RHSw")
        P.act(RHSw, ptk[0:nt, :], AF.Identity, scale=g["bge"][:, h:h + 1])
        stop_at(152)
        kdec = P.tmp([nt, 128], BF16, "kdec")
        P.act(kdec, ptk[0:nt, :], AF.Identity, scale=g["kds"][:, h:h + 1])
        stop_at(153)
        RHSu = P.tmp([nt, 128], F32, "RHSu")
        P.act(RHSu, ptv[0:nt, :], AF.Identity, scale=g["beta"][:, h:h + 1])
        stop_at(154)
        pu = psD()
        P.mm(pu[0:nt, :], RT, RHSu)
        u = P.tmp([nt, 128], F32, "u")
        P.cp("act", u, pu[0:nt, :])
        stop_at(155)
        pw = psD()
        P.mm(pw[:, 0:nt], RHSw, RT)
        wT = P.tmp([128, nt], BF16, "wT")
        P.cp("dve", wT, pw[:, 0:nt])
        stop_at(16)
        vnew = P.tmp([nt, 128], BF16, "vnew")
        o = P.tmp([nt, 128], F32, "o")
        for c in range(nch):
            rs = slice(c * CL, (c + 1) * CL)
            pp1 = psD()
            P.mm(pp1[rs, :], wT[:, rs], Sbv)
            P.tt("dve", vnew[rs, :], u[rs, :], pp1[rs, :], ALU.subtract)
            pp2 = psD()
            P.mm(pp2[rs, :], qT[:, rs], Sbv)
            t2 = P.tmp([nt, 128], F32, "t2", 2)
            P.act(t2[rs, :], pp2[rs, :], AF.Identity, scale=g["egc"][rs, h:h + 1])
            pp3 = psD()
            P.mm(pp3[rs, :], qkmT[rs, rs], vnew[rs, :])
            P.tt("dve", o[rs, :], pp3[rs, :], t2[rs, :], ALU.add)
            pp4 = psD()
            P.mm(pp4, kdec[rs, :], vnew[rs, :])
            P.stt("dve", Sv, Sv, g["dec"][:, c, h:h + 1], pp4, ALU.mult, ALU.add)
            P.cp("act", Sbv, Sv)
        stop_at(17)
        junk = P.tmp([nt, 128], F32, "junk")
        ss = P.tmp([nt, 1], F32, "ss")
        P.act(junk, o, AF.Square, accum=ss)
        P.act(ss, ss, AF.Sqrt, scale=1.0 / 128.0, bias=EPS)
        P.recip(ss, ss)
        og = P.tmp([nt, 128], F32, "og")
        P.stt("dve", og, o, ss, dnn_b[0:nt, :], ALU.mult, ALU.mult)
        ogb = P.tmp([nt, 128], BF16, "ogb")
        P.tt("pool", ogb, og, zs, ALU.mult)
        pt = psT()
        P.tr(pt[:, 0:nt], ogb, identb[0:nt, 0:nt])
        P.cp("act", oT_dst, pt[:, 0:nt])

    def rope(dst, src, cosv, sinv, nt, nh):
        cb = cosv.re("p (o f) -> p o f", o=1).bc([nt, nh, 32])
        sb_ = sinv.re("p (o f) -> p o f", o=1).bc([nt, nh, 32])
        x1, x2 = src[:, :, 0:32], src[:, :, 32:64]
        t1 = P.tmp([nt, nh, 32], F32, "rt1")
        t2 = P.tmp([nt, nh, 32], F32, "rt2")
        P.tt("dve", t1, x2, sb_, ALU.mult)
        P.tt("dve", t2, x1, cb, ALU.mult)
        P.tt("pool", dst[:, :, 0:32], t2, t1, ALU.subtract)
        t3 = P.tmp([nt, nh, 32], F32, "rt3")
        t4 = P.tmp([nt, nh, 32], F32, "rt4")
        P.tt("dve", t3, x1, sb_, ALU.mult)
        P.tt("dve", t4, x2, cb, ALU.mult)
        P.tt("pool", dst[:, :, 32:64], t4, t3, ALU.add)

    def attn_head(nt, qTv, ksrcs, vsrcs, masked, hq, zsv, og_dst):
        sc = psS()
        nk_tot = sum(nk for _, nk in ksrcs)
        if masked:
            mcols = amask[0:nt, 256 - nk_tot:256]
            P.mm(sc[0:nt, 0:nk_tot], identb[0:nt, 0:nt], mcols, start=True, stop=False)
        off = 0
        for i, (kv_, nk) in enumerate(ksrcs):
            P.mm(sc[0:nt, off:off + nk], qTv, kv_, start=not masked, stop=(not masked) or (i == len(ksrcs) - 1))
            off += nk
        mraw = P.tmp([nt, 1], F32, "mraw", 2)
        P.red(mraw, sc[0:nt, 0:nk_tot], ALU.max)
        negm = P.tmp([nt, 1], F32, "negm", 2)
        P.ts("dve", negm, mraw, -0.125, ALU.mult, nsink_b[0:nt, hq:hq + 1], ALU.min)
        p = P.tmp([nt, 256], BF16, "p", 2)
        ssum = P.tmp([nt, 1], F32, "ssum", 2)
        P.act(p[:, 0:nk_tot], sc[0:nt, 0:nk_tot], AF.Exp, bias=negm, scale=0.125, accum=ssum)
        sk = P.tmp([nt, 1], F32, "sk", 2)
        P.act(sk, sink_b[0:nt, hq:hq + 1], AF.Exp, bias=negm)
        P.tt("dve", sk, sk, ssum, ALU.add)
        P.recip(sk, sk)
        pTs = P.tmp([128, 2, nt], BF16, "pTs", 2)
        off = 0
        for i, (_, nk) in enumerate(ksrcs):
            pt = psT()
            P.tr(pt[0:nk, 0:nt], p[:, off:off + nk], identb[0:nt, 0:nt])
            P.cp("act", pTs[0:nk, i, :], pt[0:nk, 0:nt])
            off += nk
        po = psD()
        for i, (_, nk) in enumerate(ksrcs):
            P.mm(po[0:nt, 0:64], pTs[0:nk, i, :], vsrcs[i], start=(i == 0), stop=(i == len(ksrcs) - 1))
        P.stt("dve", og_dst, po[0:nt, 0:64], sk, zsv, ALU.mult, ALU.mult)

    def do_tile(l, kind, t):
        last_layer = (l == DEPTH - 1)
        if kind == "p":
            x_in = xp if l == 0 else xmid_p[l - 1]
            x_out = yp if last_layer else xmid_p[l]
            blocks = [dict(r0=t * NT + b * 128, nt=128, row=0, c0=b * 128, seq=0, gb=t * NBT + b) for b in range(NBT)]
            MK = MK_P
            nseq, L = 1, NT
        else:
            x_in = xs if l == 0 else xmid_s[l - 1]
            x_out = ys if last_layer else xmid_s[l]
            blocks = [dict(r0=s * LS, nt=LS, row=1 + s, c0=s * LS, seq=s, gb=0) for s in range(NSMP)]
            MK = MK_S
            nseq, L = NSMP, LS
        NTt = sum(b["nt"] for b in blocks)
        nb = len(blocks)

        for bi, b in enumerate(blocks):
            nt = b["nt"]
            P.dma(xbuf[0:nt, :], x_in[b["r0"]:b["r0"] + nt, :])
            ss = P.tmp([128, 1], F32, "ss0", 2)
            P.act(oT_flat[0:nt, 0:D], xbuf[0:nt, :], AF.Square, accum=ss[0:nt, :])
            P.act(ss[0:nt, :], ss[0:nt, :], AF.Sqrt, scale=1.0 / D, bias=EPS)
            P.recip(ss[0:nt, :], ss[0:nt, :])
            P.ts("dve", xnb[0:nt, :], xbuf[0:nt, :], ss[0:nt, :], ALU.mult)
            for kc in range(KC):
                pt = psT()
                P.tr(pt[:, 0:nt], xnb[0:nt, kc * 128:(kc + 1) * 128], identb[0:nt, 0:nt])
                P.act(hT[:, kc, b["c0"]:b["c0"] + nt], pt[:, 0:nt], AF.Identity,
                      bias=shT[:, kc, b["row"]:b["row"] + 1], scale=gsT[:, kc, b["row"]:b["row"] + 1])

        stop_at(3)

        def proj_tm(wb, ncols):
            acc = psA()
            res = []
            for bi, b in enumerate(blocks):
                nt = b["nt"]
                dst = acc[0:nt, bi * ncols:(bi + 1) * ncols]
                for kc in range(KC):
                    P.mm(dst, hT[:, kc, b["c0"]:b["c0"] + nt], wb[:, kc, 0:ncols], start=(kc == 0), stop=(kc == KC - 1))
                res.append(dst)
            return res

        def proj_cm(wb):
            acc = psA()
            dst = acc[:, 0:NTt]
            for kc in range(KC):
                P.mm(dst, wb[:, kc, :], hT[:, kc, 0:NTt], start=(kc == 0), stop=(kc == KC - 1))
            return dst

        wb = load_w(w_in[l, :, cfg["c_b"]:cfg["c_b"] + 2 * H])
        ba = proj_tm(wb, 2 * H)
        gates = [dn_gates(b, ba[bi], MK) for bi, b in enumerate(blocks)]

        stop_at(4)
        for h in range(H):
            acts = []
            for which in range(3):
                ct = which * H + h
                wb = load_w(w_in[l, :, ct * 128:(ct + 1) * 128])
                acc = proj_cm(wb)
                pre = P.tmp([128, nseq, L + 3], F32, "pre", 2)
                hv = V(hist.ap[:, ct, 0:nseq, :], [hist_bufs[ct]])
                P.cp("pool", pre[:, :, 0:3], hv)
                P.cp("act", pre[:, :, 3:3 + L], acc.re("p (s t) -> p s t", s=nseq))
                P.cp("pool", hv, pre[:, :, L:L + 3])
                y = P.tmp([128, nseq, L], F32, "convy")
                P.act(y, pre[:, :, 0:L], AF.Identity, scale=cwT[:, ct, 0:1])
                for j in range(1, 4):
                    P.stt("dve", y, pre[:, :, j:j + L], cwT[:, ct, j:j + 1], y, ALU.mult, ALU.add)
                a_ = P.tmp([128, NTt], F32, "cact", 3)
                P.act(a_, y.re("p s t -> p (s t)"), AF.Silu)
                acts.append(a_)
            stop_at(8)
            qa, ka, va = acts
            normed = []
            for a_, scl in ((qa, 128.0 ** -0.5), (ka, 1.0)):
                sq = P.tmp([128, NTt], F32, "sq")
                P.tt("pool", sq, a_, a_, ALU.mult)
                acc = psA()
                P.mm(acc[:, 0:NTt], ones, sq)
                rs_ = P.tmp([128, NTt], F32, "rs")
                P.act(rs_, acc[:, 0:NTt], AF.Sqrt, bias=EPS)
                P.recip(rs_, rs_)
                nb_ = P.tmp([128, NTt], BF16, "nrm", 4)
                P.stt("dve", nb_, a_, scl, rs_, ALU.mult, ALU.mult)
                normed.append(nb_)
            stop_at(9)
            qTb, kTb = normed
            vTb = P.tmp([128, NTt], BF16, "vTb", 2)
            P.cp("pool", vTb, va)
            wb = load_w(w_in[l, :, cfg["c_z"] + h * 128:cfg["c_z"] + (h + 1) * 128])
            zps = proj_tm(wb, 128)
            zs_all = P.tmp([128, nb, 128], BF16, "zs", 2)
            for bi, b in enumerate(blocks):
                P.act(zs_all[0:b["nt"], bi, :], zps[bi], AF.Silu)
            stop_at(10)
            for bi, b in enumerate(blocks):
                nt, c0 = b["nt"], b["c0"]
                if kind == "p":
                    Sv = V(S_all.ap[:, h, :], [S_bufs[h]])
                    Sbv = V(S_bf.ap[:, h, :], [Sb_bufs[h]])
                    if b["gb"] == 0:
                        P.memset("pool", Sv, 0.0)
                        P.memset("pool", Sbv, 0.0)
                else:
                    Sv = P.tmp([128, 128], F32, "Ss", 2)
                    Sbv = P.tmp([128, 128], BF16, "Ssb", 2)
                    P.dma(Sv, sdn[l, b["seq"], h])
                    P.cp("act", Sbv, Sv)
                dn_block(h, gates[bi], MK, qTb[:, c0:c0 + nt], kTb[:, c0:c0 + nt], vTb[:, c0:c0 + nt],
                         zs_all[0:nt, bi, :], Sv, Sbv, oT[:, h, c0:c0 + nt])
                if kind == "s":
                    P.dma(dns[l, b["seq"], h], Sv)
                elif b["gb"] == NBS - 1:
                    P.dma(dnp[l, h], Sv)
        stop_at(5)
        if kind == "s" or t == NTILES - 1:
            CH = min(32, KC)
            for s in range(nseq):
                for c0 in range(0, CT, CH):
                    n = min(CH, CT - c0)
                    crow = xbuf[0:3, 0:n * 128]
                    for c1 in range(0, n, 4):
                        pt = psA()
                        for i in range(4):
                            hv = V(hist.ap[:, c0 + c1 + i, s, :], [hist_bufs[c0 + c1 + i]])
                            P.tr(pt[0:3, i * 128:(i + 1) * 128], hv, ident)
                        P.cp("dve", crow[:, c1 * 128:(c1 + 4) * 128], pt[0:3, :])
                    dst = convs[l, s] if kind == "s" else convp[l]
                    P.dma(dst[:, c0 * 128:(c0 + n) * 128], crow)

        stop_at(6)
        for gk in range(KVH):
            wbk = load_w(w_in[l, :, cfg["c_k"] + gk * 64:cfg["c_k"] + (gk + 1) * 64])
            kps = proj_tm(wbk, 64)
            krot = []
            for bi, b in enumerate(blocks):
                nt = b["nt"]
                ksb = P.tmp([128, 1, 64], F32, "ksb", 2)
                P.cp("act", ksb[0:nt, 0, :], kps[bi])
                kr = P.tmp([128, 1, 64], F32, "kr", 4)
                if kind == "p":
                    cosv, sinv = cos_p[:, b["gb"], :], sin_p[:, b["gb"], :]
                else:
                    cosv, sinv = cos_s[:, 0, :], sin_s[:, 0, :]
                rope(kr[0:nt], ksb[0:nt], cosv[0:nt], sinv[0:nt], nt, 1)
                krot.append(kr)
                if kind == "s":
                    P.dma(kso[l, b["seq"], 128 - LS:128, gk * 64:(gk + 1) * 64], kr[0:nt, 0, :])
                elif b["gb"] == NBS - 1:
                    P.dma(kpo[l, :, gk * 64:(gk + 1) * 64], kr[0:nt, 0, :])
            wbv = load_w(w_in[l, :, cfg["c_v"] + gk * 64:cfg["c_v"] + (gk + 1) * 64])
            vps = proj_tm(wbv, 64)
            vsb = []
            for bi, b in enumerate(blocks):
                nt = b["nt"]
                vf = P.tmp([128, 64], F32, "vf", 4)
                P.cp("act", vf[0:nt, :], vps[bi])
                vsb.append(vf)
                if kind == "s":
                    P.dma(vso[l, b["seq"], 128 - LS:128, gk * 64:(gk + 1) * 64], vf[0:nt, :])
                elif b["gb"] == NBS - 1:
                    P.dma(vpo[l, :, gk * 64:(gk + 1) * 64], vf[0:nt, :])
            qrot = [P.tmp([128, 8, 64], BF16, "qrotb", 4) for _ in blocks]
            zat = [P.tmp([128, 512], BF16, "zat", 4) for _ in blocks]
            for part in range(4):
                cq = cfg["c_q"] + gk * 512 + part * 128
                wbq = load_w(w_in[l, :, cq:cq + 128])
                qps = proj_tm(wbq, 128)
                for bi, b in enumerate(blocks):
                    nt = b["nt"]
                    qsb = P.tmp([128, 2, 64], F32, "qsb", 2)
                    P.cp("act", qsb[0:nt], qps[bi].re("p (h d) -> p h d", h=2))
                    if kind == "p":
                        cosv, sinv = cos_p[:, b["gb"], :], sin_p[:, b["gb"], :]
                    else:
                        cosv, sinv = cos_s[:, 0, :], sin_s[:, 0, :]
                    rope(qrot[bi][0:nt, 2 * part:2 * part + 2, :], qsb[0:nt], cosv[0:nt], sinv[0:nt], nt, 2)
            for part in range(4):
                cz = cfg["c_za"] + gk * 512 + part * 128
                wbz = load_w(w_in[l, :, cz:cz + 128])
                zps = proj_tm(wbz, 128)
                for bi, b in enumerate(blocks):
                    P.act(zat[bi][0:b["nt"], part * 128:(part + 1) * 128], zps[bi], AF.Silu)
            for bi, b in enumerate(blocks):
                nt, c0 = b["nt"], b["c0"]
                if kind == "p":
                    slot = b["gb"] % 2
                    pslot = 1 - slot
                    has_prev = b["gb"] > 0
                else:
                    slot, pslot, has_prev = 1, 0, True
                    ckf = P.tmp([128, 64], F32, "ckf")
                    P.dma(ckf, ck[l, b["seq"], :, gk * 64:(gk + 1) * 64])
                    ckd = P.tmp([128, 2, 64], BF16, "ckd")
                    P.cp("dve", ckd[:, 0, :], ckf)
                    P.cp("pool", ckd[:, 1, :], ckf)
                    pt = psT()
                    P.tr(pt[:, 0:128], ckd.re("p a d -> p (a d)"), identb)
                    P.cp("act", V(kring.ap[:, gk, 0, :], [kr_bufs[gk][0]]), pt[:, 0:128])
                    cvf = P.tmp([128, 64], F32, "cvf")
                    P.dma(cvf, cv[l, b["seq"], :, gk * 64:(gk + 1) * 64])
                    P.cp("dve", V(vring.ap[:, gk, 0, :], [vr_bufs[gk][0]]), cvf)
                    if gk == 0 and LS < 128:
                        P.dma(kso[l, b["seq"], 0:128 - LS, :], ck[l, b["seq"], LS:128, :])
                        P.dma(vso[l, b["seq"], 0:128 - LS, :], cv[l, b["seq"], LS:128, :])
                kd = P.tmp([128, 2, 64], BF16, "kd", 2)
                P.cp("dve", kd[0:nt, 0, :], krot[bi][0:nt, 0, :])
                P.cp("pool", kd[0:nt, 1, :], krot[bi][0:nt, 0, :])
                pt = psT()
                P.tr(pt[:, 0:nt], kd[0:nt].re("p a d -> p (a d)"), identb[0:nt, 0:nt])
                kcur = V(kring.ap[:, gk, slot, 0:nt], [kr_bufs[gk][slot]])
                P.cp("act", kcur, pt[:, 0:nt])
                vcur = V(vring.ap[0:nt, gk, slot, :], [vr_bufs[gk][slot]])
                P.cp("dve", vcur, vsb[bi][0:nt, :])
                kprev = V(kring.ap[:, gk, pslot, :], [kr_bufs[gk][pslot]])
                vprev = V(vring.ap[:, gk, pslot, :], [vr_bufs[gk][pslot]])
                qb = qrot[bi]
                qT = P.tmp([128, 4, 128], BF16, "qT", 2)
                for pr_ in range(4):
                    pt = psT()
                    P.tr(pt[:, 0:nt], qb[0:nt, 2 * pr_:2 * pr_ + 2, :].re("p a d -> p (a d)"), identb[0:nt, 0:nt])
                    P.cp("act", qT[:, pr_, 0:nt], pt[:, 0:nt])
                ogat = P.tmp([128, 512], BF16, "ogat", 2)
                for hh in range(8):
                    base = 64 * (hh % 2)
                    qTv = qT[base:base + 64, hh // 2, 0:nt]
                    ksrcs, vsrcs = [], []
                    if has_prev:
                        ksrcs.append((kprev[base:base + 64, :], 128))
                        vsrcs.append(vprev)
                    ksrcs.append((kcur[base:base + 64, :], nt))
                    vsrcs.append(vcur)
                    attn_head(nt, qTv, ksrcs, vsrcs, kind == "p", gk * 8 + hh,
                              zat[bi][0:nt, hh * 64:(hh + 1) * 64], ogat[0:nt, hh * 64:(hh + 1) * 64])
                for q4 in range(4):
                    pt = psT()
                    P.tr(pt[:, 0:nt], ogat[0:nt, q4 * 128:(q4 + 1) * 128], identb[0:nt, 0:nt])
                    P.cp("act", oT[:, H + gk * 4 + q4, c0:c0 + nt], pt[:, 0:nt])

        stop_at(7)
        yv = [V(ybuf_flat.ap[:, bi * D:(bi + 1) * D], [hT_bufs[bi]]) for bi in range(nb)]
        for n in range(KC):
            wbo = load_w(w_out[l, :, n * 128:(n + 1) * 128])
            acc = psA()
            for bi, b in enumerate(blocks):
                nt, c0 = b["nt"], b["c0"]
                dst = acc[0:nt, bi * 128:(bi + 1) * 128]
                for c in range(KC):
                    P.mm(dst, oT[:, c, c0:c0 + nt], wbo[:, c, :], start=(c == 0), stop=(c == KC - 1))
                P.cp("act" if bi % 2 == 0 else "dve", yv[bi][0:nt, n * 128:(n + 1) * 128], dst)
        for bi, b in enumerate(blocks):
            nt = b["nt"]
            ss = P.tmp([128, 1], F32, "ss2", 2)
            P.act(oT_flat[0:nt, 0:D], yv[bi][0:nt, :], AF.Square, accum=ss[0:nt, :])
            P.act(ss[0:nt, :], ss[0:nt, :], AF.Sqrt, scale=1.0 / D, bias=EPS)
            P.recip(ss[0:nt, :], ss[0:nt, :])
            P.dma(xbuf[0:nt, :], x_in[b["r0"]:b["r0"] + nt, :])
            for j in range(0, D, 512):
                w_ = min(512, D - j)
                pg = P.tmp([128, 512], F32, "ggp", 1)
                P.dma(pg[0:nt, 0:w_], ggd[b["row"]:b["row"] + 1, j:j + w_].bc([nt, w_]))
                tmp = P.tmp([128, 512], F32, "tmpo", 1)
                P.stt("dve", tmp[0:nt, 0:w_], yv[bi][0:nt, j:j + w_], ss[0:nt, :], pg[0:nt, 0:w_], ALU.mult, ALU.mult)
                P.tt("pool", xbuf[0:nt, j:j + w_], xbuf[0:nt, j:j + w_], tmp[0:nt, 0:w_], ALU.add)
            P.dma(x_out[b["r0"]:b["r0"] + nt, :], xbuf[0:nt, :])

    def main_body():
        for l in range(DEPTH):
            stop_at(1)
            layer_params(l)
            stop_at(2)
            for ct in range(CT):
                P.memset("pool", V(hist.ap[:, ct, :, :], [hist_bufs[ct]]), 0.0)
            for t in range(NTILES):
                do_tile(l, "p", t)
            CH = min(32, KC)
            for s in range(NSMP):
                for c0 in range(0, CT, CH):
                    n = min(CH, CT - c0)
                    srow = xbuf[0:3, 0:n * 128]
                    P.dma(srow, sconv[l, s, :, c0 * 128:(c0 + n) * 128])
                    pt = psD()
                    for i in range(n):
                        P.tr(pt[:, i * 3:(i + 1) * 3], srow[:, i * 128:(i + 1) * 128], ident[0:3, 0:3])
                    for i in range(n):
                        P.cp("dve", V(hist.ap[:, c0 + i, s, :], [hist_bufs[c0 + i]]), pt[:, i * 3:(i + 1) * 3])
            do_tile(l, "s", 0)

    def stop_at(k):
        if cfg["STOP"] == k:
            raise StopBuild()

    try:
        main_body()
    except StopBuild:
        pass
    P.fence(outs)
    P.emit()
    st.close()
    return nc, P


_CACHE = {}


def _get_prog(cfg_key, cfg):
    if cfg_key not in _CACHE:
        _CACHE[cfg_key] = build(cfg)
    return _CACHE[cfg_key]


def make_in_maps(cfg, ncores, inputs):
    f = lambda a: np.ascontiguousarray(np.asarray(a, dtype=np.float32))
    B = inputs["x_prompt"].shape[0]
    NSMP, DEPTH = cfg["NSMP"], cfg["DEPTH"]
    KVW = cfg["KVW"]
    in_maps = []
    shared = {k: f(inputs[k]) for k in ("w_ada", "b_ada", "g_pre", "g_post", "w_in", "conv_w", "a_log", "dt_bias",
                                        "dn_norm", "sinks", "w_out")}
    for i in range(ncores):
        pb = i % B
        ss = slice(NSMP * i, NSMP * (i + 1))
        m = dict(shared)
        m["xp"] = f(inputs["x_prompt"][pb])
        m["xs"] = f(inputs["x_sample"][ss]).reshape(-1, cfg["D"])
        m["sconv"] = f(inputs["state_conv"][:, ss])
        m["sdn"] = f(inputs["state_dn"][:, ss])
        m["ck"] = f(inputs["cache_k"][:, ss]).reshape(DEPTH, NSMP, 128, KVW)
        m["cv"] = f(inputs["cache_v"][:, ss]).reshape(DEPTH, NSMP, 128, KVW)
        m["cvec"] = f(np.concatenate([inputs["c_prompt"][pb:pb + 1], inputs["c_sample"][ss]], axis=0))
        in_maps.append(m)
    return in_maps


def assemble(cfg, ncores, B, R):
    NSMP, DEPTH = cfg["NSMP"], cfg["DEPTH"]
    H, KVH = cfg["H"], cfg["KVH"]
    y_p = np.stack([R[i]["yp"] for i in range(B)], 0)
    y_s = np.concatenate([R[i]["ys"].reshape(NSMP, cfg["DEC_SEQ"], cfg["D"]) for i in range(ncores)], 0)
    conv_p = np.stack([R[i]["convp"] for i in range(B)], 1)
    dn_p = np.stack([R[i]["dnp"] for i in range(B)], 1)
    k_p = np.stack([R[i]["kpo"].reshape(DEPTH, 128, KVH, 64) for i in range(B)], 1)
    v_p = np.stack([R[i]["vpo"].reshape(DEPTH, 128, KVH, 64) for i in range(B)], 1)
    conv_s = np.concatenate([R[i]["convs"] for i in range(ncores)], 1)
    dn_s = np.concatenate([R[i]["dns"] for i in range(ncores)], 1)
    k_s = np.concatenate([R[i]["kso"].reshape(DEPTH, NSMP, 128, KVH, 64) for i in range(ncores)], 1)
    v_s = np.concatenate([R[i]["vso"].reshape(DEPTH, NSMP, 128, KVH, 64) for i in range(ncores)], 1)
    return tuple(np.ascontiguousarray(a, dtype=np.float32) for a in
                 (y_p, y_s, conv_p, dn_p, k_p, v_p, conv_s, dn_s, k_s, v_s))


def run_cores(cfg, ncores, inputs):
    nc, _ = _get_prog(tuple(sorted(cfg.items())), cfg)
    in_maps = make_in_maps(cfg, ncores, inputs)
    res = run_bass_kernel_spmd(nc, in_maps, core_ids=list(range(ncores)))
    return assemble(cfg, ncores, inputs["x_prompt"].shape[0], res.results)


def kernel(**inputs):
    cfg = make_cfg()
    return run_cores(cfg, 8, inputs)
```

```python
import math
import numpy as np
from contextlib import ExitStack
import concourse.bass as bass
import concourse.mybir as mybir
from concourse.bass_utils import run_bass_kernel_spmd

F32 = mybir.dt.float32
BF16 = mybir.dt.bfloat16
I32 = mybir.dt.int32
ALU = mybir.AluOpType
AF = mybir.ActivationFunctionType
AX = mybir.AxisListType

SEM_LIMIT = 30000
DMA_SLOTS = 12
NEG = -30000.0
EPS = 1e-6


class Buf:
    __slots__ = ("name", "w", "r")

    def __init__(self, name):
        self.name = name
        self.w = {}
        self.r = {}


class V:
    __slots__ = ("ap", "bufs")

    def __init__(self, ap, bufs):
        self.ap = ap
        self.bufs = bufs

    def __getitem__(self, idx):
        return V(self.ap[idx], self.bufs)

    def re(self, pat, **kw):
        return V(self.ap.rearrange(pat, **kw), self.bufs)

    def wb(self, *bufs):
        return V(self.ap, list(bufs))

    def bc(self, shape):
        return V(self.ap.broadcast_to(list(shape)), self.bufs)

    def sub(self, idx, name="s"):
        return V(self.ap[idx], [Buf(name)])


class Op:
    __slots__ = ("eng", "fn", "raw", "oth", "dma", "sig", "sem", "val", "id")


class Prog:
    def __init__(self, nc, stack):
        self.nc = nc
        self.stack = stack
        self.ops = []
        self.nname = 0

    def sb(self, shape, dt=F32, name=None):
        self.nname += 1
        name = f"{name or 't'}_{self.nname}"
        h = self.stack.enter_context(self.nc.sbuf_tensor(name, list(shape), dt))
        return V(h[:], [Buf(name)])

    def ps(self, shape, dt=F32, name=None):
        self.nname += 1
        name = f"{name or 'p'}_{self.nname}"
        h = self.stack.enter_context(self.nc.psum_tensor(name, list(shape), dt))
        return V(h[:], [Buf(name)])

    def tmp(self, shape, dt=F32, name="tmp", bufs=1):
        if not hasattr(self, "pools"):
            self.pools = {}
        shape = list(shape)
        key = (name, str(dt), len(shape))
        cands = self.pools.setdefault(key, [])
        pool = None
        for pl in cands:
            if all(a >= b for a, b in zip(pl[2], shape)) and len(pl[0]) >= bufs:
                pool = pl
                break
        if pool is None:
            full = [128] + shape[1:]
            pool = [[self.sb(full, dt, name) for _ in range(bufs)], 0, full]
            cands.append(pool)
        v = pool[0][pool[1] % len(pool[0])]
        pool[1] += 1
        if pool[2] != shape:
            v = v[tuple(slice(0, n) for n in shape)]
        return v

    def dram(self, name, shape, dt=F32, kind="Internal"):
        t = self.nc.dram_tensor(name, list(shape), dt, kind=kind)
        return V(t.ap(), [Buf(name)])

    def op(self, eng, fn, reads=(), writes=(), dma=False):
        i = len(self.ops)
        raw = set()
        oth = set()
        for v in reads:
            for b in v.bufs:
                raw.update(b.w.values())
        for v in writes:
            for b in v.bufs:
                oth.update(b.w.values())
                oth.update(b.r.values())
        key = ("d", i) if dma else eng
        for v in reads:
            for b in v.bufs:
                b.r[key] = i
        for v in writes:
            for b in v.bufs:
                b.w = {key: i}
                b.r = {}
        o = Op()
        o.eng, o.fn, o.raw, o.oth, o.dma, o.sig, o.id = eng, fn, raw, oth - raw, dma, False, i
        o.sem = o.val = None
        self.ops.append(o)
        return i

    def emit(self):
        nc = self.nc
        ops = self.ops
        engs = ["pe", "act", "dve", "pool", "sp"]
        waited = {e: {} for e in engs}
        waited_d = {e: set() for e in engs}
        need = []
        for o in ops:
            E = o.eng
            lst = []
            for d in sorted(o.raw | o.oth):
                p = ops[d]
                if p.dma:
                    if d in waited_d[E]:
                        continue
                    waited_d[E].add(d)
                    lst.append(d)
                else:
                    if p.eng == E and not o.dma:
                        if E == "pe" or (E in ("act", "dve") and d not in o.raw):
                            continue
                    if waited[E].get(p.eng, -1) >= d:
                        continue
                    waited[E][p.eng] = d
                    lst.append(d)
                    p.sig = True
            need.append(lst)
        cnt = {e: 0 for e in engs}
        for o in ops:
            if o.sig and not o.dma:
                cnt[o.eng] += 1
                o.val = cnt[o.eng]
        nsem = {e: (cnt[e] + SEM_LIMIT - 1) // SEM_LIMIT for e in engs}
        sems = {e: [self.stack.enter_context(nc.semaphore(f"s_{e}_{k}")) for k in range(nsem[e])] for e in engs}
        dma_engs = sorted({o.eng for o in ops if o.dma})
        dsem = {e: [[self.stack.enter_context(nc.semaphore(f"d_{e}_{k}_0")), 0, None] for k in range(DMA_SLOTS)]
                for e in dma_engs}
        dcount = {e: 0 for e in dma_engs}
        pre_wait = {}
        for o in ops:
            if o.dma:
                e = o.eng
                k = dcount[e] % DMA_SLOTS
                dcount[e] += 1
                slot = dsem[e][k]
                if slot[1] + 16 > SEM_LIMIT:
                    slot[0] = self.stack.enter_context(nc.semaphore(f"d_{e}_{k}_{o.id}"))
                    slot[1] = 0
                if slot[2] is not None:
                    pre_wait[o.id] = slot[2]
                slot[1] += 16
                o.sem, o.val = slot[0], slot[1]
                slot[2] = o.id

        def semval(p):
            if p.dma:
                return p.sem, p.val
            n = p.val - 1
            return sems[p.eng][n // SEM_LIMIT], (n % SEM_LIMIT) + 1

        by_eng = {e: [o for o in ops if o.eng == e] for e in engs}
        self.stats = {e: len(by_eng[e]) for e in engs}
        self.stats["sig"] = dict(cnt)

        def run(ename, eng):
            done_d = set()
            for o in by_eng[ename]:
                if o.id in pre_wait:
                    p = ops[pre_wait[o.id]]
                    if p.id not in done_d:
                        eng.wait_ge(p.sem, p.val)
                        done_d.add(p.id)
                for d in need[o.id]:
                    p = ops[d]
                    if p.dma:
                        if p.id in done_d:
                            continue
                        done_d.add(p.id)
                    s, v = semval(p)
                    eng.wait_ge(s, v)
                ins = o.fn(eng)
                if o.dma:
                    ins.then_inc(o.sem, 16)
                elif o.sig:
                    s, v = semval(o)
                    ins.then_inc(s, 1)

        with nc.Block() as block:
            @block.tensor
            def _(e):
                run("pe", e)

            @block.scalar
            def _(e):
                run("act", e)

            @block.vector
            def _(e):
                run("dve", e)

            @block.gpsimd
            def _(e):
                run("pool", e)

            @block.sync
            def _(e):
                run("sp", e)

    def dma(self, out, in_, eng="sp"):
        self.op(eng, lambda e: e.dma_start(out=out.ap, in_=in_.ap), [in_], [out], dma=True)

    def mm(self, out, lhsT, rhs, start=True, stop=True):
        self.op("pe", lambda e: e.matmul(out.ap, lhsT.ap, rhs.ap, start=start, stop=stop), [lhsT, rhs], [out])

    def tr(self, out, in_, ident):
        self.op("pe", lambda e: e.transpose(out.ap, in_.ap, ident.ap), [in_, ident], [out])

    def act(self, out, in_, func, bias=None, scale=None, accum=None):
        reads = [in_]
        kw = {}
        if isinstance(bias, V):
            reads.append(bias)
            kw["bias"] = bias.ap
        elif bias is not None:
            kw["bias"] = bias
        if isinstance(scale, V):
            reads.append(scale)
            kw["scale"] = scale.ap
        elif scale is not None:
            kw["scale"] = scale
        writes = [out]
        if accum is not None:
            kw["accum_out"] = accum.ap
            writes.append(accum)
        self.op("act", lambda e: e.activation(out.ap, in_.ap, func, **kw), reads, writes)

    def tt(self, eng, out, a, b, op):
        self.op(eng, lambda e: e.tensor_tensor(out.ap, a.ap, b.ap, op), [a, b], [out])

    def ts(self, eng, out, a, s1, op0, s2=None, op1=None):
        reads = [a]
        x1 = s1.ap if isinstance(s1, V) else s1
        x2 = s2.ap if isinstance(s2, V) else s2
        if isinstance(s1, V):
            reads.append(s1)
        if isinstance(s2, V):
            reads.append(s2)
        kw = {}
        if op1 is not None:
            kw["op1"] = op1
        self.op(eng, lambda e: e.tensor_scalar(out.ap, a.ap, x1, x2, op0, **kw), reads, [out])

    def stt(self, eng, out, a, s, b, op0, op1):
        reads = [a, b]
        x = s.ap if isinstance(s, V) else s
        if isinstance(s, V):
            reads.append(s)
        self.op(eng, lambda e: e.scalar_tensor_tensor(out.ap, a.ap, x, b.ap, op0, op1), reads, [out])

    def cp(self, eng, out, in_):
        if eng == "act":
            self.op(eng, lambda e: e.copy(out.ap, in_.ap), [in_], [out])
        else:
            self.op(eng, lambda e: e.tensor_copy(out.ap, in_.ap), [in_], [out])

    def memset(self, eng, out, val):
        self.op(eng, lambda e: e.memset(out.ap, val), [], [out])

    def recip(self, out, in_):
        self.op("dve", lambda e: e.reciprocal(out.ap, in_.ap), [in_], [out])

    def red(self, out, in_, op, axis=AX.X):
        self.op("dve", lambda e: e.tensor_reduce(out.ap, in_.ap, axis, op), [in_], [out])

    def asel(self, out, in_, pattern, cmp, fill, base, cm):
        self.op("pool", lambda e: e.affine_select(out.ap, in_.ap, pattern, cmp, fill, base=base,
                                                  channel_multiplier=cm), [in_], [out])

    def iota(self, out, pattern, base, cm):
        self.op("pool", lambda e: e.iota(out.ap, pattern, base=base, channel_multiplier=cm), [], [out])

    def fence(self, views, eng="sp"):
        self.op(eng, lambda e: e.nop(), list(views), [])


class StopBuild(Exception):
    pass


def make_cfg(D=4096, SEQ=4096, DEPTH=2, NSMP=2, DEC_SEQ=16, PAST=2048, NT=512, THETA=10000.0, STOP=0):
    c = dict(D=D, SEQ=SEQ, DEPTH=DEPTH, NSMP=NSMP, DEC_SEQ=DEC_SEQ, PAST=PAST, NT=NT, THETA=THETA, STOP=STOP)
    c["KC"] = D // 128
    c["DNW"] = D // 2
    c["H"] = c["DNW"] // 128
    c["CONV"] = 3 * c["DNW"]
    c["CT"] = c["CONV"] // 128
    c["AW"] = D - c["DNW"]
    c["QH"] = c["AW"] // 64
    c["KVH"] = c["QH"] // 8
    c["KVW"] = c["KVH"] * 64
    c["c_z"] = c["CONV"]
    c["c_b"] = c["c_z"] + c["DNW"]
    c["c_a"] = c["c_b"] + c["H"]
    c["c_q"] = c["c_a"] + c["H"]
    c["c_k"] = c["c_q"] + c["AW"]
    c["c_v"] = c["c_k"] + c["KVW"]
    c["c_za"] = c["c_v"] + c["KVW"]
    c["IN_DIM"] = c["c_za"] + c["AW"]
    c["WIN"] = 128
    return c


def build(cfg):
    D, SEQ, DEPTH, NSMP, LS, NT = cfg["D"], cfg["SEQ"], cfg["DEPTH"], cfg["NSMP"], cfg["DEC_SEQ"], cfg["NT"]
    KC, H, CT, CONV, DNW, AW, QH, KVH, KVW = (cfg[k] for k in ("KC", "H", "CT", "CONV", "DNW", "AW", "QH", "KVH", "KVW"))
    IN_DIM = cfg["IN_DIM"]
    NR = 1 + NSMP
    NBT = NT // 128
    NTILES = SEQ // NT
    NBS = SEQ // 128
    PW = max(NT, NSMP * LS)
    nc = bass.Bass("TRN2", target_bir_lowering=False)
    st = ExitStack()
    P = Prog(nc, st)

    EI, EO = "ExternalInput", "ExternalOutput"
    xp = P.dram("xp", [SEQ, D], F32, EI)
    xs = P.dram("xs", [NSMP * LS, D], F32, EI)
    sconv = P.dram("sconv", [DEPTH, NSMP, 3, CONV], F32, EI)
    sdn = P.dram("sdn", [DEPTH, NSMP, H, 128, 128], F32, EI)
    ck = P.dram("ck", [DEPTH, NSMP, 128, KVW], F32, EI)
    cv = P.dram("cv", [DEPTH, NSMP, 128, KVW], F32, EI)
    cvec = P.dram("cvec", [NR, D], F32, EI)
    w_ada = P.dram("w_ada", [DEPTH, D, 3 * D], F32, EI)
    b_ada = P.dram("b_ada", [DEPTH, 3 * D], F32, EI)
    g_pre = P.dram("g_pre", [DEPTH, D], F32, EI)
    g_post = P.dram("g_post", [DEPTH, D], F32, EI)
    w_in = P.dram("w_in", [DEPTH, D, IN_DIM], F32, EI)
    conv_w = P.dram("conv_w", [DEPTH, 4, CONV], F32, EI)
    a_log = P.dram("a_log", [DEPTH, H], F32, EI)
    dt_bias = P.dram("dt_bias", [DEPTH, H], F32, EI)
    dn_norm = P.dram("dn_norm", [DEPTH, 128], F32, EI)
    sinks = P.dram("sinks", [DEPTH, QH], F32, EI)
    w_out = P.dram("w_out", [DEPTH, D, D], F32, EI)

    yp = P.dram("yp", [SEQ, D], F32, EO)
    ys = P.dram("ys", [NSMP * LS, D], F32, EO)
    convp = P.dram("convp", [DEPTH, 3, CONV], F32, EO)
    dnp = P.dram("dnp", [DEPTH, H, 128, 128], F32, EO)
    kpo = P.dram("kpo", [DEPTH, 128, KVW], F32, EO)
    vpo = P.dram("vpo", [DEPTH, 128, KVW], F32, EO)
    convs = P.dram("convs", [DEPTH, NSMP, 3, CONV], F32, EO)
    dns = P.dram("dns", [DEPTH, NSMP, H, 128, 128], F32, EO)
    kso = P.dram("kso", [DEPTH, NSMP, 128, KVW], F32, EO)
    vso = P.dram("vso", [DEPTH, NSMP, 128, KVW], F32, EO)
    outs = [yp, ys, convp, dnp, kpo, vpo, convs, dns, kso, vso]
    ggd = P.dram("ggd", [NR, D], F32)
    xmid_p = [P.dram(f"xmid_p{l}", [SEQ, D], F32) for l in range(DEPTH - 1)]
    xmid_s = [P.dram(f"xmid_s{l}", [NSMP * LS, D], F32) for l in range(DEPTH - 1)]

    bankA = [P.ps([128, 512], F32, "bA") for _ in range(2)]
    bankT = [P.ps([128, 1024], BF16, "bT") for _ in range(2)]
    bankD = [P.ps([128, 512], F32, "bD") for _ in range(3)]
    bankS = P.ps([128, 512], F32, "bS")
    dslots = []
    for q in range(4):
        for b in bankD:
            dslots.append(V(b.ap[:, q * 128:(q + 1) * 128], b.bufs))
    tslots = []
    for q in range(8):
        for b in bankT:
            tslots.append(V(b.ap[:, q * 128:(q + 1) * 128], b.bufs))
    sslots = [V(bankS.ap[:, q * 256:(q + 1) * 256], bankS.bufs) for q in range(2)]
    ctr = {"A": 0, "D": 0, "T": 0, "S": 0, "W": 0}

    def nxt(kind, lst):
        v = lst[ctr[kind] % len(lst)]
        ctr[kind] += 1
        return v

    psA = lambda: nxt("A", bankA)
    psD = lambda: nxt("D", dslots)
    psT = lambda: nxt("T", tslots)
    psS = lambda: nxt("S", sslots)

    ident = P.sb([128, 128], F32, "ident")
    identb = P.sb([128, 128], BF16, "identb")
    ones = P.sb([128, 128], F32, "ones")
    P.memset("pool", ident, 0.0)
    P.asel(ident, ident, [[-1, 128]], ALU.not_equal, 1.0, 0, 1)
    P.cp("dve", identb, ident)
    P.memset("pool", ones, 1.0)

    def make_masks(nt, CL):
        nch = nt // CL
        tmp = P.sb([nt, nt], F32, "mtmp")
        mS = P.sb([nt, nt], BF16, "mS")
        mU = P.sb([nt, nt], BF16, "mU")
        tri = P.sb([nt, nt], F32, "tri")
        P.memset("pool", tmp, 0.0)
        P.asel(tmp, tmp, [[-1, nt]], ALU.is_gt, NEG, 0, 1)
        for c in range(1, nch):
            P.memset("pool", tmp[c * CL:(c + 1) * CL, 0:c * CL], NEG)
        P.cp("dve", mS, tmp)
        tmp2 = P.sb([nt, nt], F32, "mtmp2")
        P.memset("pool", tmp2, 0.0)
        P.asel(tmp2, tmp2, [[1, nt]], ALU.is_ge, NEG, 0, -1)
        for c in range(0, nch - 1):
            P.memset("pool", tmp2[c * CL:(c + 1) * CL, (c + 1) * CL:nt], NEG)
        P.cp("dve", mU, tmp2)
        P.memset("pool", tri, 1.0)
        P.asel(tri, tri, [[1, nt]], ALU.is_ge, 0.0, 0, -1)
        for c in range(0, nch - 1):
            P.memset("pool", tri[c * CL:(c + 1) * CL, (c + 1) * CL:nt], 0.0)
        elast = []
        for c in range(nch):
            e = P.sb([nt, 128], F32, "elast")
            P.memset("pool", e, 0.0)
            P.asel(e, e, [[0, 128]], ALU.not_equal, 1.0, -(c * CL + CL - 1), 1)
            elast.append(e)
        return dict(nt=nt, CL=CL, nch=nch, mS=mS, mU=mU, tri=tri, elast=elast)

    MK_P = make_masks(128, 64)
    MK_S = make_masks(LS, min(64, LS))
    amf = P.sb([128, 256], F32, "amf")
    amask = P.sb([128, 256], BF16, "amask")
    P.memset("pool", amf, 0.0)
    P.memset("pool", amf[0:64, 192:256], NEG)
    P.memset("pool", amf[64:128, 0:64], NEG)
    P.cp("dve", amask, amf)

    def rope_tables(npart, nblk, pos0):
        cosT = P.sb([npart, nblk, 32], F32, "cosT")
        sinT = P.sb([npart, nblk, 32], F32, "sinT")
        fi = P.tmp([npart, 32], I32, "fi")
        P.iota(fi, [[1, 32]], 0, 0)
        ff = P.tmp([npart, 32], F32, "ff")
        P.cp("dve", ff, fi)
        inv = P.tmp([npart, 32], F32, "inv")
        P.act(inv, ff, AF.Exp, scale=-math.log(cfg["THETA"]) / 32.0)
        CHB = 2
        for b0 in range(0, nblk, CHB):
            n = min(CHB, nblk - b0)
            posi = P.tmp([npart, n], I32, "posi")
            P.iota(posi, [[128, n]], pos0 + 128 * b0, 1)
            posf = P.tmp([npart, n], F32, "posf")
            P.cp("dve", posf, posi)
            ang = P.tmp([npart, n, 32], F32, "ang")
            P.tt("dve", ang, posf.re("p (b o) -> p b o", o=1).bc([npart, n, 32]),
                 inv.re("p (o f) -> p o f", o=1).bc([npart, n, 32]), ALU.mult)
            for shift, tab in ((0.0, sinT), (0.25, cosT)):
                r = P.tmp([npart, n, 32], F32, "rr")
                P.ts("dve", r, ang, 1.0 / (2.0 * math.pi), ALU.mult, shift, ALU.add)
                ri = P.tmp([npart, n, 32], I32, "ri")
                P.cp("dve", ri, r)
                rf = P.tmp([npart, n, 32], F32, "rf")
                P.cp("dve", rf, ri)
                P.tt("dve", r, r, rf, ALU.subtract)
                P.ts("dve", rf, r, 0.5, ALU.is_gt)
                P.tt("dve", r, r, rf, ALU.subtract)
                P.ts("dve", rf, r, -0.5, ALU.is_lt)
                P.tt("dve", r, r, rf, ALU.add)
                P.act(tab[:, b0:b0 + n, :], r, AF.Sin, scale=2.0 * math.pi)
        return cosT, sinT

    cos_p, sin_p = rope_tables(128, NBS, 0)
    cos_s, sin_s = rope_tables(LS, 1, cfg["PAST"])

    hT = P.sb([128, KC, PW], BF16, "hT")
    hT_bufs = [Buf("hTy") for _ in range(max(NBT, NSMP))]
    hT = hT.wb(*hT_bufs)
    ybuf_flat = V(hT.ap.rearrange("p k t -> p (k t)"), hT_bufs)
    oT = P.sb([128, KC, PW], BF16, "oT")
    oT_flat = oT.re("p k t -> p (k t)")
    xnb = oT_flat[:, D:2 * D]
    NWB = 3
    NBUFG = max(NBT, NSMP)
    wbufs = [P.sb([128, KC, 128], BF16, "wb") for _ in range(NWB)]
    xbuf = P.sb([128, D], F32, "xbuf")
    S_all = P.sb([128, H, 128], F32, "S_all")
    S_bf = P.sb([128, H, 128], BF16, "S_bf")
    S_bufs = [Buf("S") for _ in range(H)]
    Sb_bufs = [Buf("Sb") for _ in range(H)]
    hist = P.sb([128, CT, NSMP, 3], F32, "hist")
    hist_bufs = [Buf("hist") for _ in range(CT)]
    kring = P.sb([128, KVH, 2, 128], BF16, "kring")
    vring = P.sb([128, KVH, 2, 64], BF16, "vring")
    kr_bufs = [[Buf("kr") for _ in range(2)] for _ in range(KVH)]
    vr_bufs = [[Buf("vr") for _ in range(2)] for _ in range(KVH)]
    gsT = P.sb([128, KC, NR], F32, "gsT")
    shT = P.sb([128, KC, NR], F32, "shT")
    cwT = P.sb([128, CT, 4], F32, "cwT")
    nA_b = P.sb([128, H], F32, "nA")
    dtb_b = P.sb([128, H], F32, "dtb")
    dnn_b = P.sb([128, 128], F32, "dnn")
    sink_b = P.sb([128, QH], F32, "sink")
    nsink_b = P.sb([128, QH], F32, "nsink")

    wctr = [0]

    NSLOT = IN_DIM // 128 + 2 * KVH + 2 + KC
    wsc = [P.dram(f"wsc{l}", [NSLOT, 128, KC * 128], BF16) for l in range(DEPTH)]
    wcache = {}

    def load_w(src_cols_view, key=None):
        wb = wbufs[wctr[0] % NWB]
        wctr[0] += 1
        ncols = src_cols_view.ap.shape[1]
        dst = wb[:, :, 0:ncols]
        if key is not None and key in wcache:
            P.dma(dst, wcache[key], eng="sp")
            return dst
        P.dma(dst, src_cols_view.re("(kc p) n -> p kc n", p=128), eng="pool")
        if key is not None:
            l_ = key[0]
            slot = sum(1 for k_ in wcache if k_[0] == l_)
            assert slot < NSLOT
            sv = V(wsc[l_].ap[slot].rearrange("p (kc n) -> p kc n", n=128)[:, :, 0:ncols], [Buf("wsc")])
            P.dma(sv, dst, eng="sp")
            wcache[key] = sv
        return dst

    def layer_params(l):
        cs = xbuf[0:NR, :]
        P.dma(cs, cvec)
        P.act(cs, cs, AF.Silu)
        scT = P.tmp([128, KC, NR], BF16, "scT")
        for kc in range(KC):
            pt = psD()
            P.tr(pt[:, 0:NR], cs[:, kc * 128:(kc + 1) * 128], ident[0:NR, 0:NR])
            P.cp("dve", scT[:, kc, :], pt[:, 0:NR])
        ncol = 3 * KC
        baT = P.tmp([128, ncol], F32, "baT")
        for c0 in range(0, ncol, 128):
            n = min(128, ncol - c0)
            rows = P.tmp([128, 128], F32, "rows", 2)
            P.dma(rows[0:n, :], b_ada[l, c0 * 128:(c0 + n) * 128].re("(r p) -> r p", p=128))
            pt = psD()
            P.tr(pt[:, 0:n], rows[0:n, :], ident[0:n, 0:n])
            P.cp("dve", baT[:, c0:c0 + n], pt[:, 0:n])
        gpT = P.tmp([128, KC], F32, "gpT")
        rows = P.tmp([128, 128], F32, "rows", 2)
        P.dma(rows[0:KC, :], g_pre[l].re("(r p) -> r p", p=128))
        pt = psD()
        P.tr(pt[:, 0:KC], rows[0:KC, :], ident[0:KC, 0:KC])
        P.cp("dve", gpT, pt[:, 0:KC])
        adaT = P.tmp([128, 2 * KC, NR], F32, "adaT")
        for j in range(3 * KC):
            wb = load_w(w_ada[l, :, j * 128:(j + 1) * 128])
            if j < 2 * KC:
                pt = psD()
                for kc in range(KC):
                    P.mm(pt[:, 0:NR], wb[:, kc, :], scT[:, kc, :], start=(kc == 0), stop=(kc == KC - 1))
                P.ts("dve", adaT[:, j, :], pt[:, 0:NR], baT[:, j:j + 1], ALU.add)
            else:
                pt = psD()
                for kc in range(KC):
                    P.mm(pt[0:NR, :], scT[:, kc, :], wb[:, kc, :], start=(kc == 0), stop=(kc == KC - 1))
                c0 = (j - 2 * KC) * 128
                bg = P.tmp([NR, 128], F32, "bg", 2)
                gp = P.tmp([NR, 128], F32, "gp", 2)
                P.dma(bg, b_ada[l:l + 1, 2 * D + c0:2 * D + c0 + 128].bc([NR, 128]))
                P.dma(gp, g_post[l:l + 1, c0:c0 + 128].bc([NR, 128]))
                gt = P.tmp([NR, 128], F32, "gt", 2)
                P.tt("dve", gt, pt[0:NR, :], bg, ALU.add)
                P.tt("pool", gt, gt, gp, ALU.mult)
                P.dma(ggd[:, c0:c0 + 128], gt)
        for r in range(NR):
            P.stt("dve", gsT[:, :, r], adaT[:, KC:2 * KC, r], 1.0, gpT, ALU.add, ALU.mult)
        P.cp("dve", shT, adaT[:, 0:KC, :])
        CH = min(32, KC)
        for c0 in range(0, CT, CH):
            n = min(CH, CT - c0)
            cwr = xbuf[0:4, 0:n * 128]
            P.dma(cwr, conv_w[l, :, c0 * 128:(c0 + n) * 128])
            pt = psD()
            for i in range(n):
                P.tr(pt[:, i * 4:(i + 1) * 4], cwr[:, i * 128:(i + 1) * 128], ident[0:4, 0:4])
            P.cp("dve", cwT[:, c0:c0 + n, :], pt[:, 0:4 * n].re("p (c j) -> p c j", j=4))
        al = P.tmp([128, H], F32, "al")
        P.dma(al, a_log[l:l + 1, :].bc([128, H]))
        P.act(al, al, AF.Exp)
        P.ts("dve", nA_b, al, -1.0, ALU.mult)
        P.dma(dtb_b, dt_bias[l:l + 1, :].bc([128, H]))
        P.dma(dnn_b, dn_norm[l:l + 1, :].bc([128, 128]))
        P.dma(sink_b, sinks[l:l + 1, :].bc([128, QH]))
        P.ts("dve", nsink_b, sink_b, -1.0, ALU.mult)

    def dn_gates(blk, ba_ps, MK):
        nt, CL, nch = MK["nt"], MK["CL"], MK["nch"]
        g = {}
        beta = P.tmp([nt, H], F32, "beta", NBUFG)
        P.act(beta, ba_ps[:, 0:H], AF.Sigmoid)
        xa = P.tmp([nt, H], F32, "xa", NBUFG)
        P.tt("dve", xa, ba_ps[:, H:2 * H], dtb_b[0:nt, :], ALU.add)
        P.act(xa, xa, AF.Exp)
        P.act(xa, xa, AF.Ln, bias=1.0)
        gg = P.tmp([nt, H], F32, "gg", NBUFG)
        P.tt("dve", gg, xa, nA_b[0:nt, :], ALU.mult)
        pt = psD()
        P.mm(pt[0:nt, 0:H], MK["tri"], gg)
        gc = P.tmp([nt, H], F32, "gc", NBUFG)
        P.cp("dve", gc, pt[0:nt, 0:H])
        egc = P.tmp([nt, H], F32, "egc", NBUFG)
        P.act(egc, gc, AF.Exp)
        bge = P.tmp([nt, H], F32, "bge", NBUFG)
        P.tt("dve", bge, beta, egc, ALU.mult)
        dec = P.tmp([128, nch, H], F32, "dec", NBUFG)
        kds = P.tmp([nt, H], F32, "kds", NBUFG)
        for c in range(nch):
            pg = psD()
            P.mm(pg[:, 0:H], MK["elast"][c], gc)
            P.act(dec[:, c, :], pg[:, 0:H], AF.Exp)
            rs = slice(c * CL, (c + 1) * CL)
            P.tt("dve", kds[rs, :], pg[rs, 0:H], gc[rs, :], ALU.subtract)
        P.act(kds, kds, AF.Exp)
        G1 = P.tmp([nt, H, 2], F32, "G1", NBUFG)
        G2 = P.tmp([nt, H, 2], F32, "G2", NBUFG)
        P.memset("pool", G1, 1.0)
        P.memset("pool", G2, 1.0)
        P.cp("dve", G1[:, :, 0], gc)
        P.ts("dve", G2[:, :, 1], gc, -1.0, ALU.mult)
        g.update(beta=beta, egc=egc, bge=bge, dec=dec, kds=kds, G1=G1, G2=G2)
        return g

    def dn_block(h, g, MK, qT, kT, vT, zs, Sv, Sbv, oT_dst):
        nt, CL, nch = MK["nt"], MK["CL"], MK["nch"]
        p1 = psD()
        P.mm(p1[0:2, 0:nt], g["G1"][:, h, :], ident[0:nt, 0:nt])
        R1 = P.tmp([2, nt], F32, "R1")
        P.cp("act", R1, p1[0:2, 0:nt])
        p2 = psD()
        P.mm(p2[0:2, 0:nt], g["G2"][:, h, :], ident[0:nt, 0:nt])
        R2 = P.tmp([2, nt], F32, "R2")
        P.cp("dve", R2, p2[0:2, 0:nt])
        stop_at(11)
        pd = psD()
        P.mm(pd[0:nt, 0:nt], identb[0:nt, 0:nt], MK["mS"], start=True, stop=False)
        P.mm(pd[0:nt, 0:nt], R1, R2, start=False, stop=True)
        gamS = P.tmp([nt, nt], F32, "gamS")
        P.act(gamS, pd[0:nt, 0:nt], AF.Exp)
        pdt = psD()
        P.mm(pdt[0:nt, 0:nt], identb[0:nt, 0:nt], MK["mU"], start=True, stop=False)
        P.mm(pdt[0:nt, 0:nt], R2, R1, start=False, stop=True)
        gamT = P.tmp([nt, nt], F32, "gamT")
        P.act(gamT, pdt[0:nt, 0:nt], AF.Exp)
        stop_at(12)
        pkk = psD()
        P.mm(pkk[0:nt, 0:nt], kT, kT)
        A = P.tmp([nt, nt], F32, "A")
        P.stt("dve", A, pkk[0:nt, 0:nt], g["beta"][:, h:h + 1], gamS, ALU.mult, ALU.mult)
        pkq = psD()
        P.mm(pkq[0:nt, 0:nt], kT, qT)
        qkmT = P.tmp([nt, nt], BF16, "qkmT")
        P.tt("dve", qkmT, pkq[0:nt, 0:nt], gamT, ALU.mult)
        stop_at(13)
        pat = psD()
        P.tr(pat[0:nt, 0:nt], A, ident[0:nt, 0:nt])
        AT = P.tmp([nt, nt], F32, "AT")
        P.cp("act", AT, pat[0:nt, 0:nt])
        RT = P.tmp([nt, nt], F32, "RT")
        P.tt("pool", RT, ident[0:nt, 0:nt], AT, ALU.subtract)
        stop_at(14)
        X, XT = A, AT
        nlev = int(math.log2(CL)) - 1
        for lev in range(nlev):
            px = psD()
            P.mm(px[0:nt, 0:nt], XT, X)
            X2 = P.tmp([nt, nt], F32, "X2", 2)
            P.cp("act", X2, px[0:nt, 0:nt])
            if lev < nlev - 1:
                pxt = psD()
                P.mm(pxt[0:nt, 0:nt], X, XT)
                X2T = P.tmp([nt, nt], F32, "X2T", 2)
                P.cp("dve", X2T, pxt[0:nt, 0:nt])
            pr = psD()
            P.mm(pr[0:nt, 0:nt], X2, RT)
            RT2 = P.tmp([nt, nt], F32, "RT2", 2)
            P.tt("dve", RT2, pr[0:nt, 0:nt], RT, ALU.add)
            RT = RT2
            if lev < nlev - 1:
                X, XT = X2, X2T
        stop_at(15)
        ptk = psT()
        P.tr(ptk[0:nt, :], kT, identb)
        ptv = psT()
        P.tr(ptv[0:nt, :], vT, identb)
        stop_at(151)
        RHSw = P.tmp([nt, 128], F32, "RHSw")
        P.act(RHSw, ptk[0:nt, :], AF.Identity, scale=g["bge"][:, h:h + 1])
        stop_at(152)
        kdec = P.tmp([nt, 128], BF16, "kdec")
        P.act(kdec, ptk[0:nt, :], AF.Identity, scale=g["kds"][:, h:h + 1])
        stop_at(153)
        RHSu = P.tmp([nt, 128], F32, "RHSu")
        P.act(RHSu, ptv[0:nt, :], AF.Identity, scale=g["beta"][:, h:h + 1])
        stop_at(154)
        pu = psD()
        P.mm(pu[0:nt, :], RT, RHSu)
        u = P.tmp([nt, 128], F32, "u")
        P.cp("act", u, pu[0:nt, :])
        stop_at(155)
        pw = psD()
        P.mm(pw[:, 0:nt], RHSw, RT)
        wT = P.tmp([128, nt], BF16, "wT")
        P.cp("dve", wT, pw[:, 0:nt])
        stop_at(16)
        vnew = P.tmp([nt, 128], BF16, "vnew")
        o = P.tmp([nt, 128], F32, "o")
        for c in range(nch):
            rs = slice(c * CL, (c + 1) * CL)
            pp1 = psD()
            P.mm(pp1[rs, :], wT[:, rs], Sbv)
            P.tt("dve", vnew[rs, :], u[rs, :], pp1[rs, :], ALU.subtract)
            pp2 = psD()
            P.mm(pp2[rs, :], qT[:, rs], Sbv)
            t2 = P.tmp([nt, 128], F32, "t2", 2)
            P.act(t2[rs, :], pp2[rs, :], AF.Identity, scale=g["egc"][rs, h:h + 1])
            pp3 = psD()
            P.mm(pp3[rs, :], qkmT[rs, rs], vnew[rs, :])
            P.tt("dve", o[rs, :], pp3[rs, :], t2[rs, :], ALU.add)
            pp4 = psD()
            P.mm(pp4, kdec[rs, :], vnew[rs, :])
            P.stt("dve", Sv, Sv, g["dec"][:, c, h:h + 1], pp4, ALU.mult, ALU.add)
            P.cp("act", Sbv, Sv)
        stop_at(17)
        junk = P.tmp([nt, 128], F32, "junk")
        ss = P.tmp([nt, 1], F32, "ss")
        P.act(junk, o, AF.Square, accum=ss)
        P.act(ss, ss, AF.Sqrt, scale=1.0 / 128.0, bias=EPS)
        P.recip(ss, ss)
        og = P.tmp([nt, 128], F32, "og")
        P.stt("dve", og, o, ss, dnn_b[0:nt, :], ALU.mult, ALU.mult)
        ogb = P.tmp([nt, 128], BF16, "ogb")
        P.tt("pool", ogb, og, zs, ALU.mult)
        pt = psT()
        P.tr(pt[:, 0:nt], ogb, identb[0:nt, 0:nt])
        P.cp("act", oT_dst, pt[:, 0:nt])

    def rope(dst, src, cosv, sinv, nt, nh):
        cb = cosv.re("p (o f) -> p o f", o=1).bc([nt, nh, 32])
        sb_ = sinv.re("p (o f) -> p o f", o=1).bc([nt, nh, 32])
        x1, x2 = src[:, :, 0:32], src[:, :, 32:64]
        t1 = P.tmp([nt, nh, 32], F32, "rt1")
        t2 = P.tmp([nt, nh, 32], F32, "rt2")
        P.tt("dve", t1, x2, sb_, ALU.mult)
        P.tt("dve", t2, x1, cb, ALU.mult)
        P.tt("pool", dst[:, :, 0:32], t2, t1, ALU.subtract)
        t3 = P.tmp([nt, nh, 32], F32, "rt3")
        t4 = P.tmp([nt, nh, 32], F32, "rt4")
        P.tt("dve", t3, x1, sb_, ALU.mult)
        P.tt("dve", t4, x2, cb, ALU.mult)
        P.tt("pool", dst[:, :, 32:64], t4, t3, ALU.add)

    def attn_head(nt, qTv, ksrcs, vsrcs, masked, hq, zsv, og_dst):
        sc = psS()
        nk_tot = sum(nk for _, nk in ksrcs)
        if masked:
            mcols = amask[0:nt, 256 - nk_tot:256]
            P.mm(sc[0:nt, 0:nk_tot], identb[0:nt, 0:nt], mcols, start=True, stop=False)
        off = 0
        for i, (kv_, nk) in enumerate(ksrcs):
            P.mm(sc[0:nt, off:off + nk], qTv, kv_, start=not masked, stop=(not masked) or (i == len(ksrcs) - 1))
            off += nk
        mraw = P.tmp([nt, 1], F32, "mraw", 2)
        P.red(mraw, sc[0:nt, 0:nk_tot], ALU.max)
        negm = P.tmp([nt, 1], F32, "negm", 2)
        P.ts("dve", negm, mraw, -0.125, ALU.mult, nsink_b[0:nt, hq:hq + 1], ALU.min)
        p = P.tmp([nt, 256], BF16, "p", 2)
        ssum = P.tmp([nt, 1], F32, "ssum", 2)
        P.act(p[:, 0:nk_tot], sc[0:nt, 0:nk_tot], AF.Exp, bias=negm, scale=0.125, accum=ssum)
        sk = P.tmp([nt, 1], F32, "sk", 2)
        P.act(sk, sink_b[0:nt, hq:hq + 1], AF.Exp, bias=negm)
        P.tt("dve", sk, sk, ssum, ALU.add)
        P.recip(sk, sk)
        pTs = P.tmp([128, 2, nt], BF16, "pTs", 2)
        off = 0
        for i, (_, nk) in enumerate(ksrcs):
            pt = psT()
            P.tr(pt[0:nk, 0:nt], p[:, off:off + nk], identb[0:nt, 0:nt])
            P.cp("act", pTs[0:nk, i, :], pt[0:nk, 0:nt])
            off += nk
        po = psD()
        for i, (_, nk) in enumerate(ksrcs):
            P.mm(po[0:nt, 0:64], pTs[0:nk, i, :], vsrcs[i], start=(i == 0), stop=(i == len(ksrcs) - 1))
        P.stt("dve", og_dst, po[0:nt, 0:64], sk, zsv, ALU.mult, ALU.mult)

    def do_tile(l, kind, t):
        last_layer = (l == DEPTH - 1)
        if kind == "p":
            x_in = xp if l == 0 else xmid_p[l - 1]
            x_out = yp if last_layer else xmid_p[l]
            blocks = [dict(r0=t * NT + b * 128, nt=128, row=0, c0=b * 128, seq=0, gb=t * NBT + b) for b in range(NBT)]
            MK = MK_P
            nseq, L = 1, NT
        else:
            x_in = xs if l == 0 else xmid_s[l - 1]
            x_out = ys if last_layer else xmid_s[l]
            blocks = [dict(r0=s * LS, nt=LS, row=1 + s, c0=s * LS, seq=s, gb=0) for s in range(NSMP)]
            MK = MK_S
            nseq, L = NSMP, LS
        NTt = sum(b["nt"] for b in blocks)
        nb = len(blocks)

        for bi, b in enumerate(blocks):
            nt = b["nt"]
            P.dma(xbuf[0:nt, :], x_in[b["r0"]:b["r0"] + nt, :])
            ss = P.tmp([128, 1], F32, "ss0", 2)
            P.act(oT_flat[0:nt, 0:D], xbuf[0:nt, :], AF.Square, accum=ss[0:nt, :])
            P.act(ss[0:nt, :], ss[0:nt, :], AF.Sqrt, scale=1.0 / D, bias=EPS)
            P.recip(ss[0:nt, :], ss[0:nt, :])
            P.ts("dve", xnb[0:nt, :], xbuf[0:nt, :], ss[0:nt, :], ALU.mult)
            for kc in range(KC):
                pt = psT()
                P.tr(pt[:, 0:nt], xnb[0:nt, kc * 128:(kc + 1) * 128], identb[0:nt, 0:nt])
                P.act(hT[:, kc, b["c0"]:b["c0"] + nt], pt[:, 0:nt], AF.Identity,
                      bias=shT[:, kc, b["row"]:b["row"] + 1], scale=gsT[:, kc, b["row"]:b["row"] + 1])

        stop_at(3)

        def proj_tm(wb, ncols):
            acc = psA()
            res = []
            for bi, b in enumerate(blocks):
                nt = b["nt"]
                dst = acc[0:nt, bi * ncols:(bi + 1) * ncols]
                for kc in range(KC):
                    P.mm(dst, hT[:, kc, b["c0"]:b["c0"] + nt], wb[:, kc, 0:ncols], start=(kc == 0), stop=(kc == KC - 1))
                res.append(dst)
            return res

        def proj_cm(wb):
            acc = psA()
            dst = acc[:, 0:NTt]
            for kc in range(KC):
                P.mm(dst, wb[:, kc, :], hT[:, kc, 0:NTt], start=(kc == 0), stop=(kc == KC - 1))
            return dst

        wb = load_w(w_in[l, :, cfg["c_b"]:cfg["c_b"] + 2 * H], (l, "b"))
        ba = proj_tm(wb, 2 * H)
        gates = [dn_gates(b, ba[bi], MK) for bi, b in enumerate(blocks)]

        stop_at(4)
        for h in range(H):
            acts = []
            for which in range(3):
                ct = which * H + h
                wb = load_w(w_in[l, :, ct * 128:(ct + 1) * 128], (l, "c", ct))
                acc = proj_cm(wb)
                pre = P.tmp([128, nseq, L + 3], F32, "pre", 2)
                hv = V(hist.ap[:, ct, 0:nseq, :], [hist_bufs[ct]])
                P.cp("pool", pre[:, :, 0:3], hv)
                P.cp("act", pre[:, :, 3:3 + L], acc.re("p (s t) -> p s t", s=nseq))
                P.cp("pool", hv, pre[:, :, L:L + 3])
                y = P.tmp([128, nseq, L], F32, "convy")
                P.act(y, pre[:, :, 0:L], AF.Identity, scale=cwT[:, ct, 0:1])
                for j in range(1, 4):
                    P.stt("dve", y, pre[:, :, j:j + L], cwT[:, ct, j:j + 1], y, ALU.mult, ALU.add)
                a_ = P.tmp([128, NTt], F32, "cact", 3)
                P.act(a_, y.re("p s t -> p (s t)"), AF.Silu)
                acts.append(a_)
            stop_at(8)
            qa, ka, va = acts
            normed = []
            for a_, scl in ((qa, 128.0 ** -0.5), (ka, 1.0)):
                sq = P.tmp([128, NTt], F32, "sq")
                P.tt("pool", sq, a_, a_, ALU.mult)
                acc = psA()
                P.mm(acc[:, 0:NTt], ones, sq)
                rs_ = P.tmp([128, NTt], F32, "rs")
                P.act(rs_, acc[:, 0:NTt], AF.Sqrt, bias=EPS)
                P.recip(rs_, rs_)
                nb_ = P.tmp([128, NTt], BF16, "nrm", 4)
                P.stt("dve", nb_, a_, scl, rs_, ALU.mult, ALU.mult)
                normed.append(nb_)
            stop_at(9)
            qTb, kTb = normed
            vTb = P.tmp([128, NTt], BF16, "vTb", 2)
            P.cp("pool", vTb, va)
            wb = load_w(w_in[l, :, cfg["c_z"] + h * 128:cfg["c_z"] + (h + 1) * 128], (l, "z", h))
            zps = proj_tm(wb, 128)
            zs_all = P.tmp([128, nb, 128], BF16, "zs", 2)
            for bi, b in enumerate(blocks):
                P.act(zs_all[0:b["nt"], bi, :], zps[bi], AF.Silu)
            stop_at(10)
            for bi, b in enumerate(blocks):
                nt, c0 = b["nt"], b["c0"]
                if kind == "p":
                    Sv = V(S_all.ap[:, h, :], [S_bufs[h]])
                    Sbv = V(S_bf.ap[:, h, :], [Sb_bufs[h]])
                    if b["gb"] == 0:
                        P.memset("pool", Sv, 0.0)
                        P.memset("pool", Sbv, 0.0)
                else:
                    hs = (h + b["seq"]) % H
                    Sv = V(S_all.ap[:, hs, :], [S_bufs[hs]])
                    Sbv = V(S_bf.ap[:, hs, :], [Sb_bufs[hs]])
                    P.dma(Sv, sdn[l, b["seq"], h])
                    P.cp("act", Sbv, Sv)
                dn_block(h, gates[bi], MK, qTb[:, c0:c0 + nt], kTb[:, c0:c0 + nt], vTb[:, c0:c0 + nt],
                         zs_all[0:nt, bi, :], Sv, Sbv, oT[:, h, c0:c0 + nt])
                if kind == "s":
                    P.dma(dns[l, b["seq"], h], Sv)
                elif b["gb"] == NBS - 1:
                    P.dma(dnp[l, h], Sv)
        stop_at(5)
        if kind == "s" or t == NTILES - 1:
            CH = min(32, KC)
            for s in range(nseq):
                for c0 in range(0, CT, CH):
                    n = min(CH, CT - c0)
                    crow = xbuf[0:3, 0:n * 128]
                    for c1 in range(0, n, 4):
                        pt = psA()
                        for i in range(4):
                            hv = V(hist.ap[:, c0 + c1 + i, s, :], [hist_bufs[c0 + c1 + i]])
                            P.tr(pt[0:3, i * 128:(i + 1) * 128], hv, ident)
                        P.cp("dve", crow[:, c1 * 128:(c1 + 4) * 128], pt[0:3, :])
                    dst = convs[l, s] if kind == "s" else convp[l]
                    P.dma(dst[:, c0 * 128:(c0 + n) * 128], crow)

        stop_at(6)
        for gk in range(KVH):
            wbk = load_w(w_in[l, :, cfg["c_k"] + gk * 64:cfg["c_k"] + (gk + 1) * 64], (l, "k", gk))
            kps = proj_tm(wbk, 64)
            krot = []
            for bi, b in enumerate(blocks):
                nt = b["nt"]
                ksb = P.tmp([128, 1, 64], F32, "ksb", 2)
                P.cp("act", ksb[0:nt, 0, :], kps[bi])
                kr = P.tmp([128, 1, 64], F32, "kr", 4)
                if kind == "p":
                    cosv, sinv = cos_p[:, b["gb"], :], sin_p[:, b["gb"], :]
                else:
                    cosv, sinv = cos_s[:, 0, :], sin_s[:, 0, :]
                rope(kr[0:nt], ksb[0:nt], cosv[0:nt], sinv[0:nt], nt, 1)
                krot.append(kr)
                if kind == "s":
                    P.dma(kso[l, b["seq"], 128 - LS:128, gk * 64:(gk + 1) * 64], kr[0:nt, 0, :])
                elif b["gb"] == NBS - 1:
                    P.dma(kpo[l, :, gk * 64:(gk + 1) * 64], kr[0:nt, 0, :])
            wbv = load_w(w_in[l, :, cfg["c_v"] + gk * 64:cfg["c_v"] + (gk + 1) * 64], (l, "v", gk))
            vps = proj_tm(wbv, 64)
            vsb = []
            for bi, b in enumerate(blocks):
                nt = b["nt"]
                vf = P.tmp([128, 64], F32, "vf", 4)
                P.cp("act", vf[0:nt, :], vps[bi])
                vsb.append(vf)
                if kind == "s":
                    P.dma(vso[l, b["seq"], 128 - LS:128, gk * 64:(gk + 1) * 64], vf[0:nt, :])
                elif b["gb"] == NBS - 1:
                    P.dma(vpo[l, :, gk * 64:(gk + 1) * 64], vf[0:nt, :])
            qrot = [P.tmp([128, 8, 64], BF16, "qrotb", 4) for _ in blocks]
            zat = [P.tmp([128, 512], BF16, "zat", 4) for _ in blocks]
            for part in range(4):
                cq = cfg["c_q"] + gk * 512 + part * 128
                wbq = load_w(w_in[l, :, cq:cq + 128], (l, "q", cq))
                qps = proj_tm(wbq, 128)
                for bi, b in enumerate(blocks):
                    nt = b["nt"]
                    qsb = P.tmp([128, 2, 64], F32, "qsb", 2)
                    P.cp("act", qsb[0:nt], qps[bi].re("p (h d) -> p h d", h=2))
                    if kind == "p":
                        cosv, sinv = cos_p[:, b["gb"], :], sin_p[:, b["gb"], :]
                    else:
                        cosv, sinv = cos_s[:, 0, :], sin_s[:, 0, :]
                    rope(qrot[bi][0:nt, 2 * part:2 * part + 2, :], qsb[0:nt], cosv[0:nt], sinv[0:nt], nt, 2)
            for part in range(4):
                cz = cfg["c_za"] + gk * 512 + part * 128
                wbz = load_w(w_in[l, :, cz:cz + 128], (l, "za", cz))
                zps = proj_tm(wbz, 128)
                for bi, b in enumerate(blocks):
                    P.act(zat[bi][0:b["nt"], part * 128:(part + 1) * 128], zps[bi], AF.Silu)
            for bi, b in enumerate(blocks):
                nt, c0 = b["nt"], b["c0"]
                if kind == "p":
                    slot = b["gb"] % 2
                    pslot = 1 - slot
                    has_prev = b["gb"] > 0
                else:
                    slot, pslot, has_prev = 1, 0, True
                    ckf = P.tmp([128, 64], F32, "ckf")
                    P.dma(ckf, ck[l, b["seq"], :, gk * 64:(gk + 1) * 64])
                    ckd = P.tmp([128, 2, 64], BF16, "ckd")
                    P.cp("dve", ckd[:, 0, :], ckf)
                    P.cp("pool", ckd[:, 1, :], ckf)
                    pt = psT()
                    P.tr(pt[:, 0:128], ckd.re("p a d -> p (a d)"), identb)
                    P.cp("act", V(kring.ap[:, gk, 0, :], [kr_bufs[gk][0]]), pt[:, 0:128])
                    cvf = P.tmp([128, 64], F32, "cvf")
                    P.dma(cvf, cv[l, b["seq"], :, gk * 64:(gk + 1) * 64])
                    P.cp("dve", V(vring.ap[:, gk, 0, :], [vr_bufs[gk][0]]), cvf)
                    if gk == 0 and LS < 128:
                        P.dma(kso[l, b["seq"], 0:128 - LS, :], ck[l, b["seq"], LS:128, :])
                        P.dma(vso[l, b["seq"], 0:128 - LS, :], cv[l, b["seq"], LS:128, :])
                kd = P.tmp([128, 2, 64], BF16, "kd", 2)
                P.cp("dve", kd[0:nt, 0, :], krot[bi][0:nt, 0, :])
                P.cp("pool", kd[0:nt, 1, :], krot[bi][0:nt, 0, :])
                pt = psT()
                P.tr(pt[:, 0:nt], kd[0:nt].re("p a d -> p (a d)"), identb[0:nt, 0:nt])
                kcur = V(kring.ap[:, gk, slot, 0:nt], [kr_bufs[gk][slot]])
                P.cp("act", kcur, pt[:, 0:nt])
                vcur = V(vring.ap[0:nt, gk, slot, :], [vr_bufs[gk][slot]])
                P.cp("dve", vcur, vsb[bi][0:nt, :])
                kprev = V(kring.ap[:, gk, pslot, :], [kr_bufs[gk][pslot]])
                vprev = V(vring.ap[:, gk, pslot, :], [vr_bufs[gk][pslot]])
                qb = qrot[bi]
                qT = P.tmp([128, 4, 128], BF16, "qT", 2)
                for pr_ in range(4):
                    pt = psT()
                    P.tr(pt[:, 0:nt], qb[0:nt, 2 * pr_:2 * pr_ + 2, :].re("p a d -> p (a d)"), identb[0:nt, 0:nt])
                    P.cp("act", qT[:, pr_, 0:nt], pt[:, 0:nt])
                ogat = P.tmp([128, 512], BF16, "ogat", 2)
                for hh in range(8):
                    base = 64 * (hh % 2)
                    qTv = qT[base:base + 64, hh // 2, 0:nt]
                    ksrcs, vsrcs = [], []
                    if has_prev:
                        ksrcs.append((kprev[base:base + 64, :], 128))
                        vsrcs.append(vprev)
                    ksrcs.append((kcur[base:base + 64, :], nt))
                    vsrcs.append(vcur)
                    attn_head(nt, qTv, ksrcs, vsrcs, kind == "p", gk * 8 + hh,
                              zat[bi][0:nt, hh * 64:(hh + 1) * 64], ogat[0:nt, hh * 64:(hh + 1) * 64])
                for q4 in range(4):
                    pt = psT()
                    P.tr(pt[:, 0:nt], ogat[0:nt, q4 * 128:(q4 + 1) * 128], identb[0:nt, 0:nt])
                    P.cp("act", oT[:, H + gk * 4 + q4, c0:c0 + nt], pt[:, 0:nt])

        stop_at(7)
        yv = [V(ybuf_flat.ap[:, bi * D:(bi + 1) * D], [hT_bufs[bi]]) for bi in range(nb)]
        for n in range(KC):
            wbo = load_w(w_out[l, :, n * 128:(n + 1) * 128], (l, "o", n))
            acc = psA()
            for bi, b in enumerate(blocks):
                nt, c0 = b["nt"], b["c0"]
                dst = acc[0:nt, bi * 128:(bi + 1) * 128]
                for c in range(KC):
                    P.mm(dst, oT[:, c, c0:c0 + nt], wbo[:, c, :], start=(c == 0), stop=(c == KC - 1))
                P.cp("act" if bi % 2 == 0 else "dve", yv[bi][0:nt, n * 128:(n + 1) * 128], dst)
        for bi, b in enumerate(blocks):
            nt = b["nt"]
            ss = P.tmp([128, 1], F32, "ss2", 2)
            P.act(oT_flat[0:nt, 0:D], yv[bi][0:nt, :], AF.Square, accum=ss[0:nt, :])
            P.act(ss[0:nt, :], ss[0:nt, :], AF.Sqrt, scale=1.0 / D, bias=EPS)
            P.recip(ss[0:nt, :], ss[0:nt, :])
            P.dma(xbuf[0:nt, :], x_in[b["r0"]:b["r0"] + nt, :])
            for j in range(0, D, 256):
                w_ = min(256, D - j)
                pg = P.tmp([128, 256], F32, "ggp", 1)
                P.dma(pg[0:nt, 0:w_], ggd[b["row"]:b["row"] + 1, j:j + w_].bc([nt, w_]))
                tmp = P.tmp([128, 256], F32, "tmpo", 1)
                P.stt("dve", tmp[0:nt, 0:w_], yv[bi][0:nt, j:j + w_], ss[0:nt, :], pg[0:nt, 0:w_], ALU.mult, ALU.mult)
                P.tt("pool", xbuf[0:nt, j:j + w_], xbuf[0:nt, j:j + w_], tmp[0:nt, 0:w_], ALU.add)
            P.dma(x_out[b["r0"]:b["r0"] + nt, :], xbuf[0:nt, :])

    def main_body():
        for l in range(DEPTH):
            stop_at(1)
            layer_params(l)
            stop_at(2)
            for ct in range(CT):
                P.memset("pool", V(hist.ap[:, ct, :, :], [hist_bufs[ct]]), 0.0)
            for t in range(NTILES):
                do_tile(l, "p", t)
            CH = min(32, KC)
            for s in range(NSMP):
                for c0 in range(0, CT, CH):
                    n = min(CH, CT - c0)
                    srow = xbuf[0:3, 0:n * 128]
                    P.dma(srow, sconv[l, s, :, c0 * 128:(c0 + n) * 128])
                    pt = psD()
                    for i in range(n):
                        P.tr(pt[:, i * 3:(i + 1) * 3], srow[:, i * 128:(i + 1) * 128], ident[0:3, 0:3])
                    for i in range(n):
                        P.cp("dve", V(hist.ap[:, c0 + i, s, :], [hist_bufs[c0 + i]]), pt[:, i * 3:(i + 1) * 3])
            do_tile(l, "s", 0)

    def stop_at(k):
        if cfg["STOP"] == k:
            raise StopBuild()

    try:
        main_body()
    except StopBuild:
        pass
    P.fence(outs)
    P.emit()
    st.close()
    return nc, P


_CACHE = {}


def _get_prog(cfg_key, cfg):
    if cfg_key not in _CACHE:
        _CACHE[cfg_key] = build(cfg)
    return _CACHE[cfg_key]


def make_in_maps(cfg, ncores, inputs):
    f = lambda a: np.ascontiguousarray(np.asarray(a, dtype=np.float32))
    B = inputs["x_prompt"].shape[0]
    NSMP, DEPTH = cfg["NSMP"], cfg["DEPTH"]
    KVW = cfg["KVW"]
    in_maps = []
    shared = {k: f(inputs[k]) for k in ("w_ada", "b_ada", "g_pre", "g_post", "w_in", "conv_w", "a_log", "dt_bias",
                                        "dn_norm", "sinks", "w_out")}
    for i in range(ncores):
        pb = i % B
        ss = slice(NSMP * i, NSMP * (i + 1))
        m = dict(shared)
        m["xp"] = f(inputs["x_prompt"][pb])
        m["xs"] = f(inputs["x_sample"][ss]).reshape(-1, cfg["D"])
        m["sconv"] = f(inputs["state_conv"][:, ss])
        m["sdn"] = f(inputs["state_dn"][:, ss])
        m["ck"] = f(inputs["cache_k"][:, ss]).reshape(DEPTH, NSMP, 128, KVW)
        m["cv"] = f(inputs["cache_v"][:, ss]).reshape(DEPTH, NSMP, 128, KVW)
        m["cvec"] = f(np.concatenate([inputs["c_prompt"][pb:pb + 1], inputs["c_sample"][ss]], axis=0))
        in_maps.append(m)
    return in_maps


def assemble(cfg, ncores, B, R):
    NSMP, DEPTH = cfg["NSMP"], cfg["DEPTH"]
    H, KVH = cfg["H"], cfg["KVH"]
    y_p = np.stack([R[i]["yp"] for i in range(B)], 0)
    y_s = np.concatenate([R[i]["ys"].reshape(NSMP, cfg["DEC_SEQ"], cfg["D"]) for i in range(ncores)], 0)
    conv_p = np.stack([R[i]["convp"] for i in range(B)], 1)
    dn_p = np.stack([R[i]["dnp"] for i in range(B)], 1)
    k_p = np.stack([R[i]["kpo"].reshape(DEPTH, 128, KVH, 64) for i in range(B)], 1)
    v_p = np.stack([R[i]["vpo"].reshape(DEPTH, 128, KVH, 64) for i in range(B)], 1)
    conv_s = np.concatenate([R[i]["convs"] for i in range(ncores)], 1)
    dn_s = np.concatenate([R[i]["dns"] for i in range(ncores)], 1)
    k_s = np.concatenate([R[i]["kso"].reshape(DEPTH, NSMP, 128, KVH, 64) for i in range(ncores)], 1)
    v_s = np.concatenate([R[i]["vso"].reshape(DEPTH, NSMP, 128, KVH, 64) for i in range(ncores)], 1)
    return tuple(np.ascontiguousarray(a, dtype=np.float32) for a in
                 (y_p, y_s, conv_p, dn_p, k_p, v_p, conv_s, dn_s, k_s, v_s))


def run_cores(cfg, ncores, inputs):
    nc, _ = _get_prog(tuple(sorted(cfg.items())), cfg)
    in_maps = make_in_maps(cfg, ncores, inputs)
    res = run_bass_kernel_spmd(nc, in_maps, core_ids=list(range(ncores)))
    return assemble(cfg, ncores, inputs["x_prompt"].shape[0], res.results)


def kernel(**inputs):
    cfg = make_cfg()
    return run_cores(cfg, 8, inputs)
```

```python
import math
import numpy as np
from contextlib import ExitStack
import concourse.bass as bass
import concourse.mybir as mybir
from concourse.bass_utils import run_bass_kernel_spmd

F32 = mybir.dt.float32
BF16 = mybir.dt.bfloat16
I32 = mybir.dt.int32
ALU = mybir.AluOpType
AF = mybir.ActivationFunctionType
AX = mybir.AxisListType

SEM_LIMIT = 30000
DMA_SLOTS = 12
NEG = -30000.0
EPS = 1e-6


class Buf:
    __slots__ = ("name", "w", "r")

    def __init__(self, name):
        self.name = name
        self.w = {}
        self.r = {}


class V:
    __slots__ = ("ap", "bufs")

    def __init__(self, ap, bufs):
        self.ap = ap
        self.bufs = bufs

    def __getitem__(self, idx):
        return V(self.ap[idx], self.bufs)

    def re(self, pat, **kw):
        return V(self.ap.rearrange(pat, **kw), self.bufs)

    def wb(self, *bufs):
        return V(self.ap, list(bufs))

    def bc(self, shape):
        return V(self.ap.broadcast_to(list(shape)), self.bufs)

    def sub(self, idx, name="s"):
        return V(self.ap[idx], [Buf(name)])


class Op:
    __slots__ = ("eng", "fn", "raw", "oth", "dma", "sig", "sem", "val", "id")


class Prog:
    def __init__(self, nc, stack):
        self.nc = nc
        self.stack = stack
        self.ops = []
        self.nname = 0

    def sb(self, shape, dt=F32, name=None):
        self.nname += 1
        name = f"{name or 't'}_{self.nname}"
        h = self.stack.enter_context(self.nc.sbuf_tensor(name, list(shape), dt))
        return V(h[:], [Buf(name)])

    def ps(self, shape, dt=F32, name=None):
        self.nname += 1
        name = f"{name or 'p'}_{self.nname}"
        h = self.stack.enter_context(self.nc.psum_tensor(name, list(shape), dt))
        return V(h[:], [Buf(name)])

    def tmp(self, shape, dt=F32, name="tmp", bufs=1):
        if not hasattr(self, "pools"):
            self.pools = {}
        shape = list(shape)
        key = (name, str(dt), len(shape))
        cands = self.pools.setdefault(key, [])
        pool = None
        for pl in cands:
            if all(a >= b for a, b in zip(pl[2], shape)) and len(pl[0]) >= bufs:
                pool = pl
                break
        if pool is None:
            full = [128] + shape[1:]
            pool = [[self.sb(full, dt, name) for _ in range(bufs)], 0, full]
            cands.append(pool)
        v = pool[0][pool[1] % len(pool[0])]
        pool[1] += 1
        if pool[2] != shape:
            v = v[tuple(slice(0, n) for n in shape)]
        return v

    def dram(self, name, shape, dt=F32, kind="Internal"):
        t = self.nc.dram_tensor(name, list(shape), dt, kind=kind)
        return V(t.ap(), [Buf(name)])

    def op(self, eng, fn, reads=(), writes=(), dma=False):
        i = len(self.ops)
        raw = set()
        oth = set()
        for v in reads:
            for b in v.bufs:
                raw.update(b.w.values())
        for v in writes:
            for b in v.bufs:
                oth.update(b.w.values())
                oth.update(b.r.values())
        key = ("d", i) if dma else eng
        for v in reads:
            for b in v.bufs:
                b.r[key] = i
        for v in writes:
            for b in v.bufs:
                b.w = {key: i}
                b.r = {}
        o = Op()
        o.eng, o.fn, o.raw, o.oth, o.dma, o.sig, o.id = eng, fn, raw, oth - raw, dma, False, i
        o.sem = o.val = None
        self.ops.append(o)
        return i

    def emit(self):
        nc = self.nc
        ops = self.ops
        engs = ["pe", "act", "dve", "pool", "sp"]
        waited = {e: {} for e in engs}
        waited_d = {e: set() for e in engs}
        need = []
        for o in ops:
            E = o.eng
            lst = []
            for d in sorted(o.raw | o.oth):
                p = ops[d]
                if p.dma:
                    if d in waited_d[E]:
                        continue
                    waited_d[E].add(d)
                    lst.append(d)
                else:
                    if p.eng == E and not o.dma:
                        if E == "pe" or (E in ("act", "dve") and d not in o.raw):
                            continue
                    if waited[E].get(p.eng, -1) >= d:
                        continue
                    waited[E][p.eng] = d
                    lst.append(d)
                    p.sig = True
            need.append(lst)
        cnt = {e: 0 for e in engs}
        for o in ops:
            if o.sig and not o.dma:
                cnt[o.eng] += 1
                o.val = cnt[o.eng]
        nsem = {e: (cnt[e] + SEM_LIMIT - 1) // SEM_LIMIT for e in engs}
        sems = {e: [self.stack.enter_context(nc.semaphore(f"s_{e}_{k}")) for k in range(nsem[e])] for e in engs}
        dma_engs = sorted({o.eng for o in ops if o.dma})
        dsem = {e: [[self.stack.enter_context(nc.semaphore(f"d_{e}_{k}_0")), 0, None] for k in range(DMA_SLOTS)]
                for e in dma_engs}
        dcount = {e: 0 for e in dma_engs}
        pre_wait = {}
        for o in ops:
            if o.dma:
                e = o.eng
                k = dcount[e] % DMA_SLOTS
                dcount[e] += 1
                slot = dsem[e][k]
                if slot[1] + 16 > SEM_LIMIT:
                    slot[0] = self.stack.enter_context(nc.semaphore(f"d_{e}_{k}_{o.id}"))
                    slot[1] = 0
                if slot[2] is not None:
                    pre_wait[o.id] = slot[2]
                slot[1] += 16
                o.sem, o.val = slot[0], slot[1]
                slot[2] = o.id

        def semval(p):
            if p.dma:
                return p.sem, p.val
            n = p.val - 1
            return sems[p.eng][n // SEM_LIMIT], (n % SEM_LIMIT) + 1

        by_eng = {e: [o for o in ops if o.eng == e] for e in engs}
        self.stats = {e: len(by_eng[e]) for e in engs}
        self.stats["sig"] = dict(cnt)

        def run(ename, eng):
            done_d = set()
            for o in by_eng[ename]:
                if o.id in pre_wait:
                    p = ops[pre_wait[o.id]]
                    if p.id not in done_d:
                        eng.wait_ge(p.sem, p.val)
                        done_d.add(p.id)
                for d in need[o.id]:
                    p = ops[d]
                    if p.dma:
                        if p.id in done_d:
                            continue
                        done_d.add(p.id)
                    s, v = semval(p)
                    eng.wait_ge(s, v)
                ins = o.fn(eng)
                if o.dma:
                    ins.then_inc(o.sem, 16)
                elif o.sig:
                    s, v = semval(o)
                    ins.then_inc(s, 1)

        with nc.Block() as block:
            @block.tensor
            def _(e):
                run("pe", e)

            @block.scalar
            def _(e):
                run("act", e)

            @block.vector
            def _(e):
                run("dve", e)

            @block.gpsimd
            def _(e):
                run("pool", e)

            @block.sync
            def _(e):
                run("sp", e)

    def dma(self, out, in_, eng="sp"):
        self.op(eng, lambda e: e.dma_start(out=out.ap, in_=in_.ap), [in_], [out], dma=True)

    def mm(self, out, lhsT, rhs, start=True, stop=True):
        self.op("pe", lambda e: e.matmul(out.ap, lhsT.ap, rhs.ap, start=start, stop=stop), [lhsT, rhs], [out])

    def tr(self, out, in_, ident):
        self.op("pe", lambda e: e.transpose(out.ap, in_.ap, ident.ap), [in_, ident], [out])

    def act(self, out, in_, func, bias=None, scale=None, accum=None):
        reads = [in_]
        kw = {}
        if isinstance(bias, V):
            reads.append(bias)
            kw["bias"] = bias.ap
        elif bias is not None:
            kw["bias"] = bias
        if isinstance(scale, V):
            reads.append(scale)
            kw["scale"] = scale.ap
        elif scale is not None:
            kw["scale"] = scale
        writes = [out]
        if accum is not None:
            kw["accum_out"] = accum.ap
            writes.append(accum)
        self.op("act", lambda e: e.activation(out.ap, in_.ap, func, **kw), reads, writes)

    def tt(self, eng, out, a, b, op):
        self.op(eng, lambda e: e.tensor_tensor(out.ap, a.ap, b.ap, op), [a, b], [out])

    def ts(self, eng, out, a, s1, op0, s2=None, op1=None):
        reads = [a]
        x1 = s1.ap if isinstance(s1, V) else s1
        x2 = s2.ap if isinstance(s2, V) else s2
        if isinstance(s1, V):
            reads.append(s1)
        if isinstance(s2, V):
            reads.append(s2)
        kw = {}
        if op1 is not None:
            kw["op1"] = op1
        self.op(eng, lambda e: e.tensor_scalar(out.ap, a.ap, x1, x2, op0, **kw), reads, [out])

    def stt(self, eng, out, a, s, b, op0, op1):
        reads = [a, b]
        x = s.ap if isinstance(s, V) else s
        if isinstance(s, V):
            reads.append(s)
        self.op(eng, lambda e: e.scalar_tensor_tensor(out.ap, a.ap, x, b.ap, op0, op1), reads, [out])

    def cp(self, eng, out, in_):
        if eng == "act":
            self.op(eng, lambda e: e.copy(out.ap, in_.ap), [in_], [out])
        else:
            self.op(eng, lambda e: e.tensor_copy(out.ap, in_.ap), [in_], [out])

    def memset(self, eng, out, val):
        self.op(eng, lambda e: e.memset(out.ap, val), [], [out])

    def recip(self, out, in_):
        self.op("dve", lambda e: e.reciprocal(out.ap, in_.ap), [in_], [out])

    def red(self, out, in_, op, axis=AX.X):
        self.op("dve", lambda e: e.tensor_reduce(out.ap, in_.ap, axis, op), [in_], [out])

    def asel(self, out, in_, pattern, cmp, fill, base, cm):
        self.op("pool", lambda e: e.affine_select(out.ap, in_.ap, pattern, cmp, fill, base=base,
                                                  channel_multiplier=cm), [in_], [out])

    def iota(self, out, pattern, base, cm):
        self.op("pool", lambda e: e.iota(out.ap, pattern, base=base, channel_multiplier=cm), [], [out])

    def fence(self, views, eng="sp"):
        self.op(eng, lambda e: e.nop(), list(views), [])


class StopBuild(Exception):
    pass


def make_cfg(D=4096, SEQ=4096, DEPTH=2, NSMP=2, DEC_SEQ=16, PAST=2048, NT=512, THETA=10000.0, STOP=0):
    c = dict(D=D, SEQ=SEQ, DEPTH=DEPTH, NSMP=NSMP, DEC_SEQ=DEC_SEQ, PAST=PAST, NT=NT, THETA=THETA, STOP=STOP)
    c["KC"] = D // 128
    c["DNW"] = D // 2
    c["H"] = c["DNW"] // 128
    c["CONV"] = 3 * c["DNW"]
    c["CT"] = c["CONV"] // 128
    c["AW"] = D - c["DNW"]
    c["QH"] = c["AW"] // 64
    c["KVH"] = c["QH"] // 8
    c["KVW"] = c["KVH"] * 64
    c["c_z"] = c["CONV"]
    c["c_b"] = c["c_z"] + c["DNW"]
    c["c_a"] = c["c_b"] + c["H"]
    c["c_q"] = c["c_a"] + c["H"]
    c["c_k"] = c["c_q"] + c["AW"]
    c["c_v"] = c["c_k"] + c["KVW"]
    c["c_za"] = c["c_v"] + c["KVW"]
    c["IN_DIM"] = c["c_za"] + c["AW"]
    c["WIN"] = 128
    return c


def build(cfg):
    D, SEQ, DEPTH, NSMP, LS, NT = cfg["D"], cfg["SEQ"], cfg["DEPTH"], cfg["NSMP"], cfg["DEC_SEQ"], cfg["NT"]
    KC, H, CT, CONV, DNW, AW, QH, KVH, KVW = (cfg[k] for k in ("KC", "H", "CT", "CONV", "DNW", "AW", "QH", "KVH", "KVW"))
    IN_DIM = cfg["IN_DIM"]
    NR = 1 + NSMP
    NBT = NT // 128
    NTILES = SEQ // NT
    NBS = SEQ // 128
    PW = max(NT, NSMP * LS)
    nc = bass.Bass("TRN2", target_bir_lowering=False)
    st = ExitStack()
    P = Prog(nc, st)

    EI, EO = "ExternalInput", "ExternalOutput"
    xp = P.dram("xp", [SEQ, D], F32, EI)
    xs = P.dram("xs", [NSMP * LS, D], F32, EI)
    sconv = P.dram("sconv", [DEPTH, NSMP, 3, CONV], F32, EI)
    sdn = P.dram("sdn", [DEPTH, NSMP, H, 128, 128], F32, EI)
    ck = P.dram("ck", [DEPTH, NSMP, 128, KVW], F32, EI)
    cv = P.dram("cv", [DEPTH, NSMP, 128, KVW], F32, EI)
    cvec = P.dram("cvec", [NR, D], F32, EI)
    w_ada = P.dram("w_ada", [DEPTH, D, 3 * D], F32, EI)
    b_ada = P.dram("b_ada", [DEPTH, 3 * D], F32, EI)
    g_pre = P.dram("g_pre", [DEPTH, D], F32, EI)
    g_post = P.dram("g_post", [DEPTH, D], F32, EI)
    w_in = P.dram("w_in", [DEPTH, D, IN_DIM], F32, EI)
    conv_w = P.dram("conv_w", [DEPTH, 4, CONV], F32, EI)
    a_log = P.dram("a_log", [DEPTH, H], F32, EI)
    dt_bias = P.dram("dt_bias", [DEPTH, H], F32, EI)
    dn_norm = P.dram("dn_norm", [DEPTH, 128], F32, EI)
    sinks = P.dram("sinks", [DEPTH, QH], F32, EI)
    w_out = P.dram("w_out", [DEPTH, D, D], F32, EI)

    yp = P.dram("yp", [SEQ, D], F32, EO)
    ys = P.dram("ys", [NSMP * LS, D], F32, EO)
    convp = P.dram("convp", [DEPTH, 3, CONV], F32, EO)
    dnp = P.dram("dnp", [DEPTH, H, 128, 128], F32, EO)
    kpo = P.dram("kpo", [DEPTH, 128, KVW], F32, EO)
    vpo = P.dram("vpo", [DEPTH, 128, KVW], F32, EO)
    convs = P.dram("convs", [DEPTH, NSMP, 3, CONV], F32, EO)
    dns = P.dram("dns", [DEPTH, NSMP, H, 128, 128], F32, EO)
    kso = P.dram("kso", [DEPTH, NSMP, 128, KVW], F32, EO)
    vso = P.dram("vso", [DEPTH, NSMP, 128, KVW], F32, EO)
    outs = [yp, ys, convp, dnp, kpo, vpo, convs, dns, kso, vso]
    ggd = P.dram("ggd", [NR, D], F32)
    xmid_p = [P.dram(f"xmid_p{l}", [SEQ, D], F32) for l in range(DEPTH - 1)]
    xmid_s = [P.dram(f"xmid_s{l}", [NSMP * LS, D], F32) for l in range(DEPTH - 1)]

    bankA = [P.ps([128, 512], F32, "bA") for _ in range(2)]
    bankT = [P.ps([128, 1024], BF16, "bT") for _ in range(2)]
    bankD = [P.ps([128, 512], F32, "bD") for _ in range(3)]
    bankS = P.ps([128, 512], F32, "bS")
    dslots = []
    for q in range(4):
        for b in bankD:
            dslots.append(V(b.ap[:, q * 128:(q + 1) * 128], b.bufs))
    tslots = []
    for q in range(8):
        for b in bankT:
            tslots.append(V(b.ap[:, q * 128:(q + 1) * 128], b.bufs))
    sslots = [V(bankS.ap[:, q * 256:(q + 1) * 256], bankS.bufs) for q in range(2)]
    ctr = {"A": 0, "D": 0, "T": 0, "S": 0, "W": 0}

    def nxt(kind, lst):
        v = lst[ctr[kind] % len(lst)]
        ctr[kind] += 1
        return v

    psA = lambda: nxt("A", bankA)
    psD = lambda: nxt("D", dslots)
    psT = lambda: nxt("T", tslots)
    psS = lambda: nxt("S", sslots)

    ident = P.sb([128, 128], F32, "ident")
    identb = P.sb([128, 128], BF16, "identb")
    ones = P.sb([128, 128], F32, "ones")
    P.memset("pool", ident, 0.0)
    P.asel(ident, ident, [[-1, 128]], ALU.not_equal, 1.0, 0, 1)
    P.cp("dve", identb, ident)
    P.memset("pool", ones, 1.0)

    def make_masks(nt, CL):
        nch = nt // CL
        tmp = P.sb([nt, nt], F32, "mtmp")
        mS = P.sb([nt, nt], BF16, "mS")
        mU = P.sb([nt, nt], BF16, "mU")
        tri = P.sb([nt, nt], F32, "tri")
        P.memset("pool", tmp, 0.0)
        P.asel(tmp, tmp, [[-1, nt]], ALU.is_gt, NEG, 0, 1)
        for c in range(1, nch):
            P.memset("pool", tmp[c * CL:(c + 1) * CL, 0:c * CL], NEG)
        P.cp("dve", mS, tmp)
        tmp2 = P.sb([nt, nt], F32, "mtmp2")
        P.memset("pool", tmp2, 0.0)
        P.asel(tmp2, tmp2, [[1, nt]], ALU.is_ge, NEG, 0, -1)
        for c in range(0, nch - 1):
            P.memset("pool", tmp2[c * CL:(c + 1) * CL, (c + 1) * CL:nt], NEG)
        P.cp("dve", mU, tmp2)
        P.memset("pool", tri, 1.0)
        P.asel(tri, tri, [[1, nt]], ALU.is_ge, 0.0, 0, -1)
        for c in range(0, nch - 1):
            P.memset("pool", tri[c * CL:(c + 1) * CL, (c + 1) * CL:nt], 0.0)
        elast = []
        for c in range(nch):
            e = P.sb([nt, 128], F32, "elast")
            P.memset("pool", e, 0.0)
            P.asel(e, e, [[0, 128]], ALU.not_equal, 1.0, -(c * CL + CL - 1), 1)
            elast.append(e)
        return dict(nt=nt, CL=CL, nch=nch, mS=mS, mU=mU, tri=tri, elast=elast)

    MK_P = make_masks(128, 64)
    MK_S = make_masks(LS, min(64, LS))
    amf = P.sb([128, 256], F32, "amf")
    amask = P.sb([128, 256], BF16, "amask")
    P.memset("pool", amf, 0.0)
    P.memset("pool", amf[0:64, 192:256], NEG)
    P.memset("pool", amf[64:128, 0:64], NEG)
    P.cp("dve", amask, amf)

    def rope_tables(npart, nblk, pos0):
        cosT = P.sb([npart, nblk, 32], F32, "cosT")
        sinT = P.sb([npart, nblk, 32], F32, "sinT")
        fi = P.tmp([npart, 32], I32, "fi")
        P.iota(fi, [[1, 32]], 0, 0)
        ff = P.tmp([npart, 32], F32, "ff")
        P.cp("dve", ff, fi)
        inv = P.tmp([npart, 32], F32, "inv")
        P.act(inv, ff, AF.Exp, scale=-math.log(cfg["THETA"]) / 32.0)
        CHB = 2
        for b0 in range(0, nblk, CHB):
            n = min(CHB, nblk - b0)
            posi = P.tmp([npart, n], I32, "posi")
            P.iota(posi, [[128, n]], pos0 + 128 * b0, 1)
            posf = P.tmp([npart, n], F32, "posf")
            P.cp("dve", posf, posi)
            ang = P.tmp([npart, n, 32], F32, "ang")
            P.tt("dve", ang, posf.re("p (b o) -> p b o", o=1).bc([npart, n, 32]),
                 inv.re("p (o f) -> p o f", o=1).bc([npart, n, 32]), ALU.mult)
            for shift, tab in ((0.0, sinT), (0.25, cosT)):
                r = P.tmp([npart, n, 32], F32, "rr")
                P.ts("dve", r, ang, 1.0 / (2.0 * math.pi), ALU.mult, shift, ALU.add)
                ri = P.tmp([npart, n, 32], I32, "ri")
                P.cp("dve", ri, r)
                rf = P.tmp([npart, n, 32], F32, "rf")
                P.cp("dve", rf, ri)
                P.tt("dve", r, r, rf, ALU.subtract)
                P.ts("dve", rf, r, 0.5, ALU.is_gt)
                P.tt("dve", r, r, rf, ALU.subtract)
                P.ts("dve", rf, r, -0.5, ALU.is_lt)
                P.tt("dve", r, r, rf, ALU.add)
                P.act(tab[:, b0:b0 + n, :], r, AF.Sin, scale=2.0 * math.pi)
        return cosT, sinT

    cos_p, sin_p = rope_tables(128, NBS, 0)
    cos_s, sin_s = rope_tables(LS, 1, cfg["PAST"])

    hT = P.sb([128, KC, PW], BF16, "hT")
    hT_bufs = [Buf("hTy") for _ in range(max(NBT, NSMP))]
    hT = hT.wb(*hT_bufs)
    ybuf_flat = V(hT.ap.rearrange("p k t -> p (k t)"), hT_bufs)
    oT = P.sb([128, KC, PW], BF16, "oT")
    oT_flat = oT.re("p k t -> p (k t)")
    xnb = oT_flat[:, D:2 * D]
    NWB = 3
    NBUFG = max(NBT, NSMP)
    wbufs = [P.sb([128, KC, 128], BF16, "wb") for _ in range(NWB)]
    xbuf = P.sb([128, D], F32, "xbuf")
    S_all = P.sb([128, H, 128], F32, "S_all")
    S_bf = P.sb([128, H, 128], BF16, "S_bf")
    S_bufs = [Buf("S") for _ in range(H)]
    Sb_bufs = [Buf("Sb") for _ in range(H)]
    hist = P.sb([128, CT, NSMP, 3], F32, "hist")
    hist_bufs = [Buf("hist") for _ in range(CT)]
    kring = P.sb([128, KVH, 2, 128], BF16, "kring")
    vring = P.sb([128, KVH, 2, 64], BF16, "vring")
    kr_bufs = [[Buf("kr") for _ in range(2)] for _ in range(KVH)]
    vr_bufs = [[Buf("vr") for _ in range(2)] for _ in range(KVH)]
    gsT = P.sb([128, KC, NR], F32, "gsT")
    shT = P.sb([128, KC, NR], F32, "shT")
    cwT = P.sb([128, CT, 4], F32, "cwT")
    nA_b = P.sb([128, H], F32, "nA")
    dtb_b = P.sb([128, H], F32, "dtb")
    dnn_b = P.sb([128, 128], F32, "dnn")
    sink_b = P.sb([128, QH], F32, "sink")
    nsink_b = P.sb([128, QH], F32, "nsink")

    wctr = [0]

    NSLOT = IN_DIM // 128 + 2 * KVH + 2 + KC
    wsc = [P.dram(f"wsc{l}", [NSLOT, 128, KC * 128], BF16) for l in range(DEPTH)]
    wcache = {}

    def load_w(src_cols_view, key=None):
        wb = wbufs[wctr[0] % NWB]
        wctr[0] += 1
        ncols = src_cols_view.ap.shape[1]
        dst = wb[:, :, 0:ncols]
        if key is not None and key in wcache:
            P.dma(dst, wcache[key], eng="sp")
            return dst
        P.dma(dst, src_cols_view.re("(kc p) n -> p kc n", p=128), eng="pool")
        if key is not None:
            l_ = key[0]
            slot = sum(1 for k_ in wcache if k_[0] == l_)
            assert slot < NSLOT
            sv = V(wsc[l_].ap[slot].rearrange("p (kc n) -> p kc n", n=128)[:, :, 0:ncols], [Buf("wsc")])
            P.dma(sv, dst, eng="sp")
            wcache[key] = sv
        return dst

    def layer_params(l):
        cs = xbuf[0:NR, :]
        P.dma(cs, cvec)
        P.act(cs, cs, AF.Silu)
        scT = P.tmp([128, KC, NR], BF16, "scT")
        for kc in range(KC):
            pt = psD()
            P.tr(pt[:, 0:NR], cs[:, kc * 128:(kc + 1) * 128], ident[0:NR, 0:NR])
            P.cp("dve", scT[:, kc, :], pt[:, 0:NR])
        ncol = 3 * KC
        baT = P.tmp([128, ncol], F32, "baT")
        for c0 in range(0, ncol, 128):
            n = min(128, ncol - c0)
            rows = P.tmp([128, 128], F32, "rows", 2)
            P.dma(rows[0:n, :], b_ada[l, c0 * 128:(c0 + n) * 128].re("(r p) -> r p", p=128))
            pt = psD()
            P.tr(pt[:, 0:n], rows[0:n, :], ident[0:n, 0:n])
            P.cp("dve", baT[:, c0:c0 + n], pt[:, 0:n])
        gpT = P.tmp([128, KC], F32, "gpT")
        rows = P.tmp([128, 128], F32, "rows", 2)
        P.dma(rows[0:KC, :], g_pre[l].re("(r p) -> r p", p=128))
        pt = psD()
        P.tr(pt[:, 0:KC], rows[0:KC, :], ident[0:KC, 0:KC])
        P.cp("dve", gpT, pt[:, 0:KC])
        adaT = P.tmp([128, 2 * KC, NR], F32, "adaT")
        for j in range(3 * KC):
            wb = load_w(w_ada[l, :, j * 128:(j + 1) * 128])
            if j < 2 * KC:
                pt = psD()
                for kc in range(KC):
                    P.mm(pt[:, 0:NR], wb[:, kc, :], scT[:, kc, :], start=(kc == 0), stop=(kc == KC - 1))
                P.ts("dve", adaT[:, j, :], pt[:, 0:NR], baT[:, j:j + 1], ALU.add)
            else:
                pt = psD()
                for kc in range(KC):
                    P.mm(pt[0:NR, :], scT[:, kc, :], wb[:, kc, :], start=(kc == 0), stop=(kc == KC - 1))
                c0 = (j - 2 * KC) * 128
                bg = P.tmp([NR, 128], F32, "bg", 2)
                gp = P.tmp([NR, 128], F32, "gp", 2)
                P.dma(bg, b_ada[l:l + 1, 2 * D + c0:2 * D + c0 + 128].bc([NR, 128]))
                P.dma(gp, g_post[l:l + 1, c0:c0 + 128].bc([NR, 128]))
                gt = P.tmp([NR, 128], F32, "gt", 2)
                P.tt("dve", gt, pt[0:NR, :], bg, ALU.add)
                P.tt("pool", gt, gt, gp, ALU.mult)
                P.dma(ggd[:, c0:c0 + 128], gt)
        for r in range(NR):
            P.stt("dve", gsT[:, :, r], adaT[:, KC:2 * KC, r], 1.0, gpT, ALU.add, ALU.mult)
        P.cp("dve", shT, adaT[:, 0:KC, :])
        CH = min(32, KC)
        for c0 in range(0, CT, CH):
            n = min(CH, CT - c0)
            cwr = xbuf[0:4, 0:n * 128]
            P.dma(cwr, conv_w[l, :, c0 * 128:(c0 + n) * 128])
            pt = psD()
            for i in range(n):
                P.tr(pt[:, i * 4:(i + 1) * 4], cwr[:, i * 128:(i + 1) * 128], ident[0:4, 0:4])
            P.cp("dve", cwT[:, c0:c0 + n, :], pt[:, 0:4 * n].re("p (c j) -> p c j", j=4))
        al = P.tmp([128, H], F32, "al")
        P.dma(al, a_log[l:l + 1, :].bc([128, H]))
        P.act(al, al, AF.Exp)
        P.ts("dve", nA_b, al, -1.0, ALU.mult)
        P.dma(dtb_b, dt_bias[l:l + 1, :].bc([128, H]))
        P.dma(dnn_b, dn_norm[l:l + 1, :].bc([128, 128]))
        P.dma(sink_b, sinks[l:l + 1, :].bc([128, QH]))
        P.ts("dve", nsink_b, sink_b, -1.0, ALU.mult)

    def dn_gates(blk, ba_ps, MK):
        nt, CL, nch = MK["nt"], MK["CL"], MK["nch"]
        g = {}
        beta = P.tmp([nt, H], F32, "beta", NBUFG)
        P.act(beta, ba_ps[:, 0:H], AF.Sigmoid)
        xa = P.tmp([nt, H], F32, "xa", NBUFG)
        P.tt("dve", xa, ba_ps[:, H:2 * H], dtb_b[0:nt, :], ALU.add)
        P.act(xa, xa, AF.Exp)
        P.act(xa, xa, AF.Ln, bias=1.0)
        gg = P.tmp([nt, H], F32, "gg", NBUFG)
        P.tt("dve", gg, xa, nA_b[0:nt, :], ALU.mult)
        pt = psD()
        P.mm(pt[0:nt, 0:H], MK["tri"], gg)
        gc = P.tmp([nt, H], F32, "gc", NBUFG)
        P.cp("dve", gc, pt[0:nt, 0:H])
        egc = P.tmp([nt, H], F32, "egc", NBUFG)
        P.act(egc, gc, AF.Exp)
        bge = P.tmp([nt, H], F32, "bge", NBUFG)
        P.tt("dve", bge, beta, egc, ALU.mult)
        dec = P.tmp([128, nch, H], F32, "dec", NBUFG)
        kds = P.tmp([nt, H], F32, "kds", NBUFG)
        for c in range(nch):
            pg = psD()
            P.mm(pg[:, 0:H], MK["elast"][c], gc)
            P.act(dec[:, c, :], pg[:, 0:H], AF.Exp)
            rs = slice(c * CL, (c + 1) * CL)
            P.tt("dve", kds[rs, :], pg[rs, 0:H], gc[rs, :], ALU.subtract)
        P.act(kds, kds, AF.Exp)
        G1 = P.tmp([nt, H, 2], F32, "G1", NBUFG)
        G2 = P.tmp([nt, H, 2], F32, "G2", NBUFG)
        P.memset("pool", G1, 1.0)
        P.memset("pool", G2, 1.0)
        P.cp("dve", G1[:, :, 0], gc)
        P.ts("dve", G2[:, :, 1], gc, -1.0, ALU.mult)
        g.update(beta=beta, egc=egc, bge=bge, dec=dec, kds=kds, G1=G1, G2=G2)
        return g

    def dn_block(h, g, MK, qT, kT, vT, zs, Sv, Sbv, oT_dst):
        nt, CL, nch = MK["nt"], MK["CL"], MK["nch"]
        p1 = psD()
        P.mm(p1[0:2, 0:nt], g["G1"][:, h, :], ident[0:nt, 0:nt])
        R1 = P.tmp([2, nt], F32, "R1")
        P.cp("act", R1, p1[0:2, 0:nt])
        p2 = psD()
        P.mm(p2[0:2, 0:nt], g["G2"][:, h, :], ident[0:nt, 0:nt])
        R2 = P.tmp([2, nt], F32, "R2")
        P.cp("dve", R2, p2[0:2, 0:nt])
        stop_at(11)
        pd = psD()
        P.mm(pd[0:nt, 0:nt], identb[0:nt, 0:nt], MK["mS"], start=True, stop=False)
        P.mm(pd[0:nt, 0:nt], R1, R2, start=False, stop=True)
        gamS = P.tmp([nt, nt], F32, "gamS")
        P.act(gamS, pd[0:nt, 0:nt], AF.Exp)
        pdt = psD()
        P.mm(pdt[0:nt, 0:nt], identb[0:nt, 0:nt], MK["mU"], start=True, stop=False)
        P.mm(pdt[0:nt, 0:nt], R2, R1, start=False, stop=True)
        gamT = P.tmp([nt, nt], F32, "gamT")
        P.act(gamT, pdt[0:nt, 0:nt], AF.Exp)
        stop_at(12)
        pkk = psD()
        P.mm(pkk[0:nt, 0:nt], kT, kT)
        A = P.tmp([nt, nt], F32, "A")
        P.stt("dve", A, pkk[0:nt, 0:nt], g["beta"][:, h:h + 1], gamS, ALU.mult, ALU.mult)
        pkq = psD()
        P.mm(pkq[0:nt, 0:nt], kT, qT)
        qkmT = P.tmp([nt, nt], BF16, "qkmT")
        P.tt("dve", qkmT, pkq[0:nt, 0:nt], gamT, ALU.mult)
        stop_at(13)
        pat = psD()
        P.tr(pat[0:nt, 0:nt], A, ident[0:nt, 0:nt])
        AT = P.tmp([nt, nt], F32, "AT")
        P.cp("act", AT, pat[0:nt, 0:nt])
        RT = P.tmp([nt, nt], F32, "RT")
        P.tt("pool", RT, ident[0:nt, 0:nt], AT, ALU.subtract)
        stop_at(14)
        X, XT = A, AT
        nlev = int(math.log2(CL)) - 1
        for lev in range(nlev):
            px = psD()
            P.mm(px[0:nt, 0:nt], XT, X)
            X2 = P.tmp([nt, nt], F32, "X2", 2)
            P.cp("act", X2, px[0:nt, 0:nt])
            if lev < nlev - 1:
                pxt = psD()
                P.mm(pxt[0:nt, 0:nt], X, XT)
                X2T = P.tmp([nt, nt], F32, "X2T", 2)
                P.cp("dve", X2T, pxt[0:nt, 0:nt])
            pr = psD()
            P.mm(pr[0:nt, 0:nt], X2, RT)
            RT2 = P.tmp([nt, nt], F32, "RT2", 2)
            P.tt("dve", RT2, pr[0:nt, 0:nt], RT, ALU.add)
            RT = RT2
            if lev < nlev - 1:
                X, XT = X2, X2T
        stop_at(15)
        ptk = psT()
        P.tr(ptk[0:nt, :], kT, identb)
        ptv = psT()
        P.tr(ptv[0:nt, :], vT, identb)
        stop_at(151)
        RHSw = P.tmp([nt, 128], F32, "RHSw")
        P.act(RHSw, ptk[0:nt, :], AF.Identity, scale=g["bge"][:, h:h + 1])
        stop_at(152)
        kdec = P.tmp([nt, 128], BF16, "kdec")
        P.act(kdec, ptk[0:nt, :], AF.Identity, scale=g["kds"][:, h:h + 1])
        stop_at(153)
        RHSu = P.tmp([nt, 128], F32, "RHSu")
        P.act(RHSu, ptv[0:nt, :], AF.Identity, scale=g["beta"][:, h:h + 1])
        stop_at(154)
        pu = psD()
        P.mm(pu[0:nt, :], RT, RHSu)
        u = P.tmp([nt, 128], F32, "u")
        P.cp("act", u, pu[0:nt, :])
        stop_at(155)
        pw = psD()
        P.mm(pw[:, 0:nt], RHSw, RT)
        wT = P.tmp([128, nt], BF16, "wT")
        P.cp("dve", wT, pw[:, 0:nt])
        stop_at(16)
        vnew = P.tmp([nt, 128], BF16, "vnew")
        o = P.tmp([nt, 128], F32, "o")
        for c in range(nch):
            rs = slice(c * CL, (c + 1) * CL)
            pp1 = psD()
            P.mm(pp1[rs, :], wT[:, rs], Sbv)
            P.tt("dve", vnew[rs, :], u[rs, :], pp1[rs, :], ALU.subtract)
            pp2 = psD()
            P.mm(pp2[rs, :], qT[:, rs], Sbv)
            t2 = P.tmp([nt, 128], F32, "t2", 2)
            P.act(t2[rs, :], pp2[rs, :], AF.Identity, scale=g["egc"][rs, h:h + 1])
            pp3 = psD()
            P.mm(pp3[rs, :], qkmT[rs, rs], vnew[rs, :])
            P.tt("dve", o[rs, :], pp3[rs, :], t2[rs, :], ALU.add)
            pp4 = psD()
            P.mm(pp4, kdec[rs, :], vnew[rs, :])
            P.stt("dve", Sv, Sv, g["dec"][:, c, h:h + 1], pp4, ALU.mult, ALU.add)
            P.cp("act", Sbv, Sv)
        stop_at(17)
        junk = P.tmp([nt, 128], F32, "junk")
        ss = P.tmp([nt, 1], F32, "ss")
        P.act(junk, o, AF.Square, accum=ss)
        P.act(ss, ss, AF.Sqrt, scale=1.0 / 128.0, bias=EPS)
        P.recip(ss, ss)
        og = P.tmp([nt, 128], F32, "og")
        P.stt("dve", og, o, ss, dnn_b[0:nt, :], ALU.mult, ALU.mult)
        ogb = P.tmp([nt, 128], BF16, "ogb")
        P.tt("pool", ogb, og, zs, ALU.mult)
        pt = psT()
        P.tr(pt[:, 0:nt], ogb, identb[0:nt, 0:nt])
        P.cp("act", oT_dst, pt[:, 0:nt])

    def rope(dst, src, cosv, sinv, nt, nh):
        cb = cosv.re("p (o f) -> p o f", o=1).bc([nt, nh, 32])
        sb_ = sinv.re("p (o f) -> p o f", o=1).bc([nt, nh, 32])
        x1, x2 = src[:, :, 0:32], src[:, :, 32:64]
        t1 = P.tmp([nt, nh, 32], F32, "rt1")
        t2 = P.tmp([nt, nh, 32], F32, "rt2")
        P.tt("dve", t1, x2, sb_, ALU.mult)
        P.tt("dve", t2, x1, cb, ALU.mult)
        P.tt("pool", dst[:, :, 0:32], t2, t1, ALU.subtract)
        t3 = P.tmp([nt, nh, 32], F32, "rt3")
        t4 = P.tmp([nt, nh, 32], F32, "rt4")
        P.tt("dve", t3, x1, sb_, ALU.mult)
        P.tt("dve", t4, x2, cb, ALU.mult)
        P.tt("pool", dst[:, :, 32:64], t4, t3, ALU.add)

    def lockstep(gens):
        live = list(gens)
        while live:
            nxt_ = []
            for g_ in live:
                try:
                    next(g_)
                    nxt_.append(g_)
                except StopIteration:
                    pass
            live = nxt_

    def attn_head(nt, qTv, ksrcs, vsrcs, masked, hq, zsv, og_dst):
        sc = psS()
        nk_tot = sum(nk for _, nk in ksrcs)
        if masked:
            mcols = amask[0:nt, 256 - nk_tot:256]
            P.mm(sc[0:nt, 0:nk_tot], identb[0:nt, 0:nt], mcols, start=True, stop=False)
        off = 0
        for i, (kv_, nk) in enumerate(ksrcs):
            P.mm(sc[0:nt, off:off + nk], qTv, kv_, start=not masked, stop=(not masked) or (i == len(ksrcs) - 1))
            off += nk
        yield
        mraw = P.tmp([nt, 1], F32, "mraw", 2)
        P.red(mraw, sc[0:nt, 0:nk_tot], ALU.max)
        negm = P.tmp([nt, 1], F32, "negm", 2)
        P.ts("dve", negm, mraw, -0.125, ALU.mult, nsink_b[0:nt, hq:hq + 1], ALU.min)
        yield
        p = P.tmp([nt, 256], BF16, "p", 2)
        ssum = P.tmp([nt, 1], F32, "ssum", 2)
        P.act(p[:, 0:nk_tot], sc[0:nt, 0:nk_tot], AF.Exp, bias=negm, scale=0.125, accum=ssum)
        sk = P.tmp([nt, 1], F32, "sk", 2)
        P.act(sk, sink_b[0:nt, hq:hq + 1], AF.Exp, bias=negm)
        yield
        P.tt("dve", sk, sk, ssum, ALU.add)
        P.recip(sk, sk)
        pTs = P.tmp([128, 2, nt], BF16, "pTs", 2)
        off = 0
        for i, (_, nk) in enumerate(ksrcs):
            pt = psT()
            P.tr(pt[0:nk, 0:nt], p[:, off:off + nk], identb[0:nt, 0:nt])
            P.cp("act", pTs[0:nk, i, :], pt[0:nk, 0:nt])
            off += nk
        yield
        po = psD()
        for i, (_, nk) in enumerate(ksrcs):
            P.mm(po[0:nt, 0:64], pTs[0:nk, i, :], vsrcs[i], start=(i == 0), stop=(i == len(ksrcs) - 1))
        yield
        P.stt("dve", og_dst, po[0:nt, 0:64], sk, zsv, ALU.mult, ALU.mult)

    def do_tile(l, kind, t):
        last_layer = (l == DEPTH - 1)
        if kind == "p":
            x_in = xp if l == 0 else xmid_p[l - 1]
            x_out = yp if last_layer else xmid_p[l]
            blocks = [dict(r0=t * NT + b * 128, nt=128, row=0, c0=b * 128, seq=0, gb=t * NBT + b) for b in range(NBT)]
            MK = MK_P
            nseq, L = 1, NT
        else:
            x_in = xs if l == 0 else xmid_s[l - 1]
            x_out = ys if last_layer else xmid_s[l]
            blocks = [dict(r0=s * LS, nt=LS, row=1 + s, c0=s * LS, seq=s, gb=0) for s in range(NSMP)]
            MK = MK_S
            nseq, L = NSMP, LS
        NTt = sum(b["nt"] for b in blocks)
        nb = len(blocks)

        for bi, b in enumerate(blocks):
            nt = b["nt"]
            P.dma(xbuf[0:nt, :], x_in[b["r0"]:b["r0"] + nt, :])
            ss = P.tmp([128, 1], F32, "ss0", 2)
            P.act(oT_flat[0:nt, 0:D], xbuf[0:nt, :], AF.Square, accum=ss[0:nt, :])
            P.act(ss[0:nt, :], ss[0:nt, :], AF.Sqrt, scale=1.0 / D, bias=EPS)
            P.recip(ss[0:nt, :], ss[0:nt, :])
            P.ts("dve", xnb[0:nt, :], xbuf[0:nt, :], ss[0:nt, :], ALU.mult)
            for kc in range(KC):
                pt = psT()
                P.tr(pt[:, 0:nt], xnb[0:nt, kc * 128:(kc + 1) * 128], identb[0:nt, 0:nt])
                P.act(hT[:, kc, b["c0"]:b["c0"] + nt], pt[:, 0:nt], AF.Identity,
                      bias=shT[:, kc, b["row"]:b["row"] + 1], scale=gsT[:, kc, b["row"]:b["row"] + 1])

        stop_at(3)

        def proj_tm(wb, ncols):
            acc = psA()
            res = []
            for bi, b in enumerate(blocks):
                nt = b["nt"]
                dst = acc[0:nt, bi * ncols:(bi + 1) * ncols]
                for kc in range(KC):
                    P.mm(dst, hT[:, kc, b["c0"]:b["c0"] + nt], wb[:, kc, 0:ncols], start=(kc == 0), stop=(kc == KC - 1))
                res.append(dst)
            return res

        def proj_cm(wb):
            acc = psA()
            dst = acc[:, 0:NTt]
            for kc in range(KC):
                P.mm(dst, wb[:, kc, :], hT[:, kc, 0:NTt], start=(kc == 0), stop=(kc == KC - 1))
            return dst

        wb = load_w(w_in[l, :, cfg["c_b"]:cfg["c_b"] + 2 * H], (l, "b"))
        ba = proj_tm(wb, 2 * H)
        gates = [dn_gates(b, ba[bi], MK) for bi, b in enumerate(blocks)]

        stop_at(4)
        def dn_stage_a1(h):
            acts = []
            for which in range(3):
                ct = which * H + h
                wb = load_w(w_in[l, :, ct * 128:(ct + 1) * 128], (l, "c", ct))
                acc = proj_cm(wb)
                pre = P.tmp([128, nseq, L + 3], F32, "pre", 2)
                hv = V(hist.ap[:, ct, 0:nseq, :], [hist_bufs[ct]])
                P.cp("pool", pre[:, :, 0:3], hv)
                P.cp("act", pre[:, :, 3:3 + L], acc.re("p (s t) -> p s t", s=nseq))
                P.cp("pool", hv, pre[:, :, L:L + 3])
                y = P.tmp([128, nseq, L], F32, "convy")
                P.act(y, pre[:, :, 0:L], AF.Identity, scale=cwT[:, ct, 0:1])
                for j in range(1, 4):
                    P.stt("dve", y, pre[:, :, j:j + L], cwT[:, ct, j:j + 1], y, ALU.mult, ALU.add)
                a_ = P.tmp([128, NTt], F32, "cact", 3)
                P.act(a_, y.re("p s t -> p (s t)"), AF.Silu)
                acts.append(a_)
            wb = load_w(w_in[l, :, cfg["c_z"] + h * 128:cfg["c_z"] + (h + 1) * 128], (l, "z", h))
            zps = proj_tm(wb, 128)
            zs_all = P.tmp([128, nb, 128], BF16, "zs", 2)
            for bi, b in enumerate(blocks):
                P.act(zs_all[0:b["nt"], bi, :], zps[bi], AF.Silu)
            return acts, zs_all

        def dn_stage_a2(acts):
            qa, ka, va = acts
            normed = []
            for a_, scl in ((qa, 128.0 ** -0.5), (ka, 1.0)):
                sq = P.tmp([128, NTt], F32, "sq")
                P.tt("pool", sq, a_, a_, ALU.mult)
                acc = psA()
                P.mm(acc[:, 0:NTt], ones, sq)
                rs_ = P.tmp([128, NTt], F32, "rs")
                P.act(rs_, acc[:, 0:NTt], AF.Sqrt, bias=EPS)
                P.recip(rs_, rs_)
                nb_ = P.tmp([128, NTt], BF16, "nrm", 4)
                P.stt("dve", nb_, a_, scl, rs_, ALU.mult, ALU.mult)
                normed.append(nb_)
            qTb, kTb = normed
            vTb = P.tmp([128, NTt], BF16, "vTb", 2)
            P.cp("pool", vTb, va)
            return qTb, kTb, vTb

        def dn_stage_b(h, qTb, kTb, vTb, zs_all):
            for bi, b in enumerate(blocks):
                nt, c0 = b["nt"], b["c0"]
                if kind == "p":
                    Sv = V(S_all.ap[:, h, :], [S_bufs[h]])
                    Sbv = V(S_bf.ap[:, h, :], [Sb_bufs[h]])
                    if b["gb"] == 0:
                        P.memset("pool", Sv, 0.0)
                        P.memset("pool", Sbv, 0.0)
                else:
                    hs = (h + b["seq"]) % H
                    Sv = V(S_all.ap[:, hs, :], [S_bufs[hs]])
                    Sbv = V(S_bf.ap[:, hs, :], [Sb_bufs[hs]])
                    P.dma(Sv, sdn[l, b["seq"], h])
                    P.cp("act", Sbv, Sv)
                dn_block(h, gates[bi], MK, qTb[:, c0:c0 + nt], kTb[:, c0:c0 + nt], vTb[:, c0:c0 + nt],
                         zs_all[0:nt, bi, :], Sv, Sbv, oT[:, h, c0:c0 + nt])
                if kind == "s":
                    P.dma(dns[l, b["seq"], h], Sv)
                elif b["gb"] == NBS - 1:
                    P.dma(dnp[l, h], Sv)

        acts_c, zs_c = dn_stage_a1(0)
        qkv_c = dn_stage_a2(acts_c)
        for h in range(H):
            if h + 1 < H:
                acts_n, zs_n = dn_stage_a1(h + 1)
            dn_stage_b(h, *qkv_c, zs_c)
            if h + 1 < H:
                qkv_c = dn_stage_a2(acts_n)
                zs_c = zs_n
        stop_at(5)
        if kind == "s" or t == NTILES - 1:
            CH = min(32, KC)
            for s in range(nseq):
                for c0 in range(0, CT, CH):
                    n = min(CH, CT - c0)
                    crow = xbuf[0:3, 0:n * 128]
                    for c1 in range(0, n, 4):
                        pt = psA()
                        for i in range(4):
                            hv = V(hist.ap[:, c0 + c1 + i, s, :], [hist_bufs[c0 + c1 + i]])
                            P.tr(pt[0:3, i * 128:(i + 1) * 128], hv, ident)
                        P.cp("dve", crow[:, c1 * 128:(c1 + 4) * 128], pt[0:3, :])
                    dst = convs[l, s] if kind == "s" else convp[l]
                    P.dma(dst[:, c0 * 128:(c0 + n) * 128], crow)

        stop_at(6)
        for gk in range(KVH):
            wbk = load_w(w_in[l, :, cfg["c_k"] + gk * 64:cfg["c_k"] + (gk + 1) * 64], (l, "k", gk))
            kps = proj_tm(wbk, 64)
            krot = []
            for bi, b in enumerate(blocks):
                nt = b["nt"]
                ksb = P.tmp([128, 1, 64], F32, "ksb", 2)
                P.cp("act", ksb[0:nt, 0, :], kps[bi])
                kr = P.tmp([128, 1, 64], F32, "kr", 4)
                if kind == "p":
                    cosv, sinv = cos_p[:, b["gb"], :], sin_p[:, b["gb"], :]
                else:
                    cosv, sinv = cos_s[:, 0, :], sin_s[:, 0, :]
                rope(kr[0:nt], ksb[0:nt], cosv[0:nt], sinv[0:nt], nt, 1)
                krot.append(kr)
                if kind == "s":
                    P.dma(kso[l, b["seq"], 128 - LS:128, gk * 64:(gk + 1) * 64], kr[0:nt, 0, :])
                elif b["gb"] == NBS - 1:
                    P.dma(kpo[l, :, gk * 64:(gk + 1) * 64], kr[0:nt, 0, :])
            wbv = load_w(w_in[l, :, cfg["c_v"] + gk * 64:cfg["c_v"] + (gk + 1) * 64], (l, "v", gk))
            vps = proj_tm(wbv, 64)
            vsb = []
            for bi, b in enumerate(blocks):
                nt = b["nt"]
                vf = P.tmp([128, 64], F32, "vf", 4)
                P.cp("act", vf[0:nt, :], vps[bi])
                vsb.append(vf)
                if kind == "s":
                    P.dma(vso[l, b["seq"], 128 - LS:128, gk * 64:(gk + 1) * 64], vf[0:nt, :])
                elif b["gb"] == NBS - 1:
                    P.dma(vpo[l, :, gk * 64:(gk + 1) * 64], vf[0:nt, :])
            qrot = [P.tmp([128, 8, 64], BF16, "qrotb", 4) for _ in blocks]
            zat = [P.tmp([128, 512], BF16, "zat", 4) for _ in blocks]
            for part in range(4):
                cq = cfg["c_q"] + gk * 512 + part * 128
                wbq = load_w(w_in[l, :, cq:cq + 128], (l, "q", cq))
                qps = proj_tm(wbq, 128)
                for bi, b in enumerate(blocks):
                    nt = b["nt"]
                    qsb = P.tmp([128, 2, 64], F32, "qsb", 2)
                    P.cp("act", qsb[0:nt], qps[bi].re("p (h d) -> p h d", h=2))
                    if kind == "p":
                        cosv, sinv = cos_p[:, b["gb"], :], sin_p[:, b["gb"], :]
                    else:
                        cosv, sinv = cos_s[:, 0, :], sin_s[:, 0, :]
                    rope(qrot[bi][0:nt, 2 * part:2 * part + 2, :], qsb[0:nt], cosv[0:nt], sinv[0:nt], nt, 2)
            for part in range(4):
                cz = cfg["c_za"] + gk * 512 + part * 128
                wbz = load_w(w_in[l, :, cz:cz + 128], (l, "za", cz))
                zps = proj_tm(wbz, 128)
                for bi, b in enumerate(blocks):
                    P.act(zat[bi][0:b["nt"], part * 128:(part + 1) * 128], zps[bi], AF.Silu)
            for bi, b in enumerate(blocks):
                nt, c0 = b["nt"], b["c0"]
                if kind == "p":
                    slot = b["gb"] % 2
                    pslot = 1 - slot
                    has_prev = b["gb"] > 0
                else:
                    slot, pslot, has_prev = 1, 0, True
                    ckf = P.tmp([128, 64], F32, "ckf")
                    P.dma(ckf, ck[l, b["seq"], :, gk * 64:(gk + 1) * 64])
                    ckd = P.tmp([128, 2, 64], BF16, "ckd")
                    P.cp("dve", ckd[:, 0, :], ckf)
                    P.cp("pool", ckd[:, 1, :], ckf)
                    pt = psT()
                    P.tr(pt[:, 0:128], ckd.re("p a d -> p (a d)"), identb)
                    P.cp("act", V(kring.ap[:, gk, 0, :], [kr_bufs[gk][0]]), pt[:, 0:128])
                    cvf = P.tmp([128, 64], F32, "cvf")
                    P.dma(cvf, cv[l, b["seq"], :, gk * 64:(gk + 1) * 64])
                    P.cp("dve", V(vring.ap[:, gk, 0, :], [vr_bufs[gk][0]]), cvf)
                    if gk == 0 and LS < 128:
                        P.dma(kso[l, b["seq"], 0:128 - LS, :], ck[l, b["seq"], LS:128, :])
                        P.dma(vso[l, b["seq"], 0:128 - LS, :], cv[l, b["seq"], LS:128, :])
                kd = P.tmp([128, 2, 64], BF16, "kd", 2)
                P.cp("dve", kd[0:nt, 0, :], krot[bi][0:nt, 0, :])
                P.cp("pool", kd[0:nt, 1, :], krot[bi][0:nt, 0, :])
                pt = psT()
                P.tr(pt[:, 0:nt], kd[0:nt].re("p a d -> p (a d)"), identb[0:nt, 0:nt])
                kcur = V(kring.ap[:, gk, slot, 0:nt], [kr_bufs[gk][slot]])
                P.cp("act", kcur, pt[:, 0:nt])
                vcur = V(vring.ap[0:nt, gk, slot, :], [vr_bufs[gk][slot]])
                P.cp("dve", vcur, vsb[bi][0:nt, :])
                kprev = V(kring.ap[:, gk, pslot, :], [kr_bufs[gk][pslot]])
                vprev = V(vring.ap[:, gk, pslot, :], [vr_bufs[gk][pslot]])
                qb = qrot[bi]
                qT = P.tmp([128, 4, 128], BF16, "qT", 2)
                for pr_ in range(4):
                    pt = psT()
                    P.tr(pt[:, 0:nt], qb[0:nt, 2 * pr_:2 * pr_ + 2, :].re("p a d -> p (a d)"), identb[0:nt, 0:nt])
                    P.cp("act", qT[:, pr_, 0:nt], pt[:, 0:nt])
                ogat = P.tmp([128, 512], BF16, "ogat", 2)
                for h0 in range(0, 8, 2):
                    gens = []
                    for hh in range(h0, h0 + 2):
                        base = 64 * (hh % 2)
                        qTv = qT[base:base + 64, hh // 2, 0:nt]
                        ksrcs, vsrcs = [], []
                        if has_prev:
                            ksrcs.append((kprev[base:base + 64, :], 128))
                            vsrcs.append(vprev)
                        ksrcs.append((kcur[base:base + 64, :], nt))
                        vsrcs.append(vcur)
                        gens.append(attn_head(nt, qTv, ksrcs, vsrcs, kind == "p", gk * 8 + hh,
                                              zat[bi][0:nt, hh * 64:(hh + 1) * 64], ogat[0:nt, hh * 64:(hh + 1) * 64]))
                    for g_ in gens:
                        for _ in g_:
                            pass
                for q4 in range(4):
                    pt = psT()
                    P.tr(pt[:, 0:nt], ogat[0:nt, q4 * 128:(q4 + 1) * 128], identb[0:nt, 0:nt])
                    P.cp("act", oT[:, H + gk * 4 + q4, c0:c0 + nt], pt[:, 0:nt])

        stop_at(7)
        yv = [V(ybuf_flat.ap[:, bi * D:(bi + 1) * D], [hT_bufs[bi]]) for bi in range(nb)]
        for n in range(KC):
            wbo = load_w(w_out[l, :, n * 128:(n + 1) * 128], (l, "o", n))
            acc = psA()
            for bi, b in enumerate(blocks):
                nt, c0 = b["nt"], b["c0"]
                dst = acc[0:nt, bi * 128:(bi + 1) * 128]
                for c in range(KC):
                    P.mm(dst, oT[:, c, c0:c0 + nt], wbo[:, c, :], start=(c == 0), stop=(c == KC - 1))
                P.cp("act" if bi % 2 == 0 else "dve", yv[bi][0:nt, n * 128:(n + 1) * 128], dst)
        for bi, b in enumerate(blocks):
            nt = b["nt"]
            ss = P.tmp([128, 1], F32, "ss2", 2)
            P.act(oT_flat[0:nt, 0:D], yv[bi][0:nt, :], AF.Square, accum=ss[0:nt, :])
            P.act(ss[0:nt, :], ss[0:nt, :], AF.Sqrt, scale=1.0 / D, bias=EPS)
            P.recip(ss[0:nt, :], ss[0:nt, :])
            P.dma(xbuf[0:nt, :], x_in[b["r0"]:b["r0"] + nt, :])
            for j in range(0, D, 256):
                w_ = min(256, D - j)
                pg = P.tmp([128, 256], F32, "ggp", 3)
                P.dma(pg[0:nt, 0:w_], ggd[b["row"]:b["row"] + 1, j:j + w_].bc([nt, w_]))
                P.stt("dve", pg[0:nt, 0:w_], yv[bi][0:nt, j:j + w_], ss[0:nt, :], pg[0:nt, 0:w_], ALU.mult, ALU.mult)
                P.tt("pool", xbuf[0:nt, j:j + w_], xbuf[0:nt, j:j + w_], pg[0:nt, 0:w_], ALU.add)
            P.dma(x_out[b["r0"]:b["r0"] + nt, :], xbuf[0:nt, :])

    def main_body():
        for l in range(DEPTH):
            stop_at(1)
            layer_params(l)
            stop_at(2)
            for ct in range(CT):
                P.memset("pool", V(hist.ap[:, ct, :, :], [hist_bufs[ct]]), 0.0)
            for t in range(NTILES):
                do_tile(l, "p", t)
            CH = min(32, KC)
            for s in range(NSMP):
                for c0 in range(0, CT, CH):
                    n = min(CH, CT - c0)
                    srow = xbuf[0:3, 0:n * 128]
                    P.dma(srow, sconv[l, s, :, c0 * 128:(c0 + n) * 128])
                    pt = psD()
                    for i in range(n):
                        P.tr(pt[:, i * 3:(i + 1) * 3], srow[:, i * 128:(i + 1) * 128], ident[0:3, 0:3])
                    for i in range(n):
                        P.cp("dve", V(hist.ap[:, c0 + i, s, :], [hist_bufs[c0 + i]]), pt[:, i * 3:(i + 1) * 3])
            do_tile(l, "s", 0)

    def stop_at(k):
        if cfg["STOP"] == k:
            raise StopBuild()

    try:
        main_body()
    except StopBuild:
        pass
    P.fence(outs)
    P.emit()
    st.close()
    return nc, P


_CACHE = {}


def _get_prog(cfg_key, cfg):
    if cfg_key not in _CACHE:
        _CACHE[cfg_key] = build(cfg)
    return _CACHE[cfg_key]


def make_in_maps(cfg, ncores, inputs):
    f = lambda a: np.ascontiguousarray(np.asarray(a, dtype=np.float32))
    B = inputs["x_prompt"].shape[0]
    NSMP, DEPTH = cfg["NSMP"], cfg["DEPTH"]
    KVW = cfg["KVW"]
    in_maps = []
    shared = {k: f(inputs[k]) for k in ("w_ada", "b_ada", "g_pre", "g_post", "w_in", "conv_w", "a_log", "dt_bias",
                                        "dn_norm", "sinks", "w_out")}
    for i in range(ncores):
        pb = i % B
        ss = slice(NSMP * i, NSMP * (i + 1))
        m = dict(shared)
        m["xp"] = f(inputs["x_prompt"][pb])
        m["xs"] = f(inputs["x_sample"][ss]).reshape(-1, cfg["D"])
        m["sconv"] = f(inputs["state_conv"][:, ss])
        m["sdn"] = f(inputs["state_dn"][:, ss])
        m["ck"] = f(inputs["cache_k"][:, ss]).reshape(DEPTH, NSMP, 128, KVW)
        m["cv"] = f(inputs["cache_v"][:, ss]).reshape(DEPTH, NSMP, 128, KVW)
        m["cvec"] = f(np.concatenate([inputs["c_prompt"][pb:pb + 1], inputs["c_sample"][ss]], axis=0))
        in_maps.append(m)
    return in_maps


def assemble(cfg, ncores, B, R):
    NSMP, DEPTH = cfg["NSMP"], cfg["DEPTH"]
    H, KVH = cfg["H"], cfg["KVH"]
    y_p = np.stack([R[i]["yp"] for i in range(B)], 0)
    y_s = np.concatenate([R[i]["ys"].reshape(NSMP, cfg["DEC_SEQ"], cfg["D"]) for i in range(ncores)], 0)
    conv_p = np.stack([R[i]["convp"] for i in range(B)], 1)
    dn_p = np.stack([R[i]["dnp"] for i in range(B)], 1)
    k_p = np.stack([R[i]["kpo"].reshape(DEPTH, 128, KVH, 64) for i in range(B)], 1)
    v_p = np.stack([R[i]["vpo"].reshape(DEPTH, 128, KVH, 64) for i in range(B)], 1)
    conv_s = np.concatenate([R[i]["convs"] for i in range(ncores)], 1)
    dn_s = np.concatenate([R[i]["dns"] for i in range(ncores)], 1)
    k_s = np.concatenate([R[i]["kso"].reshape(DEPTH, NSMP, 128, KVH, 64) for i in range(ncores)], 1)
    v_s = np.concatenate([R[i]["vso"].reshape(DEPTH, NSMP, 128, KVH, 64) for i in range(ncores)], 1)
    return tuple(np.ascontiguousarray(a, dtype=np.float32) for a in
                 (y_p, y_s, conv_p, dn_p, k_p, v_p, conv_s, dn_s, k_s, v_s))


def run_cores(cfg, ncores, inputs):
    nc, _ = _get_prog(tuple(sorted(cfg.items())), cfg)
    in_maps = make_in_maps(cfg, ncores, inputs)
    res = run_bass_kernel_spmd(nc, in_maps, core_ids=list(range(ncores)))
    return assemble(cfg, ncores, inputs["x_prompt"].shape[0], res.results)


def kernel(**inputs):
    cfg = make_cfg()
    return run_cores(cfg, 8, inputs)
```

```python
import math
import numpy as np
from contextlib import ExitStack
import concourse.bass as bass
import concourse.mybir as mybir
from concourse.bass_utils import run_bass_kernel_spmd

F32 = mybir.dt.float32
BF16 = mybir.dt.bfloat16
I32 = mybir.dt.int32
ALU = mybir.AluOpType
AF = mybir.ActivationFunctionType
AX = mybir.AxisListType

SEM_LIMIT = 30000
DMA_SLOTS = 12
NEG = -30000.0
EPS = 1e-6


class Buf:
    __slots__ = ("name", "w", "r")

    def __init__(self, name):
        self.name = name
        self.w = {}
        self.r = {}


class V:
    __slots__ = ("ap", "bufs")

    def __init__(self, ap, bufs):
        self.ap = ap
        self.bufs = bufs

    def __getitem__(self, idx):
        return V(self.ap[idx], self.bufs)

    def re(self, pat, **kw):
        return V(self.ap.rearrange(pat, **kw), self.bufs)

    def wb(self, *bufs):
        return V(self.ap, list(bufs))

    def bc(self, shape):
        return V(self.ap.broadcast_to(list(shape)), self.bufs)

    def sub(self, idx, name="s"):
        return V(self.ap[idx], [Buf(name)])


class Op:
    __slots__ = ("eng", "fn", "raw", "oth", "dma", "sig", "sem", "val", "id")


class Prog:
    def __init__(self, nc, stack):
        self.nc = nc
        self.stack = stack
        self.ops = []
        self.nname = 0

    def sb(self, shape, dt=F32, name=None):
        self.nname += 1
        name = f"{name or 't'}_{self.nname}"
        h = self.stack.enter_context(self.nc.sbuf_tensor(name, list(shape), dt))
        return V(h[:], [Buf(name)])

    def ps(self, shape, dt=F32, name=None):
        self.nname += 1
        name = f"{name or 'p'}_{self.nname}"
        h = self.stack.enter_context(self.nc.psum_tensor(name, list(shape), dt))
        return V(h[:], [Buf(name)])

    def tmp(self, shape, dt=F32, name="tmp", bufs=1):
        if not hasattr(self, "pools"):
            self.pools = {}
        shape = list(shape)
        key = (name, str(dt), len(shape))
        cands = self.pools.setdefault(key, [])
        pool = None
        for pl in cands:
            if all(a >= b for a, b in zip(pl[2], shape)) and len(pl[0]) >= bufs:
                pool = pl
                break
        if pool is None:
            full = [128] + shape[1:]
            pool = [[self.sb(full, dt, name) for _ in range(bufs)], 0, full]
            cands.append(pool)
        v = pool[0][pool[1] % len(pool[0])]
        pool[1] += 1
        if pool[2] != shape:
            v = v[tuple(slice(0, n) for n in shape)]
        return v

    def dram(self, name, shape, dt=F32, kind="Internal"):
        t = self.nc.dram_tensor(name, list(shape), dt, kind=kind)
        return V(t.ap(), [Buf(name)])

    def op(self, eng, fn, reads=(), writes=(), dma=False):
        i = len(self.ops)
        raw = set()
        oth = set()
        for v in reads:
            for b in v.bufs:
                raw.update(b.w.values())
        for v in writes:
            for b in v.bufs:
                oth.update(b.w.values())
                oth.update(b.r.values())
        key = ("d", i) if dma else eng
        for v in reads:
            for b in v.bufs:
                b.r[key] = i
        for v in writes:
            for b in v.bufs:
                b.w = {key: i}
                b.r = {}
        o = Op()
        o.eng, o.fn, o.raw, o.oth, o.dma, o.sig, o.id = eng, fn, raw, oth - raw, dma, False, i
        o.sem = o.val = None
        self.ops.append(o)
        return i

    def emit(self):
        nc = self.nc
        ops = self.ops
        engs = ["pe", "act", "dve", "pool", "sp"]
        waited = {e: {} for e in engs}
        waited_d = {e: set() for e in engs}
        need = []
        for o in ops:
            E = o.eng
            lst = []
            for d in sorted(o.raw | o.oth):
                p = ops[d]
                if p.dma:
                    if d in waited_d[E]:
                        continue
                    waited_d[E].add(d)
                    lst.append(d)
                else:
                    if p.eng == E and not o.dma:
                        if E == "pe" or (E in ("act", "dve") and d not in o.raw):
                            continue
                    if waited[E].get(p.eng, -1) >= d:
                        continue
                    waited[E][p.eng] = d
                    lst.append(d)
                    p.sig = True
            need.append(lst)
        cnt = {e: 0 for e in engs}
        for o in ops:
            if o.sig and not o.dma:
                cnt[o.eng] += 1
                o.val = cnt[o.eng]
        nsem = {e: (cnt[e] + SEM_LIMIT - 1) // SEM_LIMIT for e in engs}
        sems = {e: [self.stack.enter_context(nc.semaphore(f"s_{e}_{k}")) for k in range(nsem[e])] for e in engs}
        dma_engs = sorted({o.eng for o in ops if o.dma})
        dsem = {e: [[self.stack.enter_context(nc.semaphore(f"d_{e}_{k}_0")), 0, None] for k in range(DMA_SLOTS)]
                for e in dma_engs}
        dcount = {e: 0 for e in dma_engs}
        pre_wait = {}
        for o in ops:
            if o.dma:
                e = o.eng
                k = dcount[e] % DMA_SLOTS
                dcount[e] += 1
                slot = dsem[e][k]
                if slot[1] + 16 > SEM_LIMIT:
                    slot[0] = self.stack.enter_context(nc.semaphore(f"d_{e}_{k}_{o.id}"))
                    slot[1] = 0
                if slot[2] is not None:
                    pre_wait[o.id] = slot[2]
                slot[1] += 16
                o.sem, o.val = slot[0], slot[1]
                slot[2] = o.id

        def semval(p):
            if p.dma:
                return p.sem, p.val
            n = p.val - 1
            return sems[p.eng][n // SEM_LIMIT], (n % SEM_LIMIT) + 1

        by_eng = {e: [o for o in ops if o.eng == e] for e in engs}
        self.stats = {e: len(by_eng[e]) for e in engs}
        self.stats["sig"] = dict(cnt)

        def run(ename, eng):
            done_d = set()
            for o in by_eng[ename]:
                if o.id in pre_wait:
                    p = ops[pre_wait[o.id]]
                    if p.id not in done_d:
                        eng.wait_ge(p.sem, p.val)
                        done_d.add(p.id)
                for d in need[o.id]:
                    p = ops[d]
                    if p.dma:
                        if p.id in done_d:
                            continue
                        done_d.add(p.id)
                    s, v = semval(p)
                    eng.wait_ge(s, v)
                ins = o.fn(eng)
                if o.dma:
                    ins.then_inc(o.sem, 16)
                elif o.sig:
                    s, v = semval(o)
                    ins.then_inc(s, 1)

        with nc.Block() as block:
            @block.tensor
            def _(e):
                run("pe", e)

            @block.scalar
            def _(e):
                run("act", e)

            @block.vector
            def _(e):
                run("dve", e)

            @block.gpsimd
            def _(e):
                run("pool", e)

            @block.sync
            def _(e):
                run("sp", e)

    def dma(self, out, in_, eng="sp"):
        self.op(eng, lambda e: e.dma_start(out=out.ap, in_=in_.ap), [in_], [out], dma=True)

    def mm(self, out, lhsT, rhs, start=True, stop=True):
        self.op("pe", lambda e: e.matmul(out.ap, lhsT.ap, rhs.ap, start=start, stop=stop), [lhsT, rhs], [out])

    def tr(self, out, in_, ident):
        self.op("pe", lambda e: e.transpose(out.ap, in_.ap, ident.ap), [in_, ident], [out])

    def act(self, out, in_, func, bias=None, scale=None, accum=None):
        reads = [in_]
        kw = {}
        if isinstance(bias, V):
            reads.append(bias)
            kw["bias"] = bias.ap
        elif bias is not None:
            kw["bias"] = bias
        if isinstance(scale, V):
            reads.append(scale)
            kw["scale"] = scale.ap
        elif scale is not None:
            kw["scale"] = scale
        writes = [out]
        if accum is not None:
            kw["accum_out"] = accum.ap
            writes.append(accum)
        self.op("act", lambda e: e.activation(out.ap, in_.ap, func, **kw), reads, writes)

    def tt(self, eng, out, a, b, op):
        self.op(eng, lambda e: e.tensor_tensor(out.ap, a.ap, b.ap, op), [a, b], [out])

    def ts(self, eng, out, a, s1, op0, s2=None, op1=None):
        reads = [a]
        x1 = s1.ap if isinstance(s1, V) else s1
        x2 = s2.ap if isinstance(s2, V) else s2
        if isinstance(s1, V):
            reads.append(s1)
        if isinstance(s2, V):
            reads.append(s2)
        kw = {}
        if op1 is not None:
            kw["op1"] = op1
        self.op(eng, lambda e: e.tensor_scalar(out.ap, a.ap, x1, x2, op0, **kw), reads, [out])

    def stt(self, eng, out, a, s, b, op0, op1):
        reads = [a, b]
        x = s.ap if isinstance(s, V) else s
        if isinstance(s, V):
            reads.append(s)
        self.op(eng, lambda e: e.scalar_tensor_tensor(out.ap, a.ap, x, b.ap, op0, op1), reads, [out])

    def cp(self, eng, out, in_):
        if eng == "act":
            self.op(eng, lambda e: e.copy(out.ap, in_.ap), [in_], [out])
        else:
            self.op(eng, lambda e: e.tensor_copy(out.ap, in_.ap), [in_], [out])

    def memset(self, eng, out, val):
        self.op(eng, lambda e: e.memset(out.ap, val), [], [out])

    def recip(self, out, in_):
        self.op("dve", lambda e: e.reciprocal(out.ap, in_.ap), [in_], [out])

    def red(self, out, in_, op, axis=AX.X):
        self.op("dve", lambda e: e.tensor_reduce(out.ap, in_.ap, axis, op), [in_], [out])

    def asel(self, out, in_, pattern, cmp, fill, base, cm):
        self.op("pool", lambda e: e.affine_select(out.ap, in_.ap, pattern, cmp, fill, base=base,
                                                  channel_multiplier=cm), [in_], [out])

    def iota(self, out, pattern, base, cm):
        self.op("pool", lambda e: e.iota(out.ap, pattern, base=base, channel_multiplier=cm), [], [out])

    def fence(self, views, eng="sp"):
        self.op(eng, lambda e: e.nop(), list(views), [])


class StopBuild(Exception):
    pass


def make_cfg(D=4096, SEQ=4096, DEPTH=2, NSMP=2, DEC_SEQ=16, PAST=2048, NT=512, THETA=10000.0, STOP=0):
    c = dict(D=D, SEQ=SEQ, DEPTH=DEPTH, NSMP=NSMP, DEC_SEQ=DEC_SEQ, PAST=PAST, NT=NT, THETA=THETA, STOP=STOP)
    c["KC"] = D // 128
    c["DNW"] = D // 2
    c["H"] = c["DNW"] // 128
    c["CONV"] = 3 * c["DNW"]
    c["CT"] = c["CONV"] // 128
    c["AW"] = D - c["DNW"]
    c["QH"] = c["AW"] // 64
    c["KVH"] = c["QH"] // 8
    c["KVW"] = c["KVH"] * 64
    c["c_z"] = c["CONV"]
    c["c_b"] = c["c_z"] + c["DNW"]
    c["c_a"] = c["c_b"] + c["H"]
    c["c_q"] = c["c_a"] + c["H"]
    c["c_k"] = c["c_q"] + c["AW"]
    c["c_v"] = c["c_k"] + c["KVW"]
    c["c_za"] = c["c_v"] + c["KVW"]
    c["IN_DIM"] = c["c_za"] + c["AW"]
    c["WIN"] = 128
    return c


def build(cfg):
    D, SEQ, DEPTH, NSMP, LS, NT = cfg["D"], cfg["SEQ"], cfg["DEPTH"], cfg["NSMP"], cfg["DEC_SEQ"], cfg["NT"]
    KC, H, CT, CONV, DNW, AW, QH, KVH, KVW = (cfg[k] for k in ("KC", "H", "CT", "CONV", "DNW", "AW", "QH", "KVH", "KVW"))
    IN_DIM = cfg["IN_DIM"]
    NR = 1 + NSMP
    NBT = NT // 128
    NTILES = SEQ // NT
    NBS = SEQ // 128
    PW = max(NT, NSMP * LS)
    nc = bass.Bass("TRN2", target_bir_lowering=False)
    st = ExitStack()
    P = Prog(nc, st)

    EI, EO = "ExternalInput", "ExternalOutput"
    xp = P.dram("xp", [SEQ, D], F32, EI)
    xs = P.dram("xs", [NSMP * LS, D], F32, EI)
    sconv = P.dram("sconv", [DEPTH, NSMP, 3, CONV], F32, EI)
    sdn = P.dram("sdn", [DEPTH, NSMP, H, 128, 128], F32, EI)
    ck = P.dram("ck", [DEPTH, NSMP, 128, KVW], F32, EI)
    cv = P.dram("cv", [DEPTH, NSMP, 128, KVW], F32, EI)
    cvec = P.dram("cvec", [NR, D], F32, EI)
    w_ada = P.dram("w_ada", [DEPTH, D, 3 * D], F32, EI)
    b_ada = P.dram("b_ada", [DEPTH, 3 * D], F32, EI)
    g_pre = P.dram("g_pre", [DEPTH, D], F32, EI)
    g_post = P.dram("g_post", [DEPTH, D], F32, EI)
    w_in = P.dram("w_in", [DEPTH, D, IN_DIM], F32, EI)
    conv_w = P.dram("conv_w", [DEPTH, 4, CONV], F32, EI)
    a_log = P.dram("a_log", [DEPTH, H], F32, EI)
    dt_bias = P.dram("dt_bias", [DEPTH, H], F32, EI)
    dn_norm = P.dram("dn_norm", [DEPTH, 128], F32, EI)
    sinks = P.dram("sinks", [DEPTH, QH], F32, EI)
    w_out = P.dram("w_out", [DEPTH, D, D], F32, EI)

    yp = P.dram("yp", [SEQ, D], F32, EO)
    ys = P.dram("ys", [NSMP * LS, D], F32, EO)
    convp = P.dram("convp", [DEPTH, 3, CONV], F32, EO)
    dnp = P.dram("dnp", [DEPTH, H, 128, 128], F32, EO)
    kpo = P.dram("kpo", [DEPTH, 128, KVW], F32, EO)
    vpo = P.dram("vpo", [DEPTH, 128, KVW], F32, EO)
    convs = P.dram("convs", [DEPTH, NSMP, 3, CONV], F32, EO)
    dns = P.dram("dns", [DEPTH, NSMP, H, 128, 128], F32, EO)
    kso = P.dram("kso", [DEPTH, NSMP, 128, KVW], F32, EO)
    vso = P.dram("vso", [DEPTH, NSMP, 128, KVW], F32, EO)
    outs = [yp, ys, convp, dnp, kpo, vpo, convs, dns, kso, vso]
    ggd = P.dram("ggd", [NR, D], F32)
    xmid_p = [P.dram(f"xmid_p{l}", [SEQ, D], F32) for l in range(DEPTH - 1)]
    xmid_s = [P.dram(f"xmid_s{l}", [NSMP * LS, D], F32) for l in range(DEPTH - 1)]

    bankA = [P.ps([128, 512], F32, "bA") for _ in range(2)]
    bankT = [P.ps([128, 1024], BF16, "bT") for _ in range(2)]
    bankD = [P.ps([128, 512], F32, "bD") for _ in range(3)]
    bankS = P.ps([128, 512], F32, "bS")
    dslots = []
    for q in range(4):
        for b in bankD:
            dslots.append(V(b.ap[:, q * 128:(q + 1) * 128], b.bufs))
    tslots = []
    for q in range(8):
        for b in bankT:
            tslots.append(V(b.ap[:, q * 128:(q + 1) * 128], b.bufs))
    sslots = [V(bankS.ap[:, q * 256:(q + 1) * 256], bankS.bufs) for q in range(2)]
    ctr = {"A": 0, "D": 0, "T": 0, "S": 0, "W": 0}

    def nxt(kind, lst):
        v = lst[ctr[kind] % len(lst)]
        ctr[kind] += 1
        return v

    psA = lambda: nxt("A", bankA)
    psD = lambda: nxt("D", dslots)
    psT = lambda: nxt("T", tslots)
    psS = lambda: nxt("S", sslots)

    ident = P.sb([128, 128], F32, "ident")
    identb = P.sb([128, 128], BF16, "identb")
    ones = P.sb([128, 128], F32, "ones")
    P.memset("pool", ident, 0.0)
    P.asel(ident, ident, [[-1, 128]], ALU.not_equal, 1.0, 0, 1)
    P.cp("dve", identb, ident)
    P.memset("pool", ones, 1.0)

    def make_masks(nt, CL):
        nch = nt // CL
        tmp = P.sb([nt, nt], F32, "mtmp")
        mS = P.sb([nt, nt], BF16, "mS")
        mU = P.sb([nt, nt], BF16, "mU")
        tri = P.sb([nt, nt], F32, "tri")
        P.memset("pool", tmp, 0.0)
        P.asel(tmp, tmp, [[-1, nt]], ALU.is_gt, NEG, 0, 1)
        for c in range(1, nch):
            P.memset("pool", tmp[c * CL:(c + 1) * CL, 0:c * CL], NEG)
        P.cp("dve", mS, tmp)
        tmp2 = P.sb([nt, nt], F32, "mtmp2")
        P.memset("pool", tmp2, 0.0)
        P.asel(tmp2, tmp2, [[1, nt]], ALU.is_ge, NEG, 0, -1)
        for c in range(0, nch - 1):
            P.memset("pool", tmp2[c * CL:(c + 1) * CL, (c + 1) * CL:nt], NEG)
        P.cp("dve", mU, tmp2)
        P.memset("pool", tri, 1.0)
        P.asel(tri, tri, [[1, nt]], ALU.is_ge, 0.0, 0, -1)
        for c in range(0, nch - 1):
            P.memset("pool", tri[c * CL:(c + 1) * CL, (c + 1) * CL:nt], 0.0)
        elast = []
        for c in range(nch):
            e = P.sb([nt, 128], F32, "elast")
            P.memset("pool", e, 0.0)
            P.asel(e, e, [[0, 128]], ALU.not_equal, 1.0, -(c * CL + CL - 1), 1)
            elast.append(e)
        return dict(nt=nt, CL=CL, nch=nch, mS=mS, mU=mU, tri=tri, elast=elast)

    MK_P = make_masks(128, 64)
    MK_S = make_masks(LS, min(64, LS))
    amf = P.sb([128, 256], F32, "amf")
    amask = P.sb([128, 256], BF16, "amask")
    P.memset("pool", amf, 0.0)
    P.memset("pool", amf[0:64, 192:256], NEG)
    P.memset("pool", amf[64:128, 0:64], NEG)
    P.cp("dve", amask, amf)

    def rope_tables(npart, nblk, pos0):
        cosT = P.sb([npart, nblk, 32], F32, "cosT")
        sinT = P.sb([npart, nblk, 32], F32, "sinT")
        fi = P.tmp([npart, 32], I32, "fi")
        P.iota(fi, [[1, 32]], 0, 0)
        ff = P.tmp([npart, 32], F32, "ff")
        P.cp("dve", ff, fi)
        inv = P.tmp([npart, 32], F32, "inv")
        P.act(inv, ff, AF.Exp, scale=-math.log(cfg["THETA"]) / 32.0)
        CHB = 2
        for b0 in range(0, nblk, CHB):
            n = min(CHB, nblk - b0)
            posi = P.tmp([npart, n], I32, "posi")
            P.iota(posi, [[128, n]], pos0 + 128 * b0, 1)
            posf = P.tmp([npart, n], F32, "posf")
            P.cp("dve", posf, posi)
            ang = P.tmp([npart, n, 32], F32, "ang")
            P.tt("dve", ang, posf.re("p (b o) -> p b o", o=1).bc([npart, n, 32]),
                 inv.re("p (o f) -> p o f", o=1).bc([npart, n, 32]), ALU.mult)
            for shift, tab in ((0.0, sinT), (0.25, cosT)):
                r = P.tmp([npart, n, 32], F32, "rr")
                P.ts("dve", r, ang, 1.0 / (2.0 * math.pi), ALU.mult, shift, ALU.add)
                ri = P.tmp([npart, n, 32], I32, "ri")
                P.cp("dve", ri, r)
                rf = P.tmp([npart, n, 32], F32, "rf")
                P.cp("dve", rf, ri)
                P.tt("dve", r, r, rf, ALU.subtract)
                P.ts("dve", rf, r, 0.5, ALU.is_gt)
                P.tt("dve", r, r, rf, ALU.subtract)
                P.ts("dve", rf, r, -0.5, ALU.is_lt)
                P.tt("dve", r, r, rf, ALU.add)
                P.act(tab[:, b0:b0 + n, :], r, AF.Sin, scale=2.0 * math.pi)
        return cosT, sinT

    cos_p, sin_p = rope_tables(128, NBS, 0)
    cos_s, sin_s = rope_tables(LS, 1, cfg["PAST"])

    hT = P.sb([128, KC, PW], BF16, "hT")
    hT_bufs = [Buf("hTy") for _ in range(max(NBT, NSMP))]
    hT = hT.wb(*hT_bufs)
    ybuf_flat = V(hT.ap.rearrange("p k t -> p (k t)"), hT_bufs)
    oT = P.sb([128, KC, PW], BF16, "oT")
    oT_flat = oT.re("p k t -> p (k t)")
    xnb = oT_flat[:, D:2 * D]
    NWB = 3
    NBUFG = max(NBT, NSMP)
    wbufs = [P.sb([128, KC, 128], BF16, "wb") for _ in range(NWB)]
    xbuf = P.sb([128, D], F32, "xbuf")
    S_all = P.sb([128, H, 128], F32, "S_all")
    S_bf = P.sb([128, H, 128], BF16, "S_bf")
    S_bufs = [Buf("S") for _ in range(H)]
    Sb_bufs = [Buf("Sb") for _ in range(H)]
    hist = P.sb([128, CT, NSMP, 3], F32, "hist")
    hist_bufs = [Buf("hist") for _ in range(CT)]
    kring = P.sb([128, KVH, 2, 128], BF16, "kring")
    vring = P.sb([128, KVH, 2, 64], BF16, "vring")
    kr_bufs = [[Buf("kr") for _ in range(2)] for _ in range(KVH)]
    vr_bufs = [[Buf("vr") for _ in range(2)] for _ in range(KVH)]
    gsT = P.sb([128, KC, NR], F32, "gsT")
    shT = P.sb([128, KC, NR], F32, "shT")
    cwT = P.sb([128, CT, 4], F32, "cwT")
    nA_b = P.sb([128, H], F32, "nA")
    dtb_b = P.sb([128, H], F32, "dtb")
    dnn_b = P.sb([128, 128], F32, "dnn")
    sink_b = P.sb([128, QH], F32, "sink")
    nsink_b = P.sb([128, QH], F32, "nsink")

    wctr = [0]

    NSLOT = IN_DIM // 128 + 2 * KVH + 2 + KC
    wsc = [P.dram(f"wsc{l}", [NSLOT, 128, KC * 128], BF16) for l in range(DEPTH)]
    wcache = {}

    def load_w(src_cols_view, key=None):
        wb = wbufs[wctr[0] % NWB]
        wctr[0] += 1
        ncols = src_cols_view.ap.shape[1]
        dst = wb[:, :, 0:ncols]
        if key is not None and key in wcache:
            P.dma(dst, wcache[key], eng="sp")
            return dst
        P.dma(dst, src_cols_view.re("(kc p) n -> p kc n", p=128), eng="pool")
        if key is not None:
            l_ = key[0]
            slot = sum(1 for k_ in wcache if k_[0] == l_)
            assert slot < NSLOT
            sv = V(wsc[l_].ap[slot].rearrange("p (kc n) -> p kc n", n=128)[:, :, 0:ncols], [Buf("wsc")])
            P.dma(sv, dst, eng="sp")
            wcache[key] = sv
        return dst

    def layer_params(l):
        cs = xbuf[0:NR, :]
        P.dma(cs, cvec)
        P.act(cs, cs, AF.Silu)
        scT = P.tmp([128, KC, NR], BF16, "scT")
        for kc in range(KC):
            pt = psD()
            P.tr(pt[:, 0:NR], cs[:, kc * 128:(kc + 1) * 128], ident[0:NR, 0:NR])
            P.cp("dve", scT[:, kc, :], pt[:, 0:NR])
        ncol = 3 * KC
        baT = P.tmp([128, ncol], F32, "baT")
        for c0 in range(0, ncol, 128):
            n = min(128, ncol - c0)
            rows = P.tmp([128, 128], F32, "rows", 2)
            P.dma(rows[0:n, :], b_ada[l, c0 * 128:(c0 + n) * 128].re("(r p) -> r p", p=128))
            pt = psD()
            P.tr(pt[:, 0:n], rows[0:n, :], ident[0:n, 0:n])
            P.cp("dve", baT[:, c0:c0 + n], pt[:, 0:n])
        gpT = P.tmp([128, KC], F32, "gpT")
        rows = P.tmp([128, 128], F32, "rows", 2)
        P.dma(rows[0:KC, :], g_pre[l].re("(r p) -> r p", p=128))
        pt = psD()
        P.tr(pt[:, 0:KC], rows[0:KC, :], ident[0:KC, 0:KC])
        P.cp("dve", gpT, pt[:, 0:KC])
        adaT = P.tmp([128, 2 * KC, NR], F32, "adaT")
        for j in range(3 * KC):
            wb = load_w(w_ada[l, :, j * 128:(j + 1) * 128])
            if j < 2 * KC:
                pt = psD()
                for kc in range(KC):
                    P.mm(pt[:, 0:NR], wb[:, kc, :], scT[:, kc, :], start=(kc == 0), stop=(kc == KC - 1))
                P.ts("dve", adaT[:, j, :], pt[:, 0:NR], baT[:, j:j + 1], ALU.add)
            else:
                pt = psD()
                for kc in range(KC):
                    P.mm(pt[0:NR, :], scT[:, kc, :], wb[:, kc, :], start=(kc == 0), stop=(kc == KC - 1))
                c0 = (j - 2 * KC) * 128
                bg = P.tmp([NR, 128], F32, "bg", 2)
                gp = P.tmp([NR, 128], F32, "gp", 2)
                P.dma(bg, b_ada[l:l + 1, 2 * D + c0:2 * D + c0 + 128].bc([NR, 128]))
                P.dma(gp, g_post[l:l + 1, c0:c0 + 128].bc([NR, 128]))
                gt = P.tmp([NR, 128], F32, "gt", 2)
                P.tt("dve", gt, pt[0:NR, :], bg, ALU.add)
                P.tt("pool", gt, gt, gp, ALU.mult)
                P.dma(ggd[:, c0:c0 + 128], gt)
        for r in range(NR):
            P.stt("dve", gsT[:, :, r], adaT[:, KC:2 * KC, r], 1.0, gpT, ALU.add, ALU.mult)
        P.cp("dve", shT, adaT[:, 0:KC, :])
        CH = min(32, KC)
        for c0 in range(0, CT, CH):
            n = min(CH, CT - c0)
            cwr = xbuf[0:4, 0:n * 128]
            P.dma(cwr, conv_w[l, :, c0 * 128:(c0 + n) * 128])
            pt = psD()
            for i in range(n):
                P.tr(pt[:, i * 4:(i + 1) * 4], cwr[:, i * 128:(i + 1) * 128], ident[0:4, 0:4])
            P.cp("dve", cwT[:, c0:c0 + n, :], pt[:, 0:4 * n].re("p (c j) -> p c j", j=4))
        al = P.tmp([128, H], F32, "al")
        P.dma(al, a_log[l:l + 1, :].bc([128, H]))
        P.act(al, al, AF.Exp)
        P.ts("dve", nA_b, al, -1.0, ALU.mult)
        P.dma(dtb_b, dt_bias[l:l + 1, :].bc([128, H]))
        P.dma(dnn_b, dn_norm[l:l + 1, :].bc([128, 128]))
        P.dma(sink_b, sinks[l:l + 1, :].bc([128, QH]))
        P.ts("dve", nsink_b, sink_b, -1.0, ALU.mult)

    def dn_gates(blk, ba_ps, MK):
        nt, CL, nch = MK["nt"], MK["CL"], MK["nch"]
        g = {}
        beta = P.tmp([nt, H], F32, "beta", NBUFG)
        P.act(beta, ba_ps[:, 0:H], AF.Sigmoid)
        xa = P.tmp([nt, H], F32, "xa", NBUFG)
        P.tt("dve", xa, ba_ps[:, H:2 * H], dtb_b[0:nt, :], ALU.add)
        P.act(xa, xa, AF.Exp)
        P.act(xa, xa, AF.Ln, bias=1.0)
        gg = P.tmp([nt, H], F32, "gg", NBUFG)
        P.tt("dve", gg, xa, nA_b[0:nt, :], ALU.mult)
        pt = psD()
        P.mm(pt[0:nt, 0:H], MK["tri"], gg)
        gc = P.tmp([nt, H], F32, "gc", NBUFG)
        P.cp("dve", gc, pt[0:nt, 0:H])
        egc = P.tmp([nt, H], F32, "egc", NBUFG)
        P.act(egc, gc, AF.Exp)
        bge = P.tmp([nt, H], F32, "bge", NBUFG)
        P.tt("dve", bge, beta, egc, ALU.mult)
        dec = P.tmp([128, nch, H], F32, "dec", NBUFG)
        kds = P.tmp([nt, H], F32, "kds", NBUFG)
        for c in range(nch):
            pg = psD()
            P.mm(pg[:, 0:H], MK["elast"][c], gc)
            P.act(dec[:, c, :], pg[:, 0:H], AF.Exp)
            rs = slice(c * CL, (c + 1) * CL)
            P.tt("dve", kds[rs, :], pg[rs, 0:H], gc[rs, :], ALU.subtract)
        P.act(kds, kds, AF.Exp)
        G1 = P.tmp([nt, H, 2], F32, "G1", NBUFG)
        G2 = P.tmp([nt, H, 2], F32, "G2", NBUFG)
        P.memset("pool", G1, 1.0)
        P.memset("pool", G2, 1.0)
        P.cp("dve", G1[:, :, 0], gc)
        P.ts("dve", G2[:, :, 1], gc, -1.0, ALU.mult)
        g.update(beta=beta, egc=egc, bge=bge, dec=dec, kds=kds, G1=G1, G2=G2)
        return g

    def dn_block(h, g, MK, qT, kT, vT, zs, Sv, Sbv, oT_dst):
        nt, CL, nch = MK["nt"], MK["CL"], MK["nch"]
        p1 = psD()
        P.mm(p1[0:2, 0:nt], g["G1"][:, h, :], ident[0:nt, 0:nt])
        R1 = P.tmp([2, nt], F32, "R1")
        P.cp("act", R1, p1[0:2, 0:nt])
        p2 = psD()
        P.mm(p2[0:2, 0:nt], g["G2"][:, h, :], ident[0:nt, 0:nt])
        R2 = P.tmp([2, nt], F32, "R2")
        P.cp("dve", R2, p2[0:2, 0:nt])
        stop_at(11)
        pd = psD()
        P.mm(pd[0:nt, 0:nt], identb[0:nt, 0:nt], MK["mS"], start=True, stop=False)
        P.mm(pd[0:nt, 0:nt], R1, R2, start=False, stop=True)
        gamS = P.tmp([nt, nt], F32, "gamS")
        P.act(gamS, pd[0:nt, 0:nt], AF.Exp)
        pdt = psD()
        P.mm(pdt[0:nt, 0:nt], identb[0:nt, 0:nt], MK["mU"], start=True, stop=False)
        P.mm(pdt[0:nt, 0:nt], R2, R1, start=False, stop=True)
        gamT = P.tmp([nt, nt], F32, "gamT")
        P.act(gamT, pdt[0:nt, 0:nt], AF.Exp)
        stop_at(12)
        pkk = psD()
        P.mm(pkk[0:nt, 0:nt], kT, kT)
        A = P.tmp([nt, nt], F32, "A")
        P.stt("dve", A, pkk[0:nt, 0:nt], g["beta"][:, h:h + 1], gamS, ALU.mult, ALU.mult)
        pkq = psD()
        P.mm(pkq[0:nt, 0:nt], kT, qT)
        qkmT = P.tmp([nt, nt], BF16, "qkmT")
        P.tt("dve", qkmT, pkq[0:nt, 0:nt], gamT, ALU.mult)
        stop_at(13)
        pat = psD()
        P.tr(pat[0:nt, 0:nt], A, ident[0:nt, 0:nt])
        AT = P.tmp([nt, nt], F32, "AT")
        P.cp("act", AT, pat[0:nt, 0:nt])
        RT = P.tmp([nt, nt], F32, "RT")
        P.tt("pool", RT, ident[0:nt, 0:nt], AT, ALU.subtract)
        stop_at(14)
        X, XT = A, AT
        nlev = int(math.log2(CL)) - 1
        for lev in range(nlev):
            px = psD()
            P.mm(px[0:nt, 0:nt], XT, X)
            X2 = P.tmp([nt, nt], F32, "X2", 2)
            P.cp("act", X2, px[0:nt, 0:nt])
            if lev < nlev - 1:
                pxt = psD()
                P.mm(pxt[0:nt, 0:nt], X, XT)
                X2T = P.tmp([nt, nt], F32, "X2T", 2)
                P.cp("dve", X2T, pxt[0:nt, 0:nt])
            pr = psD()
            P.mm(pr[0:nt, 0:nt], X2, RT)
            RT2 = P.tmp([nt, nt], F32, "RT2", 2)
            P.tt("dve", RT2, pr[0:nt, 0:nt], RT, ALU.add)
            RT = RT2
            if lev < nlev - 1:
                X, XT = X2, X2T
        stop_at(15)
        ptk = psT()
        P.tr(ptk[0:nt, :], kT, identb)
        ptv = psT()
        P.tr(ptv[0:nt, :], vT, identb)
        stop_at(151)
        RHSw = P.tmp([nt, 128], F32, "RHSw")
        P.act(RHSw, ptk[0:nt, :], AF.Identity, scale=g["bge"][:, h:h + 1])
        stop_at(152)
        kdec = P.tmp([nt, 128], BF16, "kdec")
        P.act(kdec, ptk[0:nt, :], AF.Identity, scale=g["kds"][:, h:h + 1])
        stop_at(153)
        RHSu = P.tmp([nt, 128], F32, "RHSu")
        P.act(RHSu, ptv[0:nt, :], AF.Identity, scale=g["beta"][:, h:h + 1])
        stop_at(154)
        pu = psD()
        P.mm(pu[0:nt, :], RT, RHSu)
        u = P.tmp([nt, 128], F32, "u")
        P.cp("act", u, pu[0:nt, :])
        stop_at(155)
        pw = psD()
        P.mm(pw[:, 0:nt], RHSw, RT)
        wT = P.tmp([128, nt], BF16, "wT")
        P.cp("dve", wT, pw[:, 0:nt])
        stop_at(16)
        vnew = P.tmp([nt, 128], BF16, "vnew")
        o = P.tmp([nt, 128], F32, "o")
        for c in range(nch):
            rs = slice(c * CL, (c + 1) * CL)
            pp1 = psD()
            P.mm(pp1[rs, :], wT[:, rs], Sbv)
            P.tt("dve", vnew[rs, :], u[rs, :], pp1[rs, :], ALU.subtract)
            pp2 = psD()
            P.mm(pp2[rs, :], qT[:, rs], Sbv)
            t2 = P.tmp([nt, 128], F32, "t2", 2)
            P.act(t2[rs, :], pp2[rs, :], AF.Identity, scale=g["egc"][rs, h:h + 1])
            pp3 = psD()
            P.mm(pp3[rs, :], qkmT[rs, rs], vnew[rs, :])
            P.tt("dve", o[rs, :], pp3[rs, :], t2[rs, :], ALU.add)
            pp4 = psD()
            P.mm(pp4, kdec[rs, :], vnew[rs, :])
            P.stt("dve", Sv, Sv, g["dec"][:, c, h:h + 1], pp4, ALU.mult, ALU.add)
            P.cp("act", Sbv, Sv)
        stop_at(17)
        junk = P.tmp([nt, 128], F32, "junk")
        ss = P.tmp([nt, 1], F32, "ss")
        P.act(junk, o, AF.Square, scale=128.0 ** -0.5, accum=ss)
        P.act(ss, ss, AF.Ln, bias=EPS)
        P.act(ss, ss, AF.Exp, scale=-0.5)
        og = P.tmp([nt, 128], F32, "og")
        P.stt("dve", og, o, ss, dnn_b[0:nt, :], ALU.mult, ALU.mult)
        ogb = P.tmp([nt, 128], BF16, "ogb")
        P.tt("pool", ogb, og, zs, ALU.mult)
        pt = psT()
        P.tr(pt[:, 0:nt], ogb, identb[0:nt, 0:nt])
        P.cp("act", oT_dst, pt[:, 0:nt])

    def rope(dst, src, cosv, sinv, nt, nh):
        cb = cosv.re("p (o f) -> p o f", o=1).bc([nt, nh, 32])
        sb_ = sinv.re("p (o f) -> p o f", o=1).bc([nt, nh, 32])
        x1, x2 = src[:, :, 0:32], src[:, :, 32:64]
        t1 = P.tmp([nt, nh, 32], F32, "rt1")
        t2 = P.tmp([nt, nh, 32], F32, "rt2")
        P.tt("dve", t1, x2, sb_, ALU.mult)
        P.tt("dve", t2, x1, cb, ALU.mult)
        P.tt("pool", dst[:, :, 0:32], t2, t1, ALU.subtract)
        t3 = P.tmp([nt, nh, 32], F32, "rt3")
        t4 = P.tmp([nt, nh, 32], F32, "rt4")
        P.tt("dve", t3, x1, sb_, ALU.mult)
        P.tt("dve", t4, x2, cb, ALU.mult)
        P.tt("pool", dst[:, :, 32:64], t4, t3, ALU.add)

    def lockstep(gens):
        live = list(gens)
        while live:
            nxt_ = []
            for g_ in live:
                try:
                    next(g_)
                    nxt_.append(g_)
                except StopIteration:
                    pass
            live = nxt_

    def attn_head(nt, qTv, ksrcs, vsrcs, masked, hq, zsv, og_dst):
        sc = psS()
        nk_tot = sum(nk for _, nk in ksrcs)
        if masked:
            mcols = amask[0:nt, 256 - nk_tot:256]
            P.mm(sc[0:nt, 0:nk_tot], identb[0:nt, 0:nt], mcols, start=True, stop=False)
        off = 0
        for i, (kv_, nk) in enumerate(ksrcs):
            P.mm(sc[0:nt, off:off + nk], qTv, kv_, start=not masked, stop=(not masked) or (i == len(ksrcs) - 1))
            off += nk
        yield
        mraw = P.tmp([nt, 1], F32, "mraw", 2)
        P.red(mraw, sc[0:nt, 0:nk_tot], ALU.max)
        negm = P.tmp([nt, 1], F32, "negm", 2)
        P.ts("dve", negm, mraw, -0.125, ALU.mult, nsink_b[0:nt, hq:hq + 1], ALU.min)
        yield
        p = P.tmp([nt, 256], BF16, "p", 2)
        ssum = P.tmp([nt, 1], F32, "ssum", 2)
        P.act(p[:, 0:nk_tot], sc[0:nt, 0:nk_tot], AF.Exp, bias=negm, scale=0.125, accum=ssum)
        sk = P.tmp([nt, 1], F32, "sk", 2)
        P.act(sk, sink_b[0:nt, hq:hq + 1], AF.Exp, bias=negm)
        yield
        P.tt("dve", sk, sk, ssum, ALU.add)
        P.recip(sk, sk)
        pTs = P.tmp([128, 2, nt], BF16, "pTs", 2)
        off = 0
        for i, (_, nk) in enumerate(ksrcs):
            pt = psT()
            P.tr(pt[0:nk, 0:nt], p[:, off:off + nk], identb[0:nt, 0:nt])
            P.cp("act", pTs[0:nk, i, :], pt[0:nk, 0:nt])
            off += nk
        yield
        po = psD()
        for i, (_, nk) in enumerate(ksrcs):
            P.mm(po[0:nt, 0:64], pTs[0:nk, i, :], vsrcs[i], start=(i == 0), stop=(i == len(ksrcs) - 1))
        yield
        P.stt("dve", og_dst, po[0:nt, 0:64], sk, zsv, ALU.mult, ALU.mult)

    def do_tile(l, kind, t):
        last_layer = (l == DEPTH - 1)
        if kind == "p":
            x_in = xp if l == 0 else xmid_p[l - 1]
            x_out = yp if last_layer else xmid_p[l]
            blocks = [dict(r0=t * NT + b * 128, nt=128, row=0, c0=b * 128, seq=0, gb=t * NBT + b) for b in range(NBT)]
            MK = MK_P
            nseq, L = 1, NT
        else:
            x_in = xs if l == 0 else xmid_s[l - 1]
            x_out = ys if last_layer else xmid_s[l]
            blocks = [dict(r0=s * LS, nt=LS, row=1 + s, c0=s * LS, seq=s, gb=0) for s in range(NSMP)]
            MK = MK_S
            nseq, L = NSMP, LS
        NTt = sum(b["nt"] for b in blocks)
        nb = len(blocks)

        for bi, b in enumerate(blocks):
            nt = b["nt"]
            P.dma(xbuf[0:nt, :], x_in[b["r0"]:b["r0"] + nt, :])
            ss = P.tmp([128, 1], F32, "ss0", 2)
            P.act(oT_flat[0:nt, 0:D], xbuf[0:nt, :], AF.Square, accum=ss[0:nt, :])
            P.act(ss[0:nt, :], ss[0:nt, :], AF.Sqrt, scale=1.0 / D, bias=EPS)
            P.recip(ss[0:nt, :], ss[0:nt, :])
            P.ts("dve", xnb[0:nt, :], xbuf[0:nt, :], ss[0:nt, :], ALU.mult)
            for kc in range(KC):
                pt = psT()
                P.tr(pt[:, 0:nt], xnb[0:nt, kc * 128:(kc + 1) * 128], identb[0:nt, 0:nt])
                P.act(hT[:, kc, b["c0"]:b["c0"] + nt], pt[:, 0:nt], AF.Identity,
                      bias=shT[:, kc, b["row"]:b["row"] + 1], scale=gsT[:, kc, b["row"]:b["row"] + 1])

        stop_at(3)

        def proj_tm(wb, ncols):
            acc = psA()
            res = []
            for bi, b in enumerate(blocks):
                nt = b["nt"]
                dst = acc[0:nt, bi * ncols:(bi + 1) * ncols]
                for kc in range(KC):
                    P.mm(dst, hT[:, kc, b["c0"]:b["c0"] + nt], wb[:, kc, 0:ncols], start=(kc == 0), stop=(kc == KC - 1))
                res.append(dst)
            return res

        def proj_cm(wb):
            acc = psA()
            dst = acc[:, 0:NTt]
            for kc in range(KC):
                P.mm(dst, wb[:, kc, :], hT[:, kc, 0:NTt], start=(kc == 0), stop=(kc == KC - 1))
            return dst

        wb = load_w(w_in[l, :, cfg["c_b"]:cfg["c_b"] + 2 * H], (l, "b"))
        ba = proj_tm(wb, 2 * H)
        gates = [dn_gates(b, ba[bi], MK) for bi, b in enumerate(blocks)]

        stop_at(4)
        def dn_stage_a1(h):
            acts = []
            for which in range(3):
                ct = which * H + h
                wb = load_w(w_in[l, :, ct * 128:(ct + 1) * 128], (l, "c", ct))
                acc = proj_cm(wb)
                pre = P.tmp([128, nseq, L + 3], F32, "pre", 2)
                hv = V(hist.ap[:, ct, 0:nseq, :], [hist_bufs[ct]])
                P.cp("pool", pre[:, :, 0:3], hv)
                P.cp("act", pre[:, :, 3:3 + L], acc.re("p (s t) -> p s t", s=nseq))
                P.cp("pool", hv, pre[:, :, L:L + 3])
                y = P.tmp([128, nseq, L], F32, "convy")
                P.act(y, pre[:, :, 0:L], AF.Identity, scale=cwT[:, ct, 0:1])
                for j in range(1, 4):
                    P.stt("dve", y, pre[:, :, j:j + L], cwT[:, ct, j:j + 1], y, ALU.mult, ALU.add)
                a_ = P.tmp([128, NTt], F32, "cact", 3)
                P.act(a_, y.re("p s t -> p (s t)"), AF.Silu)
                acts.append(a_)
            wb = load_w(w_in[l, :, cfg["c_z"] + h * 128:cfg["c_z"] + (h + 1) * 128], (l, "z", h))
            zps = proj_tm(wb, 128)
            zs_all = P.tmp([128, nb, 128], BF16, "zs", 2)
            for bi, b in enumerate(blocks):
                P.act(zs_all[0:b["nt"], bi, :], zps[bi], AF.Silu)
            return acts, zs_all

        def dn_stage_a2(acts):
            qa, ka, va = acts
            normed = []
            for a_, scl in ((qa, 128.0 ** -0.5), (ka, 1.0)):
                sq = P.tmp([128, NTt], F32, "sq")
                P.tt("pool", sq, a_, a_, ALU.mult)
                acc = psA()
                P.mm(acc[:, 0:NTt], ones, sq)
                rs_ = P.tmp([128, NTt], F32, "rs")
                P.act(rs_, acc[:, 0:NTt], AF.Ln, bias=EPS)
                P.act(rs_, rs_, AF.Exp, scale=-0.5)
                nb_ = P.tmp([128, NTt], BF16, "nrm", 4)
                P.stt("dve", nb_, a_, scl, rs_, ALU.mult, ALU.mult)
                normed.append(nb_)
            qTb, kTb = normed
            vTb = P.tmp([128, NTt], BF16, "vTb", 2)
            P.cp("pool", vTb, va)
            return qTb, kTb, vTb

        def dn_stage_b(h, qTb, kTb, vTb, zs_all):
            for bi, b in enumerate(blocks):
                nt, c0 = b["nt"], b["c0"]
                if kind == "p":
                    Sv = V(S_all.ap[:, h, :], [S_bufs[h]])
                    Sbv = V(S_bf.ap[:, h, :], [Sb_bufs[h]])
                    if b["gb"] == 0:
                        P.memset("pool", Sv, 0.0)
                        P.memset("pool", Sbv, 0.0)
                else:
                    hs = (h + b["seq"]) % H
                    Sv = V(S_all.ap[:, hs, :], [S_bufs[hs]])
                    Sbv = V(S_bf.ap[:, hs, :], [Sb_bufs[hs]])
                    P.dma(Sv, sdn[l, b["seq"], h])
                    P.cp("act", Sbv, Sv)
                dn_block(h, gates[bi], MK, qTb[:, c0:c0 + nt], kTb[:, c0:c0 + nt], vTb[:, c0:c0 + nt],
                         zs_all[0:nt, bi, :], Sv, Sbv, oT[:, h, c0:c0 + nt])
                if kind == "s":
                    P.dma(dns[l, b["seq"], h], Sv)
                elif b["gb"] == NBS - 1:
                    P.dma(dnp[l, h], Sv)

        acts_c, zs_c = dn_stage_a1(0)
        qkv_c = dn_stage_a2(acts_c)
        for h in range(H):
            if h + 1 < H:
                acts_n, zs_n = dn_stage_a1(h + 1)
            dn_stage_b(h, *qkv_c, zs_c)
            if h + 1 < H:
                qkv_c = dn_stage_a2(acts_n)
                zs_c = zs_n
        stop_at(5)
        if kind == "s" or t == NTILES - 1:
            CH = min(32, KC)
            for s in range(nseq):
                for c0 in range(0, CT, CH):
                    n = min(CH, CT - c0)
                    crow = xbuf[0:3, 0:n * 128]
                    for c1 in range(0, n, 4):
                        pt = psA()
                        for i in range(4):
                            hv = V(hist.ap[:, c0 + c1 + i, s, :], [hist_bufs[c0 + c1 + i]])
                            P.tr(pt[0:3, i * 128:(i + 1) * 128], hv, ident)
                        P.cp("dve", crow[:, c1 * 128:(c1 + 4) * 128], pt[0:3, :])
                    dst = convs[l, s] if kind == "s" else convp[l]
                    P.dma(dst[:, c0 * 128:(c0 + n) * 128], crow)

        stop_at(6)
        for gk in range(KVH):
            wbk = load_w(w_in[l, :, cfg["c_k"] + gk * 64:cfg["c_k"] + (gk + 1) * 64], (l, "k", gk))
            kps = proj_tm(wbk, 64)
            krot = []
            for bi, b in enumerate(blocks):
                nt = b["nt"]
                ksb = P.tmp([128, 1, 64], F32, "ksb", 2)
                P.cp("act", ksb[0:nt, 0, :], kps[bi])
                kr = P.tmp([128, 1, 64], F32, "kr", 4)
                if kind == "p":
                    cosv, sinv = cos_p[:, b["gb"], :], sin_p[:, b["gb"], :]
                else:
                    cosv, sinv = cos_s[:, 0, :], sin_s[:, 0, :]
                rope(kr[0:nt], ksb[0:nt], cosv[0:nt], sinv[0:nt], nt, 1)
                krot.append(kr)
                if kind == "s":
                    P.dma(kso[l, b["seq"], 128 - LS:128, gk * 64:(gk + 1) * 64], kr[0:nt, 0, :])
                elif b["gb"] == NBS - 1:
                    P.dma(kpo[l, :, gk * 64:(gk + 1) * 64], kr[0:nt, 0, :])
            wbv = load_w(w_in[l, :, cfg["c_v"] + gk * 64:cfg["c_v"] + (gk + 1) * 64], (l, "v", gk))
            vps = proj_tm(wbv, 64)
            vsb = []
            for bi, b in enumerate(blocks):
                nt = b["nt"]
                vf = P.tmp([128, 64], F32, "vf", 4)
                P.cp("act", vf[0:nt, :], vps[bi])
                vsb.append(vf)
                if kind == "s":
                    P.dma(vso[l, b["seq"], 128 - LS:128, gk * 64:(gk + 1) * 64], vf[0:nt, :])
                elif b["gb"] == NBS - 1:
                    P.dma(vpo[l, :, gk * 64:(gk + 1) * 64], vf[0:nt, :])
            qrot = [P.tmp([128, 8, 64], BF16, "qrotb", 4) for _ in blocks]
            zat = [P.tmp([128, 512], BF16, "zat", 4) for _ in blocks]
            for part in range(4):
                cq = cfg["c_q"] + gk * 512 + part * 128
                wbq = load_w(w_in[l, :, cq:cq + 128], (l, "q", cq))
                qps = proj_tm(wbq, 128)
                for bi, b in enumerate(blocks):
                    nt = b["nt"]
                    qsb = P.tmp([128, 2, 64], F32, "qsb", 2)
                    P.cp("act", qsb[0:nt], qps[bi].re("p (h d) -> p h d", h=2))
                    if kind == "p":
                        cosv, sinv = cos_p[:, b["gb"], :], sin_p[:, b["gb"], :]
                    else:
                        cosv, sinv = cos_s[:, 0, :], sin_s[:, 0, :]
                    rope(qrot[bi][0:nt, 2 * part:2 * part + 2, :], qsb[0:nt], cosv[0:nt], sinv[0:nt], nt, 2)
            for part in range(4):
                cz = cfg["c_za"] + gk * 512 + part * 128
                wbz = load_w(w_in[l, :, cz:cz + 128], (l, "za", cz))
                zps = proj_tm(wbz, 128)
                for bi, b in enumerate(blocks):
                    P.act(zat[bi][0:b["nt"], part * 128:(part + 1) * 128], zps[bi], AF.Silu)
            for bi, b in enumerate(blocks):
                nt, c0 = b["nt"], b["c0"]
                if kind == "p":
                    slot = b["gb"] % 2
                    pslot = 1 - slot
                    has_prev = b["gb"] > 0
                else:
                    slot, pslot, has_prev = 1, 0, True
                    ckf = P.tmp([128, 64], F32, "ckf")
                    P.dma(ckf, ck[l, b["seq"], :, gk * 64:(gk + 1) * 64])
                    ckd = P.tmp([128, 2, 64], BF16, "ckd")
                    P.cp("dve", ckd[:, 0, :], ckf)
                    P.cp("pool", ckd[:, 1, :], ckf)
                    pt = psT()
                    P.tr(pt[:, 0:128], ckd.re("p a d -> p (a d)"), identb)
                    P.cp("act", V(kring.ap[:, gk, 0, :], [kr_bufs[gk][0]]), pt[:, 0:128])
                    cvf = P.tmp([128, 64], F32, "cvf")
                    P.dma(cvf, cv[l, b["seq"], :, gk * 64:(gk + 1) * 64])
                    P.cp("dve", V(vring.ap[:, gk, 0, :], [vr_bufs[gk][0]]), cvf)
                    if gk == 0 and LS < 128:
                        P.dma(kso[l, b["seq"], 0:128 - LS, :], ck[l, b["seq"], LS:128, :])
                        P.dma(vso[l, b["seq"], 0:128 - LS, :], cv[l, b["seq"], LS:128, :])
                kd = P.tmp([128, 2, 64], BF16, "kd", 2)
                P.cp("dve", kd[0:nt, 0, :], krot[bi][0:nt, 0, :])
                P.cp("pool", kd[0:nt, 1, :], krot[bi][0:nt, 0, :])
                pt = psT()
                P.tr(pt[:, 0:nt], kd[0:nt].re("p a d -> p (a d)"), identb[0:nt, 0:nt])
                kcur = V(kring.ap[:, gk, slot, 0:nt], [kr_bufs[gk][slot]])
                P.cp("act", kcur, pt[:, 0:nt])
                vcur = V(vring.ap[0:nt, gk, slot, :], [vr_bufs[gk][slot]])
                P.cp("dve", vcur, vsb[bi][0:nt, :])
                kprev = V(kring.ap[:, gk, pslot, :], [kr_bufs[gk][pslot]])
                vprev = V(vring.ap[:, gk, pslot, :], [vr_bufs[gk][pslot]])
                qb = qrot[bi]
                qT = P.tmp([128, 4, 128], BF16, "qT", 2)
                for pr_ in range(4):
                    pt = psT()
                    P.tr(pt[:, 0:nt], qb[0:nt, 2 * pr_:2 * pr_ + 2, :].re("p a d -> p (a d)"), identb[0:nt, 0:nt])
                    P.cp("act", qT[:, pr_, 0:nt], pt[:, 0:nt])
                ogat = P.tmp([128, 512], BF16, "ogat", 2)
                for h0 in range(0, 8, 2):
                    gens = []
                    for hh in range(h0, h0 + 2):
                        base = 64 * (hh % 2)
                        qTv = qT[base:base + 64, hh // 2, 0:nt]
                        ksrcs, vsrcs = [], []
                        if has_prev:
                            ksrcs.append((kprev[base:base + 64, :], 128))
                            vsrcs.append(vprev)
                        ksrcs.append((kcur[base:base + 64, :], nt))
                        vsrcs.append(vcur)
                        gens.append(attn_head(nt, qTv, ksrcs, vsrcs, kind == "p", gk * 8 + hh,
                                              zat[bi][0:nt, hh * 64:(hh + 1) * 64], ogat[0:nt, hh * 64:(hh + 1) * 64]))
                    for g_ in gens:
                        for _ in g_:
                            pass
                for q4 in range(4):
                    pt = psT()
                    P.tr(pt[:, 0:nt], ogat[0:nt, q4 * 128:(q4 + 1) * 128], identb[0:nt, 0:nt])
                    P.cp("act", oT[:, H + gk * 4 + q4, c0:c0 + nt], pt[:, 0:nt])

        stop_at(7)
        yv = [V(ybuf_flat.ap[:, bi * D:(bi + 1) * D], [hT_bufs[bi]]) for bi in range(nb)]
        for n in range(KC):
            wbo = load_w(w_out[l, :, n * 128:(n + 1) * 128], (l, "o", n))
            acc = psA()
            for bi, b in enumerate(blocks):
                nt, c0 = b["nt"], b["c0"]
                dst = acc[0:nt, bi * 128:(bi + 1) * 128]
                for c in range(KC):
                    P.mm(dst, oT[:, c, c0:c0 + nt], wbo[:, c, :], start=(c == 0), stop=(c == KC - 1))
                P.cp("act" if bi % 2 == 0 else "dve", yv[bi][0:nt, n * 128:(n + 1) * 128], dst)
        for bi, b in enumerate(blocks):
            nt = b["nt"]
            ss = P.tmp([128, 1], F32, "ss2", 2)
            P.act(oT_flat[0:nt, 0:D], yv[bi][0:nt, :], AF.Square, accum=ss[0:nt, :])
            P.act(ss[0:nt, :], ss[0:nt, :], AF.Sqrt, scale=1.0 / D, bias=EPS)
            P.recip(ss[0:nt, :], ss[0:nt, :])
            P.dma(xbuf[0:nt, :], x_in[b["r0"]:b["r0"] + nt, :])
            for j in range(0, D, 256):
                w_ = min(256, D - j)
                pg = P.tmp([128, 256], F32, "ggp", 3)
                P.dma(pg[0:nt, 0:w_], ggd[b["row"]:b["row"] + 1, j:j + w_].bc([nt, w_]))
                P.stt("dve", pg[0:nt, 0:w_], yv[bi][0:nt, j:j + w_], ss[0:nt, :], pg[0:nt, 0:w_], ALU.mult, ALU.mult)
                P.tt("pool", xbuf[0:nt, j:j + w_], xbuf[0:nt, j:j + w_], pg[0:nt, 0:w_], ALU.add)
            P.dma(x_out[b["r0"]:b["r0"] + nt, :], xbuf[0:nt, :])

    def main_body():
        for l in range(DEPTH):
            stop_at(1)
            layer_params(l)
            stop_at(2)
            for ct in range(CT):
                P.memset("pool", V(hist.ap[:, ct, :, :], [hist_bufs[ct]]), 0.0)
            for t in range(NTILES):
                do_tile(l, "p", t)
            CH = min(32, KC)
            for s in range(NSMP):
                for c0 in range(0, CT, CH):
                    n = min(CH, CT - c0)
                    srow = xbuf[0:3, 0:n * 128]
                    P.dma(srow, sconv[l, s, :, c0 * 128:(c0 + n) * 128])
                    pt = psD()
                    for i in range(n):
                        P.tr(pt[:, i * 3:(i + 1) * 3], srow[:, i * 128:(i + 1) * 128], ident[0:3, 0:3])
                    for i in range(n):
                        P.cp("dve", V(hist.ap[:, c0 + i, s, :], [hist_bufs[c0 + i]]), pt[:, i * 3:(i + 1) * 3])
            do_tile(l, "s", 0)

    def stop_at(k):
        if cfg["STOP"] == k:
            raise StopBuild()

    try:
        main_body()
    except StopBuild:
        pass
    P.fence(outs)
    P.emit()
    st.close()
    return nc, P


_CACHE = {}


def _get_prog(cfg_key, cfg):
    if cfg_key not in _CACHE:
        _CACHE[cfg_key] = build(cfg)
    return _CACHE[cfg_key]


def make_in_maps(cfg, ncores, inputs):
    f = lambda a: np.ascontiguousarray(np.asarray(a, dtype=np.float32))
    B = inputs["x_prompt"].shape[0]
    NSMP, DEPTH = cfg["NSMP"], cfg["DEPTH"]
    KVW = cfg["KVW"]
    in_maps = []
    shared = {k: f(inputs[k]) for k in ("w_ada", "b_ada", "g_pre", "g_post", "w_in", "conv_w", "a_log", "dt_bias",
                                        "dn_norm", "sinks", "w_out")}
    for i in range(ncores):
        pb = i % B
        ss = slice(NSMP * i, NSMP * (i + 1))
        m = dict(shared)
        m["xp"] = f(inputs["x_prompt"][pb])
        m["xs"] = f(inputs["x_sample"][ss]).reshape(-1, cfg["D"])
        m["sconv"] = f(inputs["state_conv"][:, ss])
        m["sdn"] = f(inputs["state_dn"][:, ss])
        m["ck"] = f(inputs["cache_k"][:, ss]).reshape(DEPTH, NSMP, 128, KVW)
        m["cv"] = f(inputs["cache_v"][:, ss]).reshape(DEPTH, NSMP, 128, KVW)
        m["cvec"] = f(np.concatenate([inputs["c_prompt"][pb:pb + 1], inputs["c_sample"][ss]], axis=0))
        in_maps.append(m)
    return in_maps


def assemble(cfg, ncores, B, R):
    NSMP, DEPTH = cfg["NSMP"], cfg["DEPTH"]
    H, KVH = cfg["H"], cfg["KVH"]
    y_p = np.stack([R[i]["yp"] for i in range(B)], 0)
    y_s = np.concatenate([R[i]["ys"].reshape(NSMP, cfg["DEC_SEQ"], cfg["D"]) for i in range(ncores)], 0)
    conv_p = np.stack([R[i]["convp"] for i in range(B)], 1)
    dn_p = np.stack([R[i]["dnp"] for i in range(B)], 1)
    k_p = np.stack([R[i]["kpo"].reshape(DEPTH, 128, KVH, 64) for i in range(B)], 1)
    v_p = np.stack([R[i]["vpo"].reshape(DEPTH, 128, KVH, 64) for i in range(B)], 1)
    conv_s = np.concatenate([R[i]["convs"] for i in range(ncores)], 1)
    dn_s = np.concatenate([R[i]["dns"] for i in range(ncores)], 1)
    k_s = np.concatenate([R[i]["kso"].reshape(DEPTH, NSMP, 128, KVH, 64) for i in range(ncores)], 1)
    v_s = np.concatenate([R[i]["vso"].reshape(DEPTH, NSMP, 128, KVH, 64) for i in range(ncores)], 1)
    return tuple(np.ascontiguousarray(a, dtype=np.float32) for a in
                 (y_p, y_s, conv_p, dn_p, k_p, v_p, conv_s, dn_s, k_s, v_s))


def run_cores(cfg, ncores, inputs):
    nc, _ = _get_prog(tuple(sorted(cfg.items())), cfg)
    in_maps = make_in_maps(cfg, ncores, inputs)
    res = run_bass_kernel_spmd(nc, in_maps, core_ids=list(range(ncores)))
    return assemble(cfg, ncores, inputs["x_prompt"].shape[0], res.results)


def kernel(**inputs):
    cfg = make_cfg()
    return run_cores(cfg, 8, inputs)
```
